# Optimizing a Trainium2 kernel written in Bass

```python
import math
import jax, jax.numpy as jnp
from jax import lax
import numpy as np

D_MODEL = 1024
BATCH = 8
SEQ = 2048
DEPTH = 1
DEC_BATCH = 128
DEC_SEQ = 4
PAST_LEN = 16384
PAGE_SIZE = 128

MIX_WIDTH = D_MODEL
DN_HEADS = 4
DN_DK = 128
DN_DV = 128
DN_WIDTH = DN_HEADS * DN_DV
SC_WIDTH = MIX_WIDTH - DN_WIDTH
SC_GROUPS = 8
DN_CONV = 4
SC_CONV = 3
CHUNK = 64
QKV_DIM = DN_HEADS * (2 * DN_DK + DN_DV)
PROJ_DIM = QKV_DIM + 2 * DN_HEADS + DN_WIDTH + 3 * SC_WIDTH
D_FF = 2816
N_SUB = 3
ALPHA = (2 * DEPTH) ** 0.25
BETA = (8 * DEPTH) ** -0.25
LN_EPS = 1e-5
RMS_EPS = 1e-6

kernel_name = "hymba_gdn_shortconv_macaron_deepnorm_step"


def layer_norm(x, g, b):
    xf = x.astype(jnp.float32)
    mu = jnp.mean(xf, -1, keepdims=True)
    var = jnp.mean(jnp.square(xf - mu), -1, keepdims=True)
    return ((xf - mu) * lax.rsqrt(var + LN_EPS) * g + b).astype(x.dtype)


def swiglu(u, wg, wu, wd):
    return (jax.nn.silu(u @ wg) * (u @ wu)) @ wd


def causal_dwconv(x, buf, w):
    W = w.shape[0]
    T = x.shape[1]
    xp = jnp.concatenate([buf.astype(x.dtype), x], axis=1)
    y = sum(xp[:, j:j + T] * w[j] for j in range(W))
    return y, xp[:, xp.shape[1] - (W - 1):]


def l2norm(x):
    xf = x.astype(jnp.float32)
    return xf * lax.rsqrt(jnp.sum(xf * xf, -1, keepdims=True) + RMS_EPS)


def gated_delta_rule(q, k, v, g, beta, s0):
    B, T, H, _ = q.shape
    C = math.gcd(T, CHUNK)
    N = T // C
    f32 = jnp.float32

    def chunks(a):
        a = a.astype(f32).reshape((B, N, C, H) + a.shape[3:])
        return jnp.moveaxis(a, 3, 1)

    q, k, v, g, beta = chunks(q), chunks(k), chunks(v), chunks(g), chunks(beta)
    gc = jnp.cumsum(g, axis=-1)
    tril = jnp.tril(jnp.ones((C, C), bool))
    strict = jnp.tril(jnp.ones((C, C), bool), -1)
    diff = gc[..., :, None] - gc[..., None, :]
    decay = jnp.where(tril, jnp.exp(jnp.where(tril, diff, 0.0)), 0.0)
    kb = k * beta[..., None]
    L = jnp.where(strict, jnp.einsum('bhnik,bhnjk->bhnij', kb, k) * decay, 0.0)
    eye = jnp.eye(C, dtype=f32)
    Tm = lax.linalg.triangular_solve(eye + L, jnp.broadcast_to(eye, L.shape),
                                     left_side=True, lower=True)
    u = jnp.einsum('bhnij,bhnjv->bhniv', Tm, v * beta[..., None])
    w = jnp.einsum('bhnij,bhnjk->bhnik', Tm, kb * jnp.exp(gc)[..., None])
    qk = jnp.where(tril, jnp.einsum('bhnik,bhnjk->bhnij', q, k) * decay, 0.0)
    q_dec = q * jnp.exp(gc)[..., None]
    k_dec = k * jnp.exp(gc[..., -1:] - gc)[..., None]
    g_last = jnp.exp(gc[..., -1])

    def step(S, xs):
        u_n, w_n, qk_n, qd_n, kd_n, gl_n = xs
        v_new = u_n - jnp.einsum('bhck,bhkv->bhcv', w_n, S)
        o = jnp.einsum('bhck,bhkv->bhcv', qd_n, S) + jnp.einsum('bhij,bhjv->bhiv', qk_n, v_new)
        S = S * gl_n[..., None, None] + jnp.einsum('bhck,bhcv->bhkv', kd_n, v_new)
        return S, o

    xs = tuple(jnp.moveaxis(a, 2, 0) for a in (u, w, qk, q_dec, k_dec, g_last))
    S, o = lax.scan(step, s0.astype(f32), xs)
    o = jnp.transpose(o, (1, 0, 3, 2, 4)).reshape(B, T, H, -1)
    return o, S.astype(s0.dtype)


def hybrid_layer(x, c, s_ssm, s_cqkv, s_cmix, w_ada, b_ada, ln_g, ln_b,
                 f1_wg, f1_wu, f1_wd, f2_wg, f2_wu, f2_wd,
                 w_in, conv_qkv_w, a_log, dt_bias, dn_norm_g, conv_mix_w, w_out):
    B, T, D = x.shape
    mod = (jax.nn.silu(c) @ w_ada + b_ada).reshape(B, N_SUB, 3, D)
    shift, scale, gate = mod[:, :, 0], mod[:, :, 1], mod[:, :, 2]

    def modulate(h, i):
        return h * (1.0 + scale[:, None, i]) + shift[:, None, i]

    def post(h, delta, i):
        return layer_norm(ALPHA * h + gate[:, None, i] * delta, ln_g[i], ln_b[i])

    x = post(x, 0.5 * swiglu(modulate(x, 0), f1_wg, f1_wu, f1_wd), 0)

    u = modulate(x, 1)
    p = u @ w_in
    cuts = np.cumsum([QKV_DIM, DN_HEADS, DN_HEADS, DN_WIDTH, SC_WIDTH, SC_WIDTH]).tolist()
    qkv, a, b, og, sc_b, sc_c, sc_h = jnp.split(p, cuts, axis=-1)

    qkv, new_cqkv = causal_dwconv(qkv, s_cqkv, conv_qkv_w)
    qkv = jax.nn.silu(qkv)
    q, k, v = jnp.split(qkv, [DN_HEADS * DN_DK, 2 * DN_HEADS * DN_DK], axis=-1)
    q = l2norm(q.reshape(B, T, DN_HEADS, DN_DK)) * (DN_DK ** -0.5)
    k = l2norm(k.reshape(B, T, DN_HEADS, DN_DK))
    v = v.reshape(B, T, DN_HEADS, DN_DV)
    beta = jax.nn.sigmoid(b.astype(jnp.float32))
    g = -jnp.exp(a_log.astype(jnp.float32)) * jax.nn.softplus(a.astype(jnp.float32) + dt_bias)
    o, new_ssm = gated_delta_rule(q, k, v, g, beta, s_ssm)
    o = o * lax.rsqrt(jnp.mean(o * o, -1, keepdims=True) + RMS_EPS) * dn_norm_g
    o = o * jax.nn.silu(og.reshape(B, T, DN_HEADS, DN_DV).astype(jnp.float32))
    o_dn = o.reshape(B, T, DN_WIDTH).astype(x.dtype)

    z = sc_c * sc_h
    zc, new_cmix = causal_dwconv(z, s_cmix, conv_mix_w)
    o_sc = sc_b * zc

    mix = jnp.concatenate([o_dn, o_sc], axis=-1) @ w_out
    x = post(x, mix, 1)

    x = post(x, 0.5 * swiglu(modulate(x, 2), f2_wg, f2_wu, f2_wd), 2)
    return x, new_ssm, new_cqkv.astype(s_cqkv.dtype), new_cmix.astype(s_cmix.dtype)


def setup_inputs(seed: int = 0) -> dict:
    key = jax.random.key(seed)
    ks = iter(jax.random.split(key, 32))
    nrm = lambda shape, s: jax.random.normal(next(ks), shape, jnp.float32) * s
    D, L = D_MODEL, DEPTH
    return {
        "x_prompt": nrm((BATCH, SEQ, D), 1.0),
        "x_sample": nrm((DEC_BATCH, DEC_SEQ, D), 1.0),
        "state_ssm": nrm((L, DEC_BATCH, DN_HEADS, DN_DK, DN_DV), 0.1),
        "state_conv_qkv": nrm((L, DEC_BATCH, DN_CONV - 1, QKV_DIM), 1.0),
        "state_conv_mix": nrm((L, DEC_BATCH, SC_CONV - 1, SC_WIDTH), 1.0),
        "c_prompt": nrm((BATCH, D), 1.0),
        "c_sample": nrm((DEC_BATCH, D), 1.0),
        "w_ada": nrm((L, D, N_SUB * 3 * D), 0.5 * D ** -0.5),
        "b_ada": nrm((L, N_SUB * 3 * D), 0.02),
        "ln_g": 1.0 + nrm((L, N_SUB, D), 0.02),
        "ln_b": nrm((L, N_SUB, D), 0.02),
        "ffn1_wg": nrm((L, D, D_FF), D ** -0.5),
        "ffn1_wu": nrm((L, D, D_FF), D ** -0.5),
        "ffn1_wd": nrm((L, D_FF, D), BETA * D_FF ** -0.5),
        "ffn2_wg": nrm((L, D, D_FF), D ** -0.5),
        "ffn2_wu": nrm((L, D, D_FF), D ** -0.5),
        "ffn2_wd": nrm((L, D_FF, D), BETA * D_FF ** -0.5),
        "w_in": nrm((L, D, PROJ_DIM), D ** -0.5),
        "conv_qkv_w": nrm((L, DN_CONV, QKV_DIM), DN_CONV ** -0.5),
        "a_log": jnp.log(jax.random.uniform(next(ks), (L, DN_HEADS), jnp.float32, 1.0, 16.0)),
        "dt_bias": nrm((L, DN_HEADS), 0.1),
        "dn_norm_g": 1.0 + nrm((L, DN_DV), 0.02),
        "conv_mix_w": nrm((L, SC_CONV, SC_WIDTH), SC_CONV ** -0.5),
        "w_out": nrm((L, MIX_WIDTH, D), BETA * MIX_WIDTH ** -0.5),
    }


def reference(x_prompt, x_sample, state_ssm, state_conv_qkv, state_conv_mix, c_prompt, c_sample,
              w_ada, b_ada, ln_g, ln_b, ffn1_wg, ffn1_wu, ffn1_wd, ffn2_wg, ffn2_wu, ffn2_wd,
              w_in, conv_qkv_w, a_log, dt_bias, dn_norm_g, conv_mix_w, w_out):
    Bp = x_prompt.shape[0]
    dt = x_prompt.dtype
    hp, hs = x_prompt, x_sample
    sp_ssm, sp_cqkv, sp_cmix = [], [], []
    ss_ssm, ss_cqkv, ss_cmix = [], [], []
    for l in range(DEPTH):
        w = (w_ada[l], b_ada[l], ln_g[l], ln_b[l], ffn1_wg[l], ffn1_wu[l], ffn1_wd[l],
             ffn2_wg[l], ffn2_wu[l], ffn2_wd[l], w_in[l], conv_qkv_w[l], a_log[l],
             dt_bias[l], dn_norm_g[l], conv_mix_w[l], w_out[l])
        z_ssm = jnp.zeros((Bp, DN_HEADS, DN_DK, DN_DV), dt)
        z_cqkv = jnp.zeros((Bp, DN_CONV - 1, QKV_DIM), dt)
        z_cmix = jnp.zeros((Bp, SC_CONV - 1, SC_WIDTH), dt)
        hp, a1, a2, a3 = hybrid_layer(hp, c_prompt, z_ssm, z_cqkv, z_cmix, *w)
        hs, b1, b2, b3 = hybrid_layer(hs, c_sample, state_ssm[l], state_conv_qkv[l],
                                      state_conv_mix[l], *w)
        sp_ssm.append(a1); sp_cqkv.append(a2); sp_cmix.append(a3)
        ss_ssm.append(b1); ss_cqkv.append(b2); ss_cmix.append(b3)
    return (hp, hs, jnp.stack(sp_ssm), jnp.stack(sp_cqkv), jnp.stack(sp_cmix),
            jnp.stack(ss_ssm), jnp.stack(ss_cqkv), jnp.stack(ss_cmix))
```

```python
import numpy as np
import concourse.bass as bass
import concourse.mybir as mybir
from concourse.bass_utils import run_bass_kernel_spmd

F32 = mybir.dt.float32
BF16 = mybir.dt.bfloat16
AF = mybir.ActivationFunctionType
ALU = mybir.AluOpType

D = 1024
KC = 8
FF = 2816
FC = 22
NP = 2048
NTP = 1024
NS = 64
NT = NTP + NS
NB = 16
ALPHA = 2.0 ** 0.25
LN_EPS = 1e-5
RMS_EPS = 1e-6
BIGNEG = -1.0e5
ENGS = ("pe", "act", "dve", "pool", "sp")


class _Op:
    __slots__ = ("eng", "fn", "dma", "chan", "eidx", "deps", "signal", "count", "waits")


class Sched:
    def __init__(self, nc):
        self.nc = nc
        self.ops = {e: [] for e in ENGS}
        self.lastw = {}
        self.readers = {}
        self.chan_n = {}
        self.chan_last = {}

    def add(self, eng, fn, reads=(), writes=(), dma=False, chan=None):
        op = _Op()
        op.eng, op.fn, op.dma, op.chan = eng, fn, dma, chan
        op.eidx = len(self.ops[eng])
        op.signal = False
        op.count = None
        op.waits = []
        deps = []
        for r in reads:
            w = self.lastw.get(r)
            if w is not None:
                deps.append(w)
        for w_ in writes:
            w = self.lastw.get(w_)
            if w is not None:
                deps.append(w)
            deps.extend(self.readers.get(w_, ()))
        for r in reads:
            self.readers.setdefault(r, []).append(op)
        for w_ in writes:
            self.lastw[w_] = op
            self.readers[w_] = []
        op.deps = [d for d in deps if d is not op]
        if dma:
            n = self.chan_n.get(chan, 0) + 1
            self.chan_n[chan] = n
            op.count = 16 * n
            op.signal = True
            prev = self.chan_last.get(chan)
            if prev is not None:
                op.deps.append(prev)
            self.chan_last[chan] = op
        self.ops[eng].append(op)
        return op

    def fence(self):
        lasts = [self.ops[e][-1] for e in ENGS if self.ops[e]]
        lastdma = list(self.chan_last.values())
        for e in ENGS:
            op = self.add(e, lambda en: en.nop())
            op.deps = [d for d in lasts if d is not op] + lastdma

    def finalize(self, block):
        nc = self.nc
        for e in ENGS:
            wd = {}
            for op in self.ops[e]:
                need = {}
                for d in op.deps:
                    if d.dma:
                        key, val = ("c", d.chan), d.count
                    else:
                        if d.eng == e and not op.dma and e == "pe":
                            continue
                        key, val = ("e", d.eng), d.eidx
                    if key not in need or need[key][0] < val:
                        need[key] = (val, d)
                for key, (val, d) in need.items():
                    if wd.get(key, -1) >= val:
                        continue
                    wd[key] = val
                    d.signal = True
                    op.waits.append(d)
        esem = {}
        for e in ENGS:
            c = 0
            for op in self.ops[e]:
                if not op.dma and op.signal:
                    c += 1
                    op.count = c
            esem[e] = nc.alloc_semaphore("s_" + e)
        csem = {ch: nc.alloc_semaphore("c_%d" % i) for i, ch in enumerate(self.chan_n)}
        engmap = {"pe": block.tensor, "act": block.scalar, "dve": block.vector,
                  "pool": block.gpsimd, "sp": block.sync}

        def mk(e):
            def body(en):
                for op in self.ops[e]:
                    for d in op.waits:
                        if d.dma:
                            en.wait_ge(csem[d.chan], d.count)
                        else:
                            en.wait_ge(esem[d.eng], d.count)
                    ins = op.fn(en)
                    if op.dma:
                        ins.then_inc(csem[op.chan], 16)
                    elif op.signal:
                        ins.then_inc(esem[e], 1)
            return body

        for e in ENGS:
            if self.ops[e]:
                engmap[e](mk(e))


def _const_tables():
    i = np.arange(128)
    cols = {}
    ident = np.eye(128, dtype=np.float32)
    tri_p = (i[:, None] <= i[None, :]).astype(np.float32)
    seq_p = np.ones((128, 128), np.float32)
    maskT_p = np.where(i[None, :] >= i[:, None], 0.0, BIGNEG).astype(np.float32)
    nst_p = np.where(i[None, :] > i[:, None], -1.0, 0.0).astype(np.float32)
    same = (i[:, None] % 16) == (i[None, :] % 16)
    valid = (i[:, None] < 64) & (i[None, :] < 64)
    tri_s = (same & (i[:, None] <= i[None, :]) & valid).astype(np.float32)
    seq_s = (same & valid).astype(np.float32)
    maskT_s = np.where(same & (i[None, :] >= i[:, None]) & valid, 0.0, BIGNEG).astype(np.float32)
    nst_s = np.where(same & (i[None, :] > i[:, None]) & valid, -1.0, 0.0).astype(np.float32)
    s = np.arange(64)
    sel = (s[None, :] % 16 == np.arange(16)[:, None]).astype(np.float32).reshape(1, 16 * 64)
    sel = np.repeat(sel, 128, axis=0)
    selP = ((i[:, None] % 16) == np.arange(16)[None, :]).astype(np.float32)
    blocks = [("ident", ident), ("tri_p", tri_p), ("seq_p", seq_p), ("maskT_p", maskT_p), ("nst_p", nst_p),
              ("tri_s", tri_s), ("seq_s", seq_s), ("maskT_s", maskT_s), ("nst_s", nst_s),
              ("selP", selP)]
    off = 0
    for name, a in blocks:
        cols[name] = (off, a.shape[1])
        off += a.shape[1]
    tab = np.concatenate([a for _, a in blocks], axis=1).astype(np.float32)
    return tab, cols, sel


_CST, _CSTCOLS, _SEL = _const_tables()
NCST = _CST.shape[1]


class _Stop(Exception):
    pass


def build_program(debug=(), stop_after=None):
    nc = bass.Bass("TRN2", target_bir_lowering=False)
    S = Sched(nc)
    dbg_outs = {}

    def din(name, shape, dt=F32):
        return nc.dram_tensor(name, list(shape), dt, kind="ExternalInput").ap()

    def dout(name, shape, dt=F32):
        return nc.dram_tensor(name, list(shape), dt, kind="ExternalOutput").ap()

    xp_d = din("xp", [128, KC, NP])
    xs_d = din("xs", [128, KC, NS])
    cc_d = din("cc", [17, D])
    s0_d = din("s0", [128, NB, 4, 128])
    cq_d = din("cq", [128, 12, 48])
    cm_d = din("cm", [128, 4, 32])
    wada_d = din("wada", [36, 128, KC, 256])
    bada_d = din("bada", [128, 72])
    lng_d = din("lng", [128, 24])
    lnb_d = din("lnb", [128, 24])
    wg_d = [din("wg1", [11, 128, KC, 256]), din("wg2", [11, 128, KC, 256])]
    wu_d = [din("wu1", [11, 128, KC, 256]), din("wu2", [11, 128, KC, 256])]
    wd_d = [din("wd1", [8, 128, FC, 128]), din("wd2", [8, 128, FC, 128])]
    win_d = din("win", [14, 128, KC, 256])
    wab_d = din("wab", [128, KC, 8])
    wout_d = din("wout", [128, KC, D])
    cqw_d = din("cqw", [128, 4, 12])
    cmw_d = din("cmw", [128, 3, 4])
    dng_d = din("dng", [128, 1])
    alog_d = din("alog", [4])
    dtb_d = din("dtb", [4])
    cst_d = din("cst", [128, NCST])
    sel_d = din("sel", [128, NB * NS])

    yp_d = dout("yp", [128, KC, NP])
    ys_d = dout("ys", [128, KC, NS])
    ssmp_d = dout("ssm_p", [128, 4, 128])
    cqp_d = dout("cq_p", [128, 12, 3])
    cmp_d = dout("cm_p", [128, 4, 2])
    ssms_d = dout("ssm_s", [128, NB, 4, 128])
    cqs_d = dout("cq_s", [128, 12, 48])
    cms_d = dout("cm_s", [128, 4, 32])
    out_keys = []

    def sb(name, shape, dt=F32):
        return nc.alloc_sbuf_tensor("sb_" + name, list(shape), dt)

    xT = sb("xT", [128, KC, NT])
    uT = sb("uT", [128, KC, NT], BF16)
    cst = sb("cst", [128, NCST])
    identb = sb("identb", [128, 128], BF16)
    selb = sb("selb", [128, NB, NS], BF16)
    onesD = sb("onesD", [128, 128], BF16)
    ones128 = sb("ones128", [128, 128], BF16)
    epst = sb("epst", [128, 4])
    bada = sb("bada", [128, 72])
    lng = sb("lng", [128, 24])
    lnb = sb("lnb", [128, 24])
    modall = sb("modall", [128, 72, 17])
    sc1p0 = sb("sc1p0", [128, KC, 17])
    G1 = sb("G1", [128, KC, 17]); B1 = sb("B1", [128, KC, 17])
    G2 = sb("G2", [128, KC, 17]); B2 = sb("B2", [128, KC, 17])
    gsc = [sb("gsc%d" % i, [128, KC, 17]) for i in range(3)]
    cqw = sb("cqw", [128, 4, 12]); cmw = sb("cmw", [128, 3, 4]); dng = sb("dng", [128, 1])
    alog = sb("alog", [128, 4]); dtb = sb("dtb", [128, 4]); nea = sb("nea", [128, 4])
    wab = sb("wab", [128, KC, 8], BF16)
    Sst = sb("Sst", [128, 4, 128]); Sbf = sb("Sbf", [128, 4, 128], BF16)
    halo_q = sb("halo_q", [128, 12, 3]); halo_m = sb("halo_m", [128, 4, 2])
    R1 = sb("R1", [128, 24576 // 4])
    R2 = sb("R2", [128, (FC * NT * 2) // 4])
    wblk = [sb("wblk%d" % i, [128, KC, 256], BF16) for i in range(4)]
    R4 = sb("R4", [128, 16384 // 4])
    R5 = sb("R5", [128, 32768 // 4])
    ps = [nc.alloc_psum_tensor("ps%d" % i, [128, 512], F32) for i in range(8)]

    def view(t, byte_off, shape, dt):
        esz = 2 if dt == BF16 else 4
        n = int(np.prod(shape[1:]))
        a = t[:, byte_off // 4:(byte_off + n * esz + 3) // 4]
        if dt == BF16:
            a = a.bitcast(BF16)[:, 0:n]
        if len(shape) == 3:
            a = a.rearrange("p (a b) -> p a b", a=shape[1])
        elif len(shape) == 4:
            a = a.rearrange("p (a b c) -> p a b c", a=shape[1], b=shape[2])
        return a

    zb = view(R1, 0, [128, KC, 512], BF16)
    zsq = view(R1, 8192, [128, KC, 512], BF16)
    st_t1 = view(R1, 16384, [128, 512], F32)
    st_rstd = view(R1, 18432, [128, 512], F32)
    st_nmr = view(R1, 20480, [128, 512], F32)
    st_xh = view(R1, 22528, [128, 512], F32)
    hT = view(R2, 0, [128, FC, NT], BF16)
    wdb = [view(R4, i * 5632, [128, FC, 128], BF16) for i in range(2)]
    sg_t = [view(R5, i * 1024, [128, 512], BF16) for i in range(2)]
    cc_sb = view(R5, 4096, [17 if False else 128, D], F32)
    scT = view(R5, 4096 + 4096, [128, KC, 17], BF16)

    qkT = view(R2, 0, [128, 8, NT], BF16)
    vT = view(R2, 17408, [128, 4, NT], BF16)
    mix = view(R2, 26112, [128, 8, NT], BF16)
    kdm = view(R2, 43520, [128, NB, 128], BF16)
    wout = view(R4, 0, [128, KC, D], BF16)
    pc = [view(R5, i * 4112, [128, 3 + NTP], F32) for i in range(2)]
    cv = view(R5, 8224, [128, NT], F32)
    sq = view(R5, 12576, [128, NT], BF16)
    scc = view(R5, 14752, [128, NT], F32)
    rinvp = scc
    cv2 = view(R1, 0, [128, NT], F32)
    sq2 = view(R1, 4352, [128, NT], BF16)
    rinv2 = view(R1, 6528, [128, NT], F32)
    zc = view(R5, 19104, [128, 2 + NTP], F32)
    pcs = view(R5, 23216, [128, 12, 112], F32)
    zcs = view(R5, 28592, [128, 4, 96], F32)
    class _NS:
        pass

    def gdn_set(si):
        def vw(off, shape, dt):
            if si == 0:
                return view(R5, off, shape, dt)
            if off < 24064:
                return view(R1, off, shape, dt)
            return view(R5, 29184 + off - 24064, shape, dt)
        t = _NS()
        t.si = si
        t.ab_sb = vw(0, [128, 8], F32)
        t.g4 = vw(32, [128, 4], F32)
        t.beta4 = vw(48, [128, 4], F32)
        t.gcc = vw(64, [128, 8], F32)
        t.egc = vw(96, [128, 4], F32)
        t.kbgs = vw(112, [128, 4], F32)
        t.kdcs = vw(128, [128, 4], F32)
        t.glast = vw(256, [128, 4, 16], F32)
        t.glast2 = vw(256, [128, 64], F32)
        t.gbc = vw(512, [128, 4, 128], F32)
        t.bbc = vw(2560, [128, 4, 128], BF16)
        t.egr = vw(4608, [128, 4, 128], F32)
        t.dec = vw(6656, [128, 4, 128], F32)
        t.dnb = vw(8704, [128, 4, 128], F32)
        t.YT = [vw(10752 + i * 2048, [128, 4, 2, 128], BF16) for i in range(2)]
        t.XX = [vw(14848 + i * 1024, [128, 4, 128], BF16) for i in range(2)]
        t.qkm = vw(16896, [128, 4, 128], BF16)
        t.kbg = vw(17920, [128, 4, 128], BF16)
        t.kdec = vw(18944, [128, 4, 128], BF16)
        t.vb = vw(19968, [128, 4, 128], BF16)
        t.nwT = vw(20992, [128, 4, 128], BF16)
        t.qdT = vw(22016, [128, 4, 128], BF16)
        t.vnew = vw(23040, [128, 4, 128], BF16)
        t.osq = vw(24064, [128, 4, 128], BF16)
        t.rinv = vw(25088, [128, 4, 128], F32)
        return t

    TS = [gdn_set(0), gdn_set(1)]
    S0 = [view(R1, 0, [128, NB, 128], F32), view(R1, 16384, [128, NB, 128], F32)]
    S0b = view(R1, 8192, [128, NB, 128], BF16)
    nwTm = view(R1, 12288, [128, NB, NS], BF16)
    qdTm = view(R1, 14336, [128, NB, NS], BF16)
    ones1 = sb("ones1", [128, 128], BF16)
    lnc = sb("lnc", [128, 2])

    def C(name):
        o, n = _CSTCOLS[name]
        return cst[:, o:o + n]

    def dma(q, out, in_, reads, writes, chan):
        return S.add(q, lambda e: e.dma_start(out=out, in_=in_), reads=reads, writes=writes, dma=True, chan=chan)

    def mm(out, lhsT, rhs, start, stop, reads, writes):
        return S.add("pe", lambda e: e.matmul(out, lhsT=lhsT, rhs=rhs, start=start, stop=stop), reads=reads, writes=writes)

    def tr(out, in_, ident, reads, writes):
        return S.add("pe", lambda e: e.transpose(out, in_, ident), reads=reads, writes=writes)

    def act(out, in_, func, reads, writes, bias=None, scale=None, eng="act"):
        kw = {}
        if bias is not None:
            kw["bias"] = bias
        if scale is not None:
            kw["scale"] = scale
        return S.add(eng, lambda e: e.activation(out=out, in_=in_, func=func, **kw), reads=reads, writes=writes)

    def tt(eng, out, in0, in1, op, reads, writes):
        return S.add(eng, lambda e: e.tensor_tensor(out=out, in0=in0, in1=in1, op=op), reads=reads, writes=writes)

    def stt(eng, out, in0, scalar, in1, op0, op1, reads, writes):
        return S.add(eng, lambda e: e.scalar_tensor_tensor(out=out, in0=in0, scalar=scalar, in1=in1, op0=op0, op1=op1),
                     reads=reads, writes=writes)

    def ts(eng, out, in0, s1, s2, op0, op1, reads, writes):
        if s2 is None:
            return S.add(eng, lambda e: e.tensor_scalar(out=out, in0=in0, scalar1=s1, scalar2=None, op0=op0), reads=reads, writes=writes)
        return S.add(eng, lambda e: e.tensor_scalar(out=out, in0=in0, scalar1=s1, scalar2=s2, op0=op0, op1=op1), reads=reads, writes=writes)

    def cp(eng, out, in_, reads, writes):
        return S.add(eng, lambda e: e.tensor_copy(out=out, in_=in_), reads=reads, writes=writes)

    def memset(eng, ap, val, writes):
        return S.add(eng, lambda e: e.memset(ap, val), writes=writes)

    def pk(b, q0=0, q1=4):
        return [("ps", b)]

    def dump(name, src_ap, shape, reads, dt=F32):
        if name not in debug:
            return
        d = dout("dbg_" + name, shape, dt)
        dbg_outs[name] = d
        dma("sp", d, src_ap, reads=reads, writes=[("dbg", name)], chan=("dbg", name))
        out_keys.append(("dbg", name))

    def debug_dump(name, src_ap, shape, reads, dt=F32):
        d = dout("dbg_" + name, shape, dt)
        dbg_outs[name] = d
        dma("sp", d, src_ap, reads=reads, writes=[("dbg", name)], chan=("dbg", name))
        out_keys.append(("dbg", name))

    def bc_s(tile17, c):
        return tile17[:, c, 1:17].unsqueeze(1).to_broadcast([128, 4, 16])

    def v3(ap):
        return ap.rearrange("p (t b) -> p t b", t=4)

    dma("sp", cc_sb[0:17, :], cc_d, [], ["cc_sb"], "cc")
    dma("sp", cst[:], cst_d, [], ["cst"], "cst")
    dma("sp", bada[:], bada_d, [], ["bada"], "small")
    dma("sp", lng[:], lng_d, [], ["lng"], "small")
    dma("sp", lnb[:], lnb_d, [], ["lnb"], "small")
    dma("sp", cqw[:], cqw_d, [], ["cqw"], "small")
    dma("sp", cmw[:], cmw_d, [], ["cmw"], "small")
    dma("sp", dng[:], dng_d, [], ["dng"], "small")
    dma("sp", alog[:], alog_d.partition_broadcast(128), [], ["alog"], "small")
    dma("sp", dtb[:], dtb_d.partition_broadcast(128), [], ["dtb"], "small")
    memset("dve", onesD[:], 1.0 / D, ["onesD"])
    memset("dve", ones128[:], 1.0 / 128, ["ones128"])
    memset("dve", epst[:, 0:1], LN_EPS / (ALPHA * ALPHA), ["epst"])
    memset("dve", epst[:, 1:2], RMS_EPS, ["epst"])
    memset("dve", epst[:, 2:3], 1.0, ["epst"])
    memset("dve", epst[:, 3:4], 0.0, ["epst"])
    memset("dve", ones1[:], 1.0, ["ones1"])
    memset("dve", lnc[:, 0:1], 128.0 * RMS_EPS, ["lnc"])
    memset("dve", lnc[:, 1:2], RMS_EPS, ["lnc"])
    cp("dve", identb[:], C("ident"), ["cst"], ["identb"])
    memset("dve", Sst[:], 0.0, ["Sst"])
    memset("dve", Sbf[:], 0.0, ["Sbf"])
    memset("dve", halo_q[:], 0.0, ["halo_q"])
    memset("dve", halo_m[:], 0.0, ["halo_m"])

    act(cc_sb[0:17, :], cc_sb[0:17, :], AF.Silu, ["cc_sb"], ["cc_sb"])
    for k in range(KC):
        tr(ps[7][:, k * 17:(k + 1) * 17], cc_sb[0:17, k * 128:(k + 1) * 128], C("ident")[0:17, 0:17],
           ["cc_sb", "cst"], pk(7, 0, 2))
    cp("dve", scT[:], ps[7][:, 0:KC * 17].rearrange("p (a b) -> p a b", a=KC), pk(7, 0, 2), ["scT"])
    ring = [0]

    def ring_next():
        i = ring[0] % 4
        ring[0] += 1
        return wblk[i], ("wblk", i)

    def modv(i, j):
        return modall[:, (i * 3 + j) * 8:(i * 3 + j + 1) * 8, :]

    def ada_block(blk):
        wb, kw = ring_next()
        dma("pool", wb[:], wada_d[blk], [], [kw], kw)
        for cpos in range(2):
            fc = blk * 2 + cpos
            grp = fc // 8
            bank = 6 + grp % 2
            col = (fc % 8) * 17
            for k in range(KC):
                mm(ps[bank][:, col:col + 17], wb[:, k, cpos * 128:(cpos + 1) * 128], scT[:, k, :], k == 0, k == KC - 1,
                   [kw, "scT"], pk(bank))
            if fc % 8 == 7:
                tt("dve", modall[:, grp * 8:(grp + 1) * 8, :],
                   ps[bank][:, 0:8 * 17].rearrange("p (a b) -> p a b", a=8),
                   bada[:, grp * 8:(grp + 1) * 8].unsqueeze(2).to_broadcast([128, 8, 17]), ALU.add,
                   pk(bank) + ["bada"], [("mod", grp)])

    def ada_derive(i):
        if i == 0:
            ts("dve", sc1p0[:], modv(0, 1), 1.0, None, ALU.add, None, [("mod", 1)], ["sc1p0"])
        else:
            G, B = (G1, B1) if i == 1 else (G2, B2)
            gb = lng[:, (i - 1) * 8:i * 8].unsqueeze(2).to_broadcast([128, KC, 17])
            bb = lnb[:, (i - 1) * 8:i * 8].unsqueeze(2).to_broadcast([128, KC, 17])
            ts("dve", G[:], modv(i, 1), 1.0, None, ALU.add, None, [("mod", i * 3 + 1)], [("GB", i)])
            tt("dve", B[:], G[:], bb, ALU.mult, [("GB", i), "lnb"], [("GB", i)])
            tt("dve", B[:], B[:], modv(i, 0), ALU.add, [("GB", i), ("mod", i * 3)], [("GB", i)])
            tt("dve", G[:], G[:], gb, ALU.mult, [("GB", i), "lng"], [("GB", i)])

    def ada_gate(i):
        f = (0.5 if i != 1 else 1.0) / ALPHA
        ts("dve", gsc[i][:], modv(i, 2), f, None, ALU.mult, None, [("mod", i * 3 + 2)], [("gsc", i)])

    ada_state = {"next": 0}

    def ada_more(n):
        for _ in range(n):
            if ada_state["next"] < 36:
                ada_block(ada_state["next"])
                ada_state["next"] += 1

    ada_more(8)
    ada_derive(0)
    dma("pool", wab[:], wab_d, [], ["wab"], "wab")
    dma("pool", selb[:], sel_d.rearrange("p (a b) -> p a b", a=NB), [], ["selb"], "wab")
    act(nea[:], alog[:], AF.Exp, ["alog"], ["nea"])
    ts("dve", nea[:], nea[:], -1.0, None, ALU.mult, None, ["nea"], ["nea"])
    if stop_after == "ada":
        ada_more(36)
        for i in range(3):
            ada_gate(i)
        ada_derive(1)
        ada_derive(2)
    dump("modall", modall[:], [128, 72, 17], [("mod", g_) for g_ in range(9)])

    def groups(st):
        g = [("A", 0, 512), ("B", 512, 1024)]
        if st == 0:
            g.append(("S", 1024, 1088))
        return g

    def load_x(st):
        for gi_, (g, lo, hi) in enumerate((("A", 0, 512), ("B", 512, 1024))):
            dma("sp", xT[:, :, lo:hi], xp_d[:, :, st * NTP + lo:st * NTP + hi], [],
                [("xT", c, g) for c in range(KC)], ("xload", gi_))
        if st == 0:
            dma("sp", xT[:, :, NTP:NT], xs_d, [], [("xT", c, "S") for c in range(KC)], ("xload", 2))

    def modulate0(st):
        for (g, lo, hi) in groups(st):
            for c in range(KC):
                if g != "S":
                    act(uT[:, c, lo:hi], xT[:, c, lo:hi], AF.Identity, [("xT", c, g), "sc1p0", ("mod", 0)], [("uT", c, g)],
                        bias=modv(0, 0)[:, c, 0:1], scale=sc1p0[:, c, 0:1])
                else:
                    tt("dve", v3(st_xh[:, 0:64]), v3(xT[:, c, lo:hi]), bc_s(sc1p0, c), ALU.mult,
                       [("xT", c, g), "sc1p0"], ["st_xh"])
                    tt("dve", v3(uT[:, c, lo:hi]), v3(st_xh[:, 0:64]), bc_s(modv(0, 0), c), ALU.add,
                       ["st_xh", ("mod", 0)], [("uT", c, g)])

    def ffn(st, w, gi):
        grp = groups(st)
        for fb in range(11):
            bg, kg = ring_next()
            bu, ku = ring_next()
            dma("pool", bg[:], wg_d[w][fb], [], [kg], kg)
            dma("pool", bu[:], wu_d[w][fb], [], [ku], ku)

            for (g, lo, hi) in grp:
                for fp in range(2):
                    f = fb * 2 + fp
                    if g != "S":
                        par = f % 2
                        pg, pu = ps[par * 2], ps[par * 2 + 1]
                        og, ou = pg[:, :], pu[:, :]
                        kpg, kpu = pk(par * 2), pk(par * 2 + 1)
                    else:
                        par = f % 2
                        og = ps[4 + par][:, 0:64]
                        ou = ps[4 + par][:, 64:128]
                        kpg = kpu = pk(4 + par)
                    for k in range(KC):
                        mm(og, bg[:, k, fp * 128:(fp + 1) * 128], uT[:, k, lo:hi], k == 0, k == KC - 1,
                           [kg, ("uT", k, g)], kpg)
                    for k in range(KC):
                        mm(ou, bu[:, k, fp * 128:(fp + 1) * 128], uT[:, k, lo:hi], k == 0, k == KC - 1,
                           [ku, ("uT", k, g)], kpu)
                    w_ = hi - lo
                    sgt = sg_t[par]
                    act(sgt[:, 0:w_], og, AF.Silu, kpg, [("sg", par)])
                    tt("dve", hT[:, f, lo:hi], sgt[:, 0:w_], ou, ALU.mult, [("sg", par)] + kpu, [("hT", f, g)])
            if st == 0 and w == 0:
                ada_more(2 if fb < 8 else 1)
        if st == 0 and w == 0:
            ada_more(1)
            ada_gate(0)
            ada_derive(1)
        if stop_after == "up":
            return
        for m in range(KC):
            wb = wdb[m % 2]
            kw = ("wdb", m % 2)
            dma("pool", wb[:], wd_d[w][m], [], [kw], kw)
            if st == 0 and w == 0:
                ada_more(2)
                if m == KC - 1:
                    ada_more(36)
                    ada_gate(1)
                    ada_gate(2)
                    ada_derive(2)
            for (g, lo, hi) in grp:
                if g != "S":
                    bank = (m % 2) * 2 + (0 if g == "A" else 1)
                    o = ps[bank][:, :]
                    kp = pk(bank)
                else:
                    o = ps[4 + m % 2][:, 0:64]
                    kp = pk(4 + m % 2)
                for f in range(FC):
                    mm(o, wb[:, f, :], hT[:, f, lo:hi], f == 0, f == FC - 1, [kw, ("hT", f, g)], kp)
                if g != "S":
                    stt("dve", xT[:, m, lo:hi], o, gsc[gi][:, m, 0:1], xT[:, m, lo:hi], ALU.mult, ALU.add,
                        kp + [("gsc", gi), ("xT", m, g)], [("xT", m, g)])
                else:
                    tt("dve", v3(st_xh[:, 0:64]), v3(o), bc_s(gsc[gi], m), ALU.mult, kp + [("gsc", gi)], ["st_xh"])
                    tt("dve", xT[:, m, lo:hi], st_xh[:, 0:64], xT[:, m, lo:hi], ALU.add, ["st_xh", ("xT", m, g)], [("xT", m, g)])

    lnset = [dict(t1=st_t1, rstd=st_rstd, nmr=st_nmr, xh=st_xh, k="0"),
             dict(t1=view(R5, 4096, [128, 512], F32), rstd=view(R5, 6144, [128, 512], F32),
                  nmr=view(R5, 8192, [128, 512], F32), xh=view(R5, 10240, [128, 512], F32), k="1")]

    def ln_pre(g, lo, hi, ss_):
        w_ = hi - lo
        t1, rstd, nmr, kk = ss_["t1"], ss_["rstd"], ss_["nmr"], ss_["k"]
        for c in range(KC):
            act(zb[:, c, 0:w_], xT[:, c, lo:hi], AF.Copy, [("xT", c, g)], [("zb", c)])
            tt("pool", zsq[:, c, 0:w_], xT[:, c, lo:hi], xT[:, c, lo:hi], ALU.mult, [("xT", c, g)], [("zsq", c)])
            if c % 2 == 1:
                yield
        for c in range(KC):
            mm(ps[6][:, 0:w_], onesD[:], zb[:, c, 0:w_], c == 0, c == KC - 1, ["onesD", ("zb", c)], pk(6))
        for c in range(KC):
            mm(ps[7][:, 0:w_], onesD[:], zsq[:, c, 0:w_], c == 0, c == KC - 1, ["onesD", ("zsq", c)], pk(7))
        yield
        cp("dve", nmr[:, 0:w_], ps[6][:, 0:w_], pk(6), ["nmr" + kk])
        tt("dve", t1[:, 0:w_], nmr[:, 0:w_], nmr[:, 0:w_], ALU.mult, ["nmr" + kk], ["t1" + kk])
        tt("dve", t1[:, 0:w_], ps[7][:, 0:w_], t1[:, 0:w_], ALU.subtract, pk(7) + ["t1" + kk], ["t1" + kk])
        yield
        act(rstd[:, 0:w_], t1[:, 0:w_], AF.Ln, ["t1" + kk, "epst"], ["rstd" + kk], bias=epst[:, 0:1], scale=1.0)
        act(rstd[:, 0:w_], rstd[:, 0:w_], AF.Exp, ["rstd" + kk], ["rstd" + kk], scale=-0.5)
        yield
        stt("dve", nmr[:, 0:w_], nmr[:, 0:w_], -1.0, rstd[:, 0:w_], ALU.mult, ALU.mult, ["nmr" + kk, "rstd" + kk], ["nmr" + kk])

    def ln_loop(g, lo, hi, ss_, li, G, B, gbkey):
        w_ = hi - lo
        t1, rstd, nmr, kk = ss_["t1"], ss_["rstd"], ss_["nmr"], ss_["k"]
        for c in range(KC):
            xh = ss_["xh"] if c % 2 == 0 else t1
            kxh = ("xh" + kk) if c % 2 == 0 else ("t1" + kk)
            tt("dve", xh[:, 0:w_], xT[:, c, lo:hi], rstd[:, 0:w_], ALU.mult, [("xT", c, g), "rstd" + kk], [kxh])
            tt("dve", xh[:, 0:w_], xh[:, 0:w_], nmr[:, 0:w_], ALU.add, [kxh, "nmr" + kk], [kxh])
            act(xT[:, c, lo:hi], xh[:, 0:w_], AF.Identity, [kxh, "lng", "lnb"], [("xT", c, g)],
                bias=lnb[:, li * 8 + c:li * 8 + c + 1], scale=lng[:, li * 8 + c:li * 8 + c + 1])
            if G is not None:
                if g != "S":
                    act(uT[:, c, lo:hi], xh[:, 0:w_], AF.Identity, [kxh, gbkey], [("uT", c, g)],
                        bias=B[:, c, 0:1], scale=G[:, c, 0:1])
                else:
                    tt("dve", v3(xh[:, 0:64]), v3(xh[:, 0:64]), bc_s(G, c), ALU.mult, [kxh, gbkey], [kxh])
                    tt("dve", v3(uT[:, c, lo:hi]), v3(xh[:, 0:64]), bc_s(B, c), ALU.add, [kxh, gbkey], [("uT", c, g)])
            yield

    def layernorm(st, li, G, B, gbkey):
        grp = groups(st)
        lockstep([ln_pre(*grp[0], lnset[0])])
        for i, gg in enumerate(grp):
            gens = [ln_loop(*gg, lnset[i % 2], li, G, B, gbkey)]
            if i + 1 < len(grp):
                gens.append(ln_pre(*grp[i + 1], lnset[(i + 1) % 2]))
            lockstep(gens)

    def mixer_proj(st):
        grp = groups(st)
        NTx = NT if st == 0 else NTP
        if st == 0:
            dma("sp", pcs[:, :, 0:48], cq_d, [], ["pcs"], "cstate")
            dma("sp", zcs[:, :, 0:32], cm_d, [], ["zcs"], "cstate")
        pbank = {"A": 0, "B": 1}
        deferred = []
        for blk in range(14):
            wb, kw = ring_next()
            dma("pool", wb[:], win_d[blk], [], [kw], kw)
            for cpos in range(2):
                ci = blk * 2 + cpos
                par = ci % 2
                pkeys = {}
                prev_deferred, deferred = deferred, []
                for (g, lo, hi) in grp:
                    if g != "S":
                        bank = par * 2 + pbank[g]
                        o = ps[bank][:, :]
                    else:
                        bank = 4
                        o = ps[4][:, 0:64]
                    pkeys[g] = (o, pk(bank))
                    for k in range(KC):
                        mm(o, wb[:, k, cpos * 128:(cpos + 1) * 128], uT[:, k, lo:hi], k == 0, k == KC - 1,
                           [kw, ("uT", k, g)], pk(bank))
                if ci < 12:
                    j = ci
                    cvj, kcv = (cv, ("cv", 0)) if j % 2 == 0 else (cv2, ("cv", 1))
                    sqj, ksq = (sq, ("sq", 0)) if j % 2 == 0 else (sq2, ("sq", 1))
                    rvj, krv = (rinvp, "scc") if j % 2 == 0 else (rinv2, "rinv2")
                    p_ = pc[j % 2]
                    kpc = ("pc", j % 2)
                    cp("pool", p_[:, 0:3], halo_q[:, j, :], ["halo_q"], [kpc])
                    for (g, lo, hi) in grp:
                        o, kp = pkeys[g]
                        if g != "S":
                            act(p_[:, 3 + lo:3 + hi], o, AF.Copy, kp, [kpc])
                        else:
                            act(pcs[:, j, 48:112], o, AF.Copy, kp, ["pcs"])
                    cp("pool", halo_q[:, j, :], p_[:, NTP:NTP + 3], [kpc], ["halo_q"])
                    for t in range(4):
                        wsc = cqw[:, t, j:j + 1]
                        if t == 0:
                            ts("dve", cvj[:, 0:NTP], p_[:, 0:NTP], wsc, None, ALU.mult, None, [kpc, "cqw"], [kcv])
                        else:
                            stt("dve", cvj[:, 0:NTP], p_[:, t:t + NTP], wsc, cvj[:, 0:NTP], ALU.mult, ALU.add, [kpc, "cqw", kcv], [kcv])
                    if st == 0:
                        for t in range(4):
                            wsc = cqw[:, t, j:j + 1]
                            if t == 0:
                                ts("dve", cvj[:, NTP:NT], pcs[:, j, 0:64], wsc, None, ALU.mult, None, ["pcs", "cqw"], [kcv])
                            else:
                                stt("dve", cvj[:, NTP:NT], pcs[:, j, 16 * t:16 * t + 64], wsc, cvj[:, NTP:NT], ALU.mult, ALU.add,
                                    ["pcs", "cqw", kcv], [kcv])
                    def part2(j=j, cvj=cvj, kcv=kcv, sqj=sqj, ksq=ksq, rvj=rvj, krv=krv):
                        if j >= 8:
                            act(vT[:, j - 8, 0:NTx], cvj[:, 0:NTx], AF.Silu, [kcv], [("vT", j - 8)])
                            return
                        act(cvj[:, 0:NTx], cvj[:, 0:NTx], AF.Silu, [kcv], [kcv])
                        act(sqj[:, 0:NTx], cvj[:, 0:NTx], AF.Square, [kcv], [ksq])
                        isq = j < 4
                        for gi_, (g, lo, hi) in enumerate(grp):
                            bank = 5 + gi_
                            w_ = hi - lo
                            mm(ps[bank][:, 0:w_], ones1[:], sqj[:, lo:hi], True, True, ["ones1", ksq], pk(bank))
                            act(rvj[:, lo:hi], ps[bank][:, 0:w_], AF.Ln, pk(bank) + ["lnc"], [krv],
                                bias=lnc[:, 0:1] if isq else lnc[:, 1:2], scale=128.0 if isq else 1.0)
                        act(rvj[:, 0:NTx], rvj[:, 0:NTx], AF.Exp, [krv], [krv], scale=-0.5)
                        tt("dve", qkT[:, j, 0:NTx], cvj[:, 0:NTx], rvj[:, 0:NTx], ALU.mult, [kcv, krv], [("qkT", j)])
                    deferred.append(part2)
                elif ci < 16:
                    j = ci - 12
                    for (g, lo, hi) in grp:
                        o, kp = pkeys[g]
                        act(mix[:, j, lo:hi], o, AF.Silu, kp, [("mix", j)])
                elif ci < 20:
                    j = ci - 16
                    for (g, lo, hi) in grp:
                        o, kp = pkeys[g]
                        act(mix[:, 4 + j, lo:hi], o, AF.Copy, kp, [("mix", 4 + j)])
                else:
                    j, is_h = (ci - 20) // 2, (ci - 20) % 2
                    if not is_h:
                        for (g, lo, hi) in grp:
                            o, kp = pkeys[g]
                            act(scc[:, lo:hi], o, AF.Copy, kp, ["scc"])
                    else:
                        cp("pool", zc[:, 0:2], halo_m[:, j, :], ["halo_m"], ["zc"])
                        for (g, lo, hi) in grp:
                            o, kp = pkeys[g]
                            if g != "S":
                                tt("dve", zc[:, 2 + lo:2 + hi], scc[:, lo:hi], o, ALU.mult, ["scc"] + kp, ["zc"])
                            else:
                                tt("dve", zcs[:, j, 32:96], scc[:, lo:hi], o, ALU.mult, ["scc"] + kp, ["zcs"])
                        cp("pool", halo_m[:, j, :], zc[:, NTP:NTP + 2], ["zc"], ["halo_m"])
                        for t in range(3):
                            wsc = cmw[:, t, j:j + 1]
                            if t == 0:
                                ts("dve", cv[:, 0:NTP], zc[:, 0:NTP], wsc, None, ALU.mult, None, ["zc", "cmw"], [("cv", 0)])
                            else:
                                stt("dve", cv[:, 0:NTP], zc[:, t:t + NTP], wsc, cv[:, 0:NTP], ALU.mult, ALU.add, ["zc", "cmw", ("cv", 0)], [("cv", 0)])
                        if st == 0:
                            for t in range(3):
                                wsc = cmw[:, t, j:j + 1]
                                if t == 0:
                                    ts("dve", cv[:, NTP:NT], zcs[:, j, 0:64], wsc, None, ALU.mult, None, ["zcs", "cmw"], [("cv", 0)])
                                else:
                                    stt("dve", cv[:, NTP:NT], zcs[:, j, 16 * t:16 * t + 64], wsc, cv[:, NTP:NT], ALU.mult, ALU.add,
                                        ["zcs", "cmw", ("cv", 0)], [("cv", 0)])
                        tt("dve", mix[:, 4 + j, 0:NTx], mix[:, 4 + j, 0:NTx], cv[:, 0:NTx], ALU.mult, [("mix", 4 + j), ("cv", 0)], [("mix", 4 + j)])
                for fn_ in prev_deferred:
                    fn_()
        for fn_ in deferred:
            fn_()
        if st == 0:
            dma("sp", cqs_d, pcs[:, :, 64:112], ["pcs"], ["cq_s"], "cq_s")
            dma("sp", cms_d, zcs[:, :, 64:96], ["zcs"], ["cm_s"], "cm_s")
            out_keys.extend(["cq_s", "cm_s"])
        if st == 1:
            dma("sp", cqp_d, halo_q[:], ["halo_q"], ["cq_p"], "cq_p")
            dma("sp", cmp_d, halo_m[:], ["halo_m"], ["cm_p"], "cm_p")
            out_keys.extend(["cq_p", "cm_p"])

    def gdn_intra(C_, c0, tb, nfull, t):
        si = t.si
        P = slice(0, C_)
        tri, seqt, maskT, nstt = C("tri_" + tb), C("seq_" + tb), C("maskT_" + tb), C("nst_" + tb)
        ident = C("ident")

        def Bk(b):
            return ps[(b + 4 * si) % 8]

        def bk(b):
            return pk((b + 4 * si) % 8)

        def K(n, *a):
            return (n, si) + a

        def h3(x):
            return x.rearrange("p (h c) -> p h c", h=4)[:, :, 0:C_]
        ab_sb, g4, beta4, gcc, egc, kbgs, kdcs = t.ab_sb, t.g4, t.beta4, t.gcc, t.egc, t.kbgs, t.kdcs
        gbc, bbc, egr, dec, dnb, YT, XX = t.gbc, t.bbc, t.egr, t.dec, t.dnb, t.YT, t.XX
        for k in range(KC):
            mm(Bk(0)[P, 0:8], uT[:, k, c0:c0 + C_], wab[:, k, :], k == 0, k == KC - 1, [("uT", k, "A"), ("uT", k, "B"), ("uT", k, "S"), "wab"], bk(0))
        yield
        act(ab_sb[P, :], Bk(0)[P, 0:8], AF.Copy, bk(0), [K("ab_sb")])
        tt("dve", g4[P, :], ab_sb[P, 0:4], dtb[P, :], ALU.add, [K("ab_sb"), "dtb"], [K("g4")])
        act(g4[P, :], g4[P, :], AF.Exp, [K("g4")], [K("g4")])
        act(g4[P, :], g4[P, :], AF.Ln, [K("g4"), "epst"], [K("g4")], bias=epst[P, 2:3], scale=1.0)
        tt("dve", g4[P, :], g4[P, :], nea[P, :], ALU.mult, [K("g4"), "nea"], [K("g4")])
        act(beta4[P, :], ab_sb[P, 4:8], AF.Exp, [K("ab_sb")], [K("beta4")], scale=-1.0)
        ts("dve", beta4[P, :], beta4[P, :], 1.0, None, ALU.add, None, [K("beta4")], [K("beta4")])
        S.add("dve", lambda e: e.reciprocal(out=beta4[P, :], in_=beta4[P, :]), reads=[K("beta4")], writes=[K("beta4")])
        cp("dve", gbc[P, :, :], g4[P, :].unsqueeze(2).to_broadcast([C_, 4, 128]), [K("g4")], [K("gbc")])
        cp("dve", bbc[P, :, :], beta4[P, :].unsqueeze(2).to_broadcast([C_, 4, 128]), [K("beta4")], [K("bbc")])
        yield
        mm(Bk(1)[P, 0:4], tri[P, P], g4[P, :], True, True, ["cst", K("g4")], bk(1))
        mm(Bk(1)[P, 4:8], seqt[P, P], g4[P, :], True, True, ["cst", K("g4")], bk(1))
        for h in range(4):
            mm(Bk(2)[:, h * 128:h * 128 + C_], gbc[P, h, :], tri[P, P], True, True, [K("gbc"), "cst"], bk(2))
            mm(Bk(3)[:, h * 128:h * 128 + C_], bbc[P, h, :], identb[P, P], True, True, [K("bbc"), "identb"], bk(3))
            mm(Bk(1)[:, 64 + h * 16:64 + h * 16 + 16], gbc[P, h, :], seqt[P, 0:16], True, True, [K("gbc"), "cst"], bk(1))
        yield
        cp("dve", gcc[P, :], Bk(1)[P, 0:8], bk(1), [K("gcc")])
        act(t.glast2[:, :], Bk(1)[:, 64:128], AF.Copy, bk(1), [K("glast")])
        act(t.glast2[:, :], t.glast2[:, :], AF.Exp, [K("glast")], [K("glast")])
        act(egr[:, :, 0:C_], h3(Bk(2)[:, :]), AF.Copy, bk(2), [K("egr")])
        act(egr[:, :, 0:C_], egr[:, :, 0:C_], AF.Exp, [K("egr")], [K("egr")])
        act(egc[P, :], gcc[P, 0:4], AF.Exp, [K("gcc")], [K("egc")])
        tt("dve", kbgs[P, :], beta4[P, :], egc[P, :], ALU.mult, [K("beta4"), K("egc")], [K("kbgs")])
        tt("dve", kdcs[P, :], gcc[P, 4:8], gcc[P, 0:4], ALU.subtract, [K("gcc")], [K("kdcs")])
        act(kdcs[P, :], kdcs[P, :], AF.Exp, [K("kdcs")], [K("kdcs")])
        for h in range(4):
            stt("dve", dec[P, h, 0:C_], Bk(2)[P, h * 128:h * 128 + C_], gcc[P, h:h + 1], maskT[P, P], ALU.subtract, ALU.add,
                bk(2) + [K("gcc"), "cst"], [K("dec")])
        act(dec[P, :, 0:C_], dec[P, :, 0:C_], AF.Exp, [K("dec")], [K("dec")])
        yield
        for h in range(4):
            kh = qkT[:, 4 + h, c0:c0 + C_]
            qh = qkT[:, h, c0:c0 + C_]
            mm(Bk(4)[P, h * 128:h * 128 + C_], kh, kh, True, True, [("qkT", 4 + h)], bk(4))
            mm(Bk(5)[P, h * 128:h * 128 + C_], kh, qh, True, True, [("qkT", 4 + h), ("qkT", h)], bk(5))
        psb7 = Bk(2)[:, :].bitcast(BF16)
        for h in range(4):
            tr(psb7[P, h * 128:(h + 1) * 128], qkT[:, 4 + h, c0:c0 + C_], identb[:, :], [("qkT", 4 + h), "identb"], bk(2))
            tr(psb7[P, 512 + h * 128:512 + (h + 1) * 128], vT[:, h, c0:c0 + C_], identb[:, :], [("vT", h), "identb"], bk(2))
        yield
        tt("dve", dnb[P, :, 0:C_], dec[P, :, 0:C_], nstt[P, P].unsqueeze(1).to_broadcast([C_, 4, C_]), ALU.mult, [K("dec"), "cst"], [K("dnb")])
        tt("dve", dnb[P, :, 0:C_], h3(Bk(3)[P, :]), dnb[P, :, 0:C_], ALU.mult, bk(3) + [K("dnb")], [K("dnb")])
        Y0 = YT[0]
        tt("dve", Y0[P, :, 0, 0:C_], h3(Bk(4)[P, :]), dnb[P, :, 0:C_], ALU.mult, bk(4) + [K("dnb")], [K("YT", 0)])
        tt("dve", t.qkm[P, :, 0:C_], h3(Bk(5)[P, :]), dec[P, :, 0:C_], ALU.mult, bk(5) + [K("dec")], [K("qkm")])
        cp("dve", Y0[P, :, 1, 0:C_], identb[P, P].unsqueeze(1).to_broadcast([C_, 4, C_]), ["identb"], [K("YT", 0)])
        k3 = psb7[P, 0:512].rearrange("p (h d) -> p h d", h=4)
        v3_ = psb7[P, 512:1024].rearrange("p (h d) -> p h d", h=4)
        tt("dve", t.kbg[P, :, :], k3, kbgs[P, :].unsqueeze(2).to_broadcast([C_, 4, 128]), ALU.mult, bk(2) + [K("kbgs")], [K("kbg")])
        tt("dve", t.kdec[P, :, :], k3, kdcs[P, :].unsqueeze(2).to_broadcast([C_, 4, 128]), ALU.mult, bk(2) + [K("kdcs")], [K("kdec")])
        tt("dve", t.vb[P, :, :], v3_, beta4[P, :].unsqueeze(2).to_broadcast([C_, 4, 128]), ALU.mult, bk(2) + [K("beta4")], [K("vb")])
        yield
        psb6 = Bk(6)[:, :].bitcast(BF16)
        for h in range(4):
            tr(psb6[P, h * 128:h * 128 + C_], Y0[P, h, 0, 0:C_], identb[P, P], [K("YT", 0), "identb"], bk(6))
        yield
        cp("dve", XX[0][P, :, 0:C_], psb6[P, 0:512].rearrange("p (h c) -> p h c", h=4)[:, :, 0:C_], bk(6), [K("XX", 0)])
        yield
        cur = 0
        for s_ in range(nfull):
            nxt = 1 - cur
            for h in range(4):
                bank = 0 if h < 2 else 1
                off = (h % 2) * 256
                if C_ == 128:
                    mm(Bk(bank)[P, off:off + 256], XX[cur][P, h, 0:C_], YT[cur][P, h, :, :].rearrange("p t c -> p (t c)"), True, True,
                       [K("XX", cur), K("YT", cur)], bk(bank))
                else:
                    mm(Bk(bank)[P, off:off + C_], XX[cur][P, h, 0:C_], YT[cur][P, h, 0, 0:C_], True, True, [K("XX", cur), K("YT", cur)], bk(bank))
                    mm(Bk(bank)[P, off + 128:off + 128 + C_], XX[cur][P, h, 0:C_], YT[cur][P, h, 1, 0:C_], True, True, [K("XX", cur), K("YT", cur)], bk(bank))
                mm(Bk(2)[P, h * 128:h * 128 + C_], YT[cur][P, h, 0, 0:C_], XX[cur][P, h, 0:C_], True, True, [K("XX", cur), K("YT", cur)], bk(2))
            yield
            for bank in range(2):
                pv = Bk(bank)[P, :].rearrange("p (h t c) -> p h t c", h=2, t=2)
                act(YT[nxt][P, 2 * bank:2 * bank + 2, 0, 0:C_], pv[:, :, 0, 0:C_], AF.Copy, bk(bank), [K("YT", nxt)])
                tt("dve", YT[nxt][P, 2 * bank:2 * bank + 2, 1, 0:C_], pv[:, :, 1, 0:C_], YT[cur][P, 2 * bank:2 * bank + 2, 1, 0:C_], ALU.add,
                   bk(bank) + [K("YT", cur)], [K("YT", nxt)])
            act(XX[nxt][P, :, 0:C_], h3(Bk(2)[P, :]), AF.Copy, bk(2), [K("XX", nxt)])
            yield
            cur = nxt
        nxt = 1 - cur
        for h in range(4):
            mm(Bk(0)[P, h * 128:h * 128 + C_], XX[cur][P, h, 0:C_], YT[cur][P, h, 1, 0:C_], True, True, [K("XX", cur), K("YT", cur)], bk(0))
        yield
        tt("dve", YT[nxt][P, :, 1, 0:C_], h3(Bk(0)[P, :]), YT[cur][P, :, 1, 0:C_], ALU.add, bk(0) + [K("YT", cur)], [K("YT", nxt)])
        Tm = YT[nxt]
        kT_ = K("YT", nxt)
        yield
        for h in range(4):
            mm(Bk(3)[:, h * 128:h * 128 + C_], t.kbg[P, h, :], Tm[P, h, 1, 0:C_], True, True, [K("kbg"), kT_], bk(3))
        yield
        act(t.nwT[:, :, 0:C_], h3(Bk(3)[:, :]), AF.Copy, bk(3), [K("nwT")], scale=-1.0)
        q3 = qkT[:, 0:4, c0:c0 + C_]
        tt("dve", t.qdT[:, :, 0:C_], q3, egr[:, :, 0:C_], ALU.mult, [("qkT", h) for h in range(4)] + [K("egr")], [K("qdT")])
        return Tm, kT_

    def lockstep(gens):
        res = [None] * len(gens)
        live = list(range(len(gens)))
        while live:
            for i in list(live):
                try:
                    next(gens[i])
                except StopIteration as e:
                    res[i] = e.value
                    live.remove(i)
        return res

    def gdn_out(C_, c0, t):
        si = t.si

        def Bk(b):
            return ps[(b + 4 * si) % 8]

        def bk(b):
            return pk((b + 4 * si) % 8)

        def K(n, *a):
            return (n, si) + a

        def h3(x):
            return x.rearrange("p (h c) -> p h c", h=4)[:, :, 0:C_]
        act(t.osq[:, :, 0:C_], h3(Bk(5)[:, :]), AF.Square, bk(5), [K("osq")])
        yield
        for h in range(4):
            mm(Bk(7)[:, h * 128:h * 128 + C_], ones128[:], t.osq[:, h, 0:C_], True, True, ["ones128", K("osq")], bk(7))
        yield
        act(t.rinv[:, :, 0:C_], h3(Bk(7)[:, :]), AF.Ln, bk(7) + ["epst"], [K("rinv")], bias=epst[:, 1:2], scale=1.0)
        act(t.rinv[:, :, 0:C_], t.rinv[:, :, 0:C_], AF.Exp, [K("rinv")], [K("rinv")], scale=-0.5)
        yield
        tt("dve", t.rinv[:, :, 0:C_], h3(Bk(5)[:, :]), t.rinv[:, :, 0:C_], ALU.mult, bk(5) + [K("rinv")], [K("rinv")])
        stt("dve", mix[:, 0:4, c0:c0 + C_], t.rinv[:, :, 0:C_], dng[:, 0:1], mix[:, 0:4, c0:c0 + C_], ALU.mult, ALU.mult,
            [K("rinv"), "dng"] + [("mix", h) for h in range(4)], [("mix", h) for h in range(4)])

    def gdn_prompt_inter(c0, t, Tm, kT_):
        C_ = 128
        P = slice(0, C_)
        si = t.si

        def Bk(b):
            return ps[(b + 4 * si) % 8]

        def bk(b):
            return pk((b + 4 * si) % 8)

        def K(n, *a):
            return (n, si) + a
        for h in range(4):
            mm(Bk(4)[P, h * 128:(h + 1) * 128], Tm[P, h, 1, 0:C_], t.vb[P, h, :], True, False, [kT_, K("vb")], bk(4))
            mm(Bk(4)[P, h * 128:(h + 1) * 128], t.nwT[:, h, 0:C_], Sbf[:, h, :], False, True, [K("nwT"), "Sbf"], bk(4))
        act(t.vnew[P, :, :], Bk(4)[P, :].rearrange("p (h d) -> p h d", h=4), AF.Copy, bk(4), [K("vnew")])
        for h in range(4):
            mm(Bk(5)[:, h * 128:h * 128 + C_], Sbf[:, h, :], t.qdT[:, h, 0:C_], True, False, ["Sbf", K("qdT")], bk(5))
            mm(Bk(5)[:, h * 128:h * 128 + C_], t.vnew[P, h, :], t.qkm[P, h, 0:C_], False, True, [K("vnew"), K("qkm")], bk(5))
        for h in range(4):
            mm(Bk(6)[:, h * 128:(h + 1) * 128], t.kdec[P, h, :], t.vnew[P, h, :], True, True, [K("kdec"), K("vnew")], bk(6))
        for h in range(4):
            stt("dve", Sst[:, h, :], Sst[:, h, :], t.glast[:, h, 0:1], Bk(6)[:, h * 128:(h + 1) * 128], ALU.mult, ALU.add,
                ["Sst", K("glast")] + bk(6), ["Sst"])
        act(Sbf[:, :, :], Sst[:, :, :], AF.Copy, ["Sst"], ["Sbf"])

    def gdn_prompt_pair(c0a, c0b):
        if "SEQ" in debug:
            r = lockstep([gdn_intra(128, c0a, "p", 6, TS[0])]) + lockstep([gdn_intra(128, c0b, "p", 6, TS[1])])
        else:
            r = lockstep([gdn_intra(128, c0a, "p", 6, TS[0]), gdn_intra(128, c0b, "p", 6, TS[1])])
        if c0a == 0 and "pairdump" in debug and not dbg_outs:
            for si in range(2):
                t = TS[si]
                for nm in ("kbg", "kdec", "vb", "nwT", "qdT", "qkm"):
                    debug_dump(nm + str(si), getattr(t, nm)[:, :, :], [128, 4, 128], [(nm, si)], BF16)
                debug_dump("T" + str(si), r[si][0][:, :, :, :], [128, 4, 2, 128], [r[si][1]], BF16)
                debug_dump("egr" + str(si), t.egr[:, :, :], [128, 4, 128], [("egr", si)], F32)
                debug_dump("dec" + str(si), t.dec[:, :, :], [128, 4, 128], [("dec", si)], F32)
            raise _Stop()
        gdn_prompt_inter(c0a, TS[0], *r[0])
        gdn_prompt_inter(c0b, TS[1], *r[1])
        lockstep([gdn_out(128, c0a, TS[0]), gdn_out(128, c0b, TS[1])])

    def gdn_sample():
        C_ = NS
        c0 = NTP
        P = slice(0, C_)
        t = TS[0]
        (Tm, kT_), = lockstep([gdn_intra(C_, c0, "s", 1, t)])
        K = lambda n, *a: (n, 0) + a
        selP = C("selP")
        for h in range(4):
            s0f = S0[h % 2]
            ks0 = ("S0", h % 2)
            dma("sp", s0f[:, :, :], s0_d[:, :, h, :], [], [ks0], ks0)
            dma("pool", S0b[:, :, :], s0_d[:, :, h, :], [], ["S0b"], "S0b")
            tt("dve", nwTm[:, :, :], t.nwT[:, h, 0:C_].unsqueeze(1).to_broadcast([128, NB, C_]), selb[:, :, :], ALU.mult, [K("nwT"), "selb"], ["nwTm"])
            tt("dve", qdTm[:, :, :], t.qdT[:, h, 0:C_].unsqueeze(1).to_broadcast([128, NB, C_]), selb[:, :, :], ALU.mult, [K("qdT"), "selb"], ["qdTm"])
            tt("dve", kdm[P, :, :], t.kdec[P, h, :].unsqueeze(1).to_broadcast([C_, NB, 128]),
               selP[P, 0:NB].unsqueeze(2).to_broadcast([C_, NB, 128]), ALU.mult, [K("kdec"), "cst"], ["kdm"])
            mm(ps[4][P, 0:128], Tm[P, h, 1, 0:C_], t.vb[P, h, :], True, False, [kT_, K("vb")], pk(4))
            for b in range(NB):
                mm(ps[4][P, 0:128], nwTm[:, b, :], S0b[:, b, :], False, b == NB - 1, ["nwTm", "S0b"], pk(4))
            act(t.vnew[P, 0, :], ps[4][P, 0:128], AF.Copy, pk(4), [K("vnew")])
            for b in range(NB):
                mm(ps[5][:, h * 128:h * 128 + C_], S0b[:, b, :], qdTm[:, b, :], b == 0, False, ["qdTm", "S0b"], pk(5))
            mm(ps[5][:, h * 128:h * 128 + C_], t.vnew[P, 0, :], t.qkm[P, h, 0:C_], False, True, [K("vnew"), K("qkm")], pk(5))
            for b in range(NB):
                mm(ps[b // 4][:, (b % 4) * 128:(b % 4 + 1) * 128], kdm[P, b, :], t.vnew[P, 0, :], True, True, ["kdm", K("vnew")], pk(b // 4))
            for q in range(4):
                tt("dve", s0f[:, 4 * q:4 * q + 4, :], s0f[:, 4 * q:4 * q + 4, :],
                   t.glast[:, h, 4 * q:4 * q + 4].unsqueeze(2).to_broadcast([128, 4, 128]), ALU.mult, [ks0, K("glast")], [ks0])
                tt("dve", s0f[:, 4 * q:4 * q + 4, :], s0f[:, 4 * q:4 * q + 4, :], ps[q][:, :].rearrange("p (b d) -> p b d", b=4), ALU.add,
                   [ks0] + pk(q), [ks0])
            dma("sp", ssms_d[:, :, h, :], s0f[:, :, :], [ks0], [("ssm_s", h)], ks0)
            out_keys.append(("ssm_s", h))
        lockstep([gdn_out(C_, c0, t)])

    def mixer_out(st):
        for m in range(KC):
            for (g, lo, hi) in groups(st):
                if g != "S":
                    bank = (m % 2) * 2 + (0 if g == "A" else 1)
                    o = ps[bank][:, :]
                else:
                    bank = 4 + m % 2
                    o = ps[bank][:, 0:64]
                kp = pk(bank)
                for k in range(KC):
                    mm(o, wout[:, k, m * 128:(m + 1) * 128], mix[:, k, lo:hi], k == 0, k == KC - 1, ["wout", ("mix", k)], kp)
                if g != "S":
                    stt("dve", xT[:, m, lo:hi], o, gsc[1][:, m, 0:1], xT[:, m, lo:hi], ALU.mult, ALU.add,
                        kp + [("gsc", 1), ("xT", m, g)], [("xT", m, g)])
                else:
                    tt("dve", v3(st_xh[:, 0:64]), v3(o), bc_s(gsc[1], m), ALU.mult, kp + [("gsc", 1)], ["st_xh"])
                    tt("dve", xT[:, m, lo:hi], st_xh[:, 0:64], xT[:, m, lo:hi], ALU.add, ["st_xh", ("xT", m, g)], [("xT", m, g)])

    def store_y(st):
        for gi_, (g, lo, hi) in enumerate((("A", 0, 512), ("B", 512, 1024))):
            dma("sp", yp_d[:, :, st * NTP + lo:st * NTP + hi], xT[:, :, lo:hi], [("xT", c, g) for c in range(KC)],
                [("yp", st, g)], ("yout", st, g))
            out_keys.append(("yp", st, g))
        if st == 0:
            dma("sp", ys_d, xT[:, :, NTP:NT], [("xT", c, "S") for c in range(KC)], ["ys"], "ysout")
            out_keys.append("ys")

    try:
        for st in range(2 if stop_after != "ada" else 0):
            load_x(st)
            if stop_after == "load":
                store_y(st)
                continue
            modulate0(st)
            if stop_after == "mod0":
                S.fence()
                store_y(st)
                continue
            ffn(st, 0, 0)
            if stop_after in ("ffn1", "up"):
                store_y(st)
                continue
            layernorm(st, 0, G1, B1, ("GB", 1))
            S.fence()
            if stop_after == "ln1":
                store_y(st)
                continue
            mixer_proj(st)
            S.fence()
            if stop_after == "proj":
                continue
            dma("pool", wout[:, :, :], wout_d, [], ["wout"], "wout")
            for n in range(4):
                gdn_prompt_pair(2 * n * 128, (2 * n + 1) * 128)
            if st == 0:
                S.fence()
                gdn_sample()
            S.fence()
            if st == 1:
                dma("sp", ssmp_d, Sst[:, :, :], ["Sst"], ["ssm_p"], "ssm_p")
                out_keys.append("ssm_p")
            if stop_after == "gdn":
                continue
            mixer_out(st)
            layernorm(st, 1, G2, B2, ("GB", 2))
            S.fence()
            if stop_after == "ln2":
                store_y(st)
                continue
            ffn(st, 1, 2)
            layernorm(st, 2, None, None, None)
            store_y(st)


    except _Stop:
        pass

    S.add("sp", lambda e: e.nop(), reads=out_keys)
    with nc.Block() as block:
        S.finalize(block)
    return nc, dbg_outs


def _blk(w, nb):
    return np.ascontiguousarray(w.reshape(KC, 128, nb, 256).transpose(2, 1, 0, 3))


def _prep_shared(inp):
    f = lambda a: np.asarray(a, dtype=np.float32)
    sh = {}
    sh["wada"] = _blk(f(inp["w_ada"])[0], 36)
    sh["bada"] = np.ascontiguousarray(f(inp["b_ada"])[0].reshape(72, 128).T)
    sh["lng"] = np.ascontiguousarray(f(inp["ln_g"])[0].reshape(24, 128).T)
    sh["lnb"] = np.ascontiguousarray(f(inp["ln_b"])[0].reshape(24, 128).T)
    for i, nm in ((1, "ffn1"), (2, "ffn2")):
        sh["wg%d" % i] = _blk(f(inp[nm + "_wg"])[0], 11)
        sh["wu%d" % i] = _blk(f(inp[nm + "_wu"])[0], 11)
        wd = f(inp[nm + "_wd"])[0]
        sh["wd%d" % i] = np.ascontiguousarray(wd.reshape(FC, 128, KC, 128).transpose(2, 1, 0, 3))
    win = f(inp["w_in"])[0]
    qkv, ab, og = win[:, 0:1536], win[:, 1536:1544], win[:, 1544:2056]
    scb, scc, sch = win[:, 2056:2568], win[:, 2568:3080], win[:, 3080:3592]
    inter = np.concatenate([np.concatenate([scc[:, j * 128:(j + 1) * 128], sch[:, j * 128:(j + 1) * 128]], axis=1)
                            for j in range(4)], axis=1)
    main = np.concatenate([qkv, og, scb, inter], axis=1)
    sh["win"] = _blk(main, 14)
    sh["wab"] = np.ascontiguousarray(ab.reshape(KC, 128, 8).transpose(1, 0, 2))
    sh["wout"] = np.ascontiguousarray(f(inp["w_out"])[0].reshape(KC, 128, D).transpose(1, 0, 2))
    sh["cqw"] = np.ascontiguousarray(f(inp["conv_qkv_w"])[0].reshape(4, 12, 128).transpose(2, 0, 1))
    sh["cmw"] = np.ascontiguousarray(f(inp["conv_mix_w"])[0].reshape(3, 4, 128).transpose(2, 0, 1))
    sh["dng"] = np.ascontiguousarray(f(inp["dn_norm_g"])[0].reshape(128, 1))
    sh["alog"] = np.ascontiguousarray(f(inp["a_log"])[0])
    sh["dtb"] = np.ascontiguousarray(f(inp["dt_bias"])[0])
    sh["cst"] = _CST
    sh["sel"] = _SEL
    return sh


def _prep_core(inp, c):
    f = lambda a: np.asarray(a, dtype=np.float32)
    m = {}
    xp = f(inp["x_prompt"])[c]
    m["xp"] = np.ascontiguousarray(xp.T.reshape(KC, 128, NP).transpose(1, 0, 2))
    xs = f(inp["x_sample"])[NB * c:NB * (c + 1)]
    xs = xs.transpose(1, 0, 2).reshape(NS, D)
    m["xs"] = np.ascontiguousarray(xs.T.reshape(KC, 128, NS).transpose(1, 0, 2))
    m["cc"] = np.ascontiguousarray(np.concatenate([f(inp["c_prompt"])[c:c + 1], f(inp["c_sample"])[NB * c:NB * (c + 1)]], axis=0))
    m["s0"] = np.ascontiguousarray(f(inp["state_ssm"])[0, NB * c:NB * (c + 1)].transpose(2, 0, 1, 3))
    cq = f(inp["state_conv_qkv"])[0, NB * c:NB * (c + 1)]
    m["cq"] = np.ascontiguousarray(cq.reshape(NB, 3, 12, 128).transpose(3, 2, 1, 0).reshape(128, 12, 48))
    cm = f(inp["state_conv_mix"])[0, NB * c:NB * (c + 1)]
    m["cm"] = np.ascontiguousarray(cm.reshape(NB, 2, 4, 128).transpose(3, 2, 1, 0).reshape(128, 4, 32))
    return m


def _run(inp, debug=(), stop_after=None, trace=False, ncores=8):
    nc, dbg = build_program(debug=debug, stop_after=stop_after)
    sh = _prep_shared(inp)
    in_maps = []
    for c in range(ncores):
        m = dict(sh)
        m.update(_prep_core(inp, c))
        in_maps.append(m)
    res = run_bass_kernel_spmd(nc, in_maps, core_ids=list(range(ncores)), trace=trace)
    return res


def _assemble(res):
    R = list(res.results)
    while len(R) < 8:
        R.append(R[0])
    y_p = np.stack([R[c]["yp"].transpose(1, 0, 2).reshape(D, NP).T for c in range(8)])
    ys = []
    for c in range(8):
        a = R[c]["ys"].transpose(1, 0, 2).reshape(D, NS).T
        ys.append(a.reshape(4, NB, D).transpose(1, 0, 2))
    y_s = np.concatenate(ys, axis=0)
    ssm_p = np.stack([R[c]["ssm_p"].transpose(1, 0, 2) for c in range(8)])[None]
    cq_p = np.stack([R[c]["cq_p"].transpose(2, 1, 0).reshape(3, 1536) for c in range(8)])[None]
    cm_p = np.stack([R[c]["cm_p"].transpose(2, 1, 0).reshape(2, 512) for c in range(8)])[None]
    ssm_s = np.concatenate([R[c]["ssm_s"].transpose(1, 2, 0, 3) for c in range(8)], axis=0)[None]
    cq_s = np.concatenate([R[c]["cq_s"].reshape(128, 12, 3, NB).transpose(3, 2, 1, 0).reshape(NB, 3, 1536)
                           for c in range(8)], axis=0)[None]
    cm_s = np.concatenate([R[c]["cm_s"].reshape(128, 4, 2, NB).transpose(3, 2, 1, 0).reshape(NB, 2, 512)
                           for c in range(8)], axis=0)[None]
    outs = (y_p, y_s, ssm_p, cq_p, cm_p, ssm_s, cq_s, cm_s)
    return tuple(np.ascontiguousarray(o, dtype=np.float32) for o in outs)


def kernel(**inputs):
    res = _run(inputs)
    return _assemble(res)
```

```python
import numpy as np
import concourse.bass as bass
import concourse.mybir as mybir
from concourse.bass_utils import run_bass_kernel_spmd

F32 = mybir.dt.float32
BF16 = mybir.dt.bfloat16
AF = mybir.ActivationFunctionType
ALU = mybir.AluOpType

D = 1024
KC = 8
FF = 2816
FC = 22
NP = 2048
NTP = 1024
NS = 64
NT = NTP + NS
NB = 16
ALPHA = 2.0 ** 0.25
LN_EPS = 1e-5
RMS_EPS = 1e-6
BIGNEG = -1.0e5
ENGS = ("pe", "act", "dve", "pool", "sp")


class _Op:
    __slots__ = ("eng", "fn", "dma", "chan", "eidx", "deps", "signal", "count", "waits")


class Sched:
    def __init__(self, nc):
        self.nc = nc
        self.ops = {e: [] for e in ENGS}
        self.lastw = {}
        self.readers = {}
        self.chan_n = {}
        self.chan_last = {}

    def add(self, eng, fn, reads=(), writes=(), dma=False, chan=None):
        op = _Op()
        op.eng, op.fn, op.dma, op.chan = eng, fn, dma, chan
        op.eidx = len(self.ops[eng])
        op.signal = False
        op.count = None
        op.waits = []
        deps = []
        for r in reads:
            w = self.lastw.get(r)
            if w is not None:
                deps.append(w)
        for w_ in writes:
            w = self.lastw.get(w_)
            if w is not None:
                deps.append(w)
            deps.extend(self.readers.get(w_, ()))
        for r in reads:
            self.readers.setdefault(r, []).append(op)
        for w_ in writes:
            self.lastw[w_] = op
            self.readers[w_] = []
        op.deps = [d for d in deps if d is not op]
        if dma:
            n = self.chan_n.get(chan, 0) + 1
            self.chan_n[chan] = n
            op.count = 16 * n
            op.signal = True
            prev = self.chan_last.get(chan)
            if prev is not None:
                op.deps.append(prev)
            self.chan_last[chan] = op
        self.ops[eng].append(op)
        return op

    def fence(self):
        lasts = [self.ops[e][-1] for e in ENGS if self.ops[e]]
        lastdma = list(self.chan_last.values())
        for e in ENGS:
            op = self.add(e, lambda en: en.nop())
            op.deps = [d for d in lasts if d is not op] + lastdma

    def finalize(self, block):
        nc = self.nc
        for e in ENGS:
            wd = {}
            for op in self.ops[e]:
                need = {}
                for d in op.deps:
                    if d.dma:
                        key, val = ("c", d.chan), d.count
                    else:
                        if d.eng == e and not op.dma and e == "pe":
                            continue
                        key, val = ("e", d.eng), d.eidx
                    if key not in need or need[key][0] < val:
                        need[key] = (val, d)
                for key, (val, d) in need.items():
                    if wd.get(key, -1) >= val:
                        continue
                    wd[key] = val
                    d.signal = True
                    op.waits.append(d)
        esem = {}
        for e in ENGS:
            c = 0
            for op in self.ops[e]:
                if not op.dma and op.signal:
                    c += 1
                    op.count = c
            esem[e] = nc.alloc_semaphore("s_" + e)
        csem = {ch: nc.alloc_semaphore("c_%d" % i) for i, ch in enumerate(self.chan_n)}
        engmap = {"pe": block.tensor, "act": block.scalar, "dve": block.vector,
                  "pool": block.gpsimd, "sp": block.sync}

        def mk(e):
            def body(en):
                for op in self.ops[e]:
                    for d in op.waits:
                        if d.dma:
                            en.wait_ge(csem[d.chan], d.count)
                        else:
                            en.wait_ge(esem[d.eng], d.count)
                    ins = op.fn(en)
                    if op.dma:
                        ins.then_inc(csem[op.chan], 16)
                    elif op.signal:
                        ins.then_inc(esem[e], 1)
            return body

        for e in ENGS:
            if self.ops[e]:
                engmap[e](mk(e))


def _const_tables():
    i = np.arange(128)
    cols = {}
    ident = np.eye(128, dtype=np.float32)
    tri_p = (i[:, None] <= i[None, :]).astype(np.float32)
    seq_p = np.ones((128, 128), np.float32)
    maskT_p = np.where(i[None, :] >= i[:, None], 0.0, BIGNEG).astype(np.float32)
    nst_p = np.where(i[None, :] > i[:, None], -1.0, 0.0).astype(np.float32)
    same = (i[:, None] % 16) == (i[None, :] % 16)
    valid = (i[:, None] < 64) & (i[None, :] < 64)
    tri_s = (same & (i[:, None] <= i[None, :]) & valid).astype(np.float32)
    seq_s = (same & valid).astype(np.float32)
    maskT_s = np.where(same & (i[None, :] >= i[:, None]) & valid, 0.0, BIGNEG).astype(np.float32)
    nst_s = np.where(same & (i[None, :] > i[:, None]) & valid, -1.0, 0.0).astype(np.float32)
    s = np.arange(64)
    sel = (s[None, :] % 16 == np.arange(16)[:, None]).astype(np.float32).reshape(1, 16 * 64)
    sel = np.repeat(sel, 128, axis=0)
    selP = ((i[:, None] % 16) == np.arange(16)[None, :]).astype(np.float32)
    blocks = [("ident", ident), ("tri_p", tri_p), ("seq_p", seq_p), ("maskT_p", maskT_p), ("nst_p", nst_p),
              ("tri_s", tri_s), ("seq_s", seq_s), ("maskT_s", maskT_s), ("nst_s", nst_s),
              ("selP", selP)]
    off = 0
    for name, a in blocks:
        cols[name] = (off, a.shape[1])
        off += a.shape[1]
    tab = np.concatenate([a for _, a in blocks], axis=1).astype(np.float32)
    return tab, cols, sel


_CST, _CSTCOLS, _SEL = _const_tables()
NCST = _CST.shape[1]


class _Stop(Exception):
    pass


def build_program(debug=(), stop_after=None):
    nc = bass.Bass("TRN2", target_bir_lowering=False)
    S = Sched(nc)
    dbg_outs = {}

    def din(name, shape, dt=F32):
        return nc.dram_tensor(name, list(shape), dt, kind="ExternalInput").ap()

    def dout(name, shape, dt=F32):
        return nc.dram_tensor(name, list(shape), dt, kind="ExternalOutput").ap()

    xp_d = din("xp", [128, KC, NP])
    xs_d = din("xs", [128, KC, NS])
    cc_d = din("cc", [17, D])
    s0_d = din("s0", [128, NB, 4, 128])
    cq_d = din("cq", [128, 12, 48])
    cm_d = din("cm", [128, 4, 32])
    wada_d = din("wada", [36, 128, KC, 256])
    bada_d = din("bada", [128, 72])
    lng_d = din("lng", [128, 24])
    lnb_d = din("lnb", [128, 24])
    wg_d = [din("wg1", [11, 128, KC, 256]), din("wg2", [11, 128, KC, 256])]
    wu_d = [din("wu1", [11, 128, KC, 256]), din("wu2", [11, 128, KC, 256])]
    wd_d = [din("wd1", [8, 128, FC, 128]), din("wd2", [8, 128, FC, 128])]
    win_d = din("win", [14, 128, KC, 256])
    wab_d = din("wab", [128, KC, 8])
    wout_d = din("wout", [128, KC, D])
    cqw_d = din("cqw", [128, 4, 12])
    cmw_d = din("cmw", [128, 3, 4])
    dng_d = din("dng", [128, 1])
    alog_d = din("alog", [4])
    dtb_d = din("dtb", [4])
    cst_d = din("cst", [128, NCST])
    sel_d = din("sel", [128, NB * NS])

    yp_d = dout("yp", [128, KC, NP])
    ys_d = dout("ys", [128, KC, NS])
    ssmp_d = dout("ssm_p", [128, 4, 128])
    cqp_d = dout("cq_p", [128, 12, 3])
    cmp_d = dout("cm_p", [128, 4, 2])
    ssms_d = dout("ssm_s", [128, NB, 4, 128])
    cqs_d = dout("cq_s", [128, 12, 48])
    cms_d = dout("cm_s", [128, 4, 32])
    out_keys = []

    def sb(name, shape, dt=F32):
        return nc.alloc_sbuf_tensor("sb_" + name, list(shape), dt)

    xT = sb("xT", [128, KC, NT])
    uT = sb("uT", [128, KC, NT], BF16)
    cst = sb("cst", [128, NCST])
    identb = sb("identb", [128, 128], BF16)
    selb = sb("selb", [128, NB, NS], BF16)
    onesD = sb("onesD", [128, 128], BF16)
    ones128 = sb("ones128", [128, 128], BF16)
    epst = sb("epst", [128, 4])
    bada = sb("bada", [128, 72])
    lng = sb("lng", [128, 24])
    lnb = sb("lnb", [128, 24])
    modall = sb("modall", [128, 72, 17])
    sc1p0 = sb("sc1p0", [128, KC, 17])
    G1 = sb("G1", [128, KC, 17]); B1 = sb("B1", [128, KC, 17])
    G2 = sb("G2", [128, KC, 17]); B2 = sb("B2", [128, KC, 17])
    gsc = [sb("gsc%d" % i, [128, KC, 17]) for i in range(3)]
    cqw = sb("cqw", [128, 4, 12]); cmw = sb("cmw", [128, 3, 4]); dng = sb("dng", [128, 1])
    alog = sb("alog", [128, 4]); dtb = sb("dtb", [128, 4]); nea = sb("nea", [128, 4])
    wab = sb("wab", [128, KC, 8], BF16)
    Sst = sb("Sst", [128, 4, 128]); Sbf = sb("Sbf", [128, 4, 128], BF16)
    halo_q = sb("halo_q", [128, 12, 3]); halo_m = sb("halo_m", [128, 4, 2])
    R1 = sb("R1", [128, 24576 // 4])
    R2 = sb("R2", [128, (FC * NT * 2) // 4])
    wblk = [sb("wblk%d" % i, [128, KC, 256], BF16) for i in range(4)]
    R4 = sb("R4", [128, 16384 // 4])
    R5 = sb("R5", [128, 32768 // 4])
    ps = [nc.alloc_psum_tensor("ps%d" % i, [128, 512], F32) for i in range(8)]

    def view(t, byte_off, shape, dt):
        esz = 2 if dt == BF16 else 4
        n = int(np.prod(shape[1:]))
        a = t[:, byte_off // 4:(byte_off + n * esz + 3) // 4]
        if dt == BF16:
            a = a.bitcast(BF16)[:, 0:n]
        if len(shape) == 3:
            a = a.rearrange("p (a b) -> p a b", a=shape[1])
        elif len(shape) == 4:
            a = a.rearrange("p (a b c) -> p a b c", a=shape[1], b=shape[2])
        return a

    zb = view(R1, 0, [128, KC, 512], BF16)
    zsq = view(R1, 8192, [128, KC, 512], BF16)
    st_t1 = view(R1, 16384, [128, 512], F32)
    st_rstd = view(R1, 18432, [128, 512], F32)
    st_nmr = view(R1, 20480, [128, 512], F32)
    st_xh = view(R1, 22528, [128, 512], F32)
    hT = view(R2, 0, [128, FC, NT], BF16)
    wdb = [view(R4, i * 5632, [128, FC, 128], BF16) for i in range(2)]
    sg_t = [view(R5, i * 1024, [128, 512], BF16) for i in range(2)]
    cc_sb = view(R5, 4096, [17 if False else 128, D], F32)
    scT = view(R5, 4096 + 4096, [128, KC, 17], BF16)

    qkT = view(R2, 0, [128, 8, NT], BF16)
    vT = view(R2, 17408, [128, 4, NT], BF16)
    mix = view(R2, 26112, [128, 8, NT], BF16)
    kdm = view(R2, 43520, [128, NB, 128], BF16)
    wout = view(R4, 0, [128, KC, D], BF16)
    pc = [view(R5, i * 4112, [128, 3 + NTP], F32) for i in range(2)]
    cv = view(R5, 8224, [128, NT], F32)
    sq = view(R5, 12576, [128, NT], BF16)
    scc = view(R5, 14752, [128, NT], F32)
    rinvp = scc
    cv2 = view(R1, 0, [128, NT], F32)
    sq2 = view(R1, 4352, [128, NT], BF16)
    rinv2 = view(R1, 6528, [128, NT], F32)
    zc = view(R5, 19104, [128, 2 + NTP], F32)
    pcs = view(R5, 23216, [128, 12, 112], F32)
    zcs = view(R5, 28592, [128, 4, 96], F32)
    class _NS:
        pass

    def gdn_set(si):
        def vw(off, shape, dt):
            if si == 0:
                return view(R5, off, shape, dt)
            if off < 24064:
                return view(R1, off, shape, dt)
            return view(R5, 29184 + off - 24064, shape, dt)
        t = _NS()
        t.si = si
        t.ab_sb = vw(0, [128, 8], F32)
        t.g4 = vw(32, [128, 4], F32)
        t.beta4 = vw(48, [128, 4], F32)
        t.gcc = vw(64, [128, 8], F32)
        t.egc = vw(96, [128, 4], F32)
        t.kbgs = vw(112, [128, 4], F32)
        t.kdcs = vw(128, [128, 4], F32)
        t.glast = vw(256, [128, 4, 16], F32)
        t.glast2 = vw(256, [128, 64], F32)
        t.gbc = vw(512, [128, 4, 128], F32)
        t.bbc = vw(2560, [128, 4, 128], BF16)
        t.egr = vw(4608, [128, 4, 128], F32)
        t.dec = vw(6656, [128, 4, 128], F32)
        t.dnb = vw(8704, [128, 4, 128], F32)
        t.YT = [vw(10752 + i * 2048, [128, 4, 2, 128], BF16) for i in range(2)]
        t.XX = [vw(14848 + i * 1024, [128, 4, 128], BF16) for i in range(2)]
        t.qkm = vw(16896, [128, 4, 128], BF16)
        t.kbg = vw(17920, [128, 4, 128], BF16)
        t.kdec = vw(18944, [128, 4, 128], BF16)
        t.vb = vw(19968, [128, 4, 128], BF16)
        t.nwT = vw(20992, [128, 4, 128], BF16)
        t.qdT = vw(22016, [128, 4, 128], BF16)
        t.vnew = vw(23040, [128, 4, 128], BF16)
        t.osq = vw(24064, [128, 4, 128], BF16)
        t.rinv = vw(25088, [128, 4, 128], F32)
        return t

    TS = [gdn_set(0), gdn_set(1)]
    S0 = [view(R1, 0, [128, NB, 128], F32), view(R1, 16384, [128, NB, 128], F32)]
    S0b = view(R1, 8192, [128, NB, 128], BF16)
    nwTm = view(R1, 12288, [128, NB, NS], BF16)
    qdTm = view(R1, 14336, [128, NB, NS], BF16)
    ones1 = sb("ones1", [128, 128], BF16)
    lnc = sb("lnc", [128, 2])

    def C(name):
        o, n = _CSTCOLS[name]
        return cst[:, o:o + n]

    def dma(q, out, in_, reads, writes, chan):
        return S.add(q, lambda e: e.dma_start(out=out, in_=in_), reads=reads, writes=writes, dma=True, chan=chan)

    def mm(out, lhsT, rhs, start, stop, reads, writes):
        return S.add("pe", lambda e: e.matmul(out, lhsT=lhsT, rhs=rhs, start=start, stop=stop), reads=reads, writes=writes)

    def tr(out, in_, ident, reads, writes):
        return S.add("pe", lambda e: e.transpose(out, in_, ident), reads=reads, writes=writes)

    def act(out, in_, func, reads, writes, bias=None, scale=None, eng="act"):
        kw = {}
        if bias is not None:
            kw["bias"] = bias
        if scale is not None:
            kw["scale"] = scale
        return S.add(eng, lambda e: e.activation(out=out, in_=in_, func=func, **kw), reads=reads, writes=writes)

    def tt(eng, out, in0, in1, op, reads, writes):
        return S.add(eng, lambda e: e.tensor_tensor(out=out, in0=in0, in1=in1, op=op), reads=reads, writes=writes)

    def stt(eng, out, in0, scalar, in1, op0, op1, reads, writes):
        return S.add(eng, lambda e: e.scalar_tensor_tensor(out=out, in0=in0, scalar=scalar, in1=in1, op0=op0, op1=op1),
                     reads=reads, writes=writes)

    def ts(eng, out, in0, s1, s2, op0, op1, reads, writes):
        if s2 is None:
            return S.add(eng, lambda e: e.tensor_scalar(out=out, in0=in0, scalar1=s1, scalar2=None, op0=op0), reads=reads, writes=writes)
        return S.add(eng, lambda e: e.tensor_scalar(out=out, in0=in0, scalar1=s1, scalar2=s2, op0=op0, op1=op1), reads=reads, writes=writes)

    def cp(eng, out, in_, reads, writes):
        return S.add(eng, lambda e: e.tensor_copy(out=out, in_=in_), reads=reads, writes=writes)

    def memset(eng, ap, val, writes):
        return S.add(eng, lambda e: e.memset(ap, val), writes=writes)

    def pk(b, q0=0, q1=4):
        return [("ps", b)]

    def dump(name, src_ap, shape, reads, dt=F32):
        if name not in debug:
            return
        d = dout("dbg_" + name, shape, dt)
        dbg_outs[name] = d
        dma("sp", d, src_ap, reads=reads, writes=[("dbg", name)], chan=("dbg", name))
        out_keys.append(("dbg", name))

    def debug_dump(name, src_ap, shape, reads, dt=F32):
        d = dout("dbg_" + name, shape, dt)
        dbg_outs[name] = d
        dma("sp", d, src_ap, reads=reads, writes=[("dbg", name)], chan=("dbg", name))
        out_keys.append(("dbg", name))

    def bc_s(tile17, c):
        return tile17[:, c, 1:17].unsqueeze(1).to_broadcast([128, 4, 16])

    def v3(ap):
        return ap.rearrange("p (t b) -> p t b", t=4)

    dma("sp", cc_sb[0:17, :], cc_d, [], ["cc_sb"], "cc")
    dma("sp", cst[:], cst_d, [], ["cst"], "cst")
    dma("sp", bada[:], bada_d, [], ["bada"], "small")
    dma("sp", lng[:], lng_d, [], ["lng"], "small")
    dma("sp", lnb[:], lnb_d, [], ["lnb"], "small")
    dma("sp", cqw[:], cqw_d, [], ["cqw"], "small")
    dma("sp", cmw[:], cmw_d, [], ["cmw"], "small")
    dma("sp", dng[:], dng_d, [], ["dng"], "small")
    dma("sp", alog[:], alog_d.partition_broadcast(128), [], ["alog"], "small")
    dma("sp", dtb[:], dtb_d.partition_broadcast(128), [], ["dtb"], "small")
    memset("dve", onesD[:], 1.0 / D, ["onesD"])
    memset("dve", ones128[:], 1.0 / 128, ["ones128"])
    memset("dve", epst[:, 0:1], LN_EPS / (ALPHA * ALPHA), ["epst"])
    memset("dve", epst[:, 1:2], RMS_EPS, ["epst"])
    memset("dve", epst[:, 2:3], 1.0, ["epst"])
    memset("dve", epst[:, 3:4], 0.0, ["epst"])
    memset("dve", ones1[:], 1.0, ["ones1"])
    memset("dve", lnc[:, 0:1], 128.0 * RMS_EPS, ["lnc"])
    memset("dve", lnc[:, 1:2], RMS_EPS, ["lnc"])
    cp("dve", identb[:], C("ident"), ["cst"], ["identb"])
    memset("dve", Sst[:], 0.0, ["Sst"])
    memset("dve", Sbf[:], 0.0, ["Sbf"])
    memset("dve", halo_q[:], 0.0, ["halo_q"])
    memset("dve", halo_m[:], 0.0, ["halo_m"])

    act(cc_sb[0:17, :], cc_sb[0:17, :], AF.Silu, ["cc_sb"], ["cc_sb"])
    for k in range(KC):
        tr(ps[7][:, k * 17:(k + 1) * 17], cc_sb[0:17, k * 128:(k + 1) * 128], C("ident")[0:17, 0:17],
           ["cc_sb", "cst"], pk(7, 0, 2))
    cp("dve", scT[:], ps[7][:, 0:KC * 17].rearrange("p (a b) -> p a b", a=KC), pk(7, 0, 2), ["scT"])
    ring = [0]

    def ring_next():
        i = ring[0] % 4
        ring[0] += 1
        return wblk[i], ("wblk", i)

    pref = {}

    def prefetch(tag, src):
        wb, kw = ring_next()
        dma("pool", wb[:], src, [], [kw], kw)
        pref[tag] = (wb, kw)

    def get_block(tag, src):
        if tag in pref:
            return pref.pop(tag)
        wb, kw = ring_next()
        dma("pool", wb[:], src, [], [kw], kw)
        return wb, kw

    def modv(i, j):
        return modall[:, (i * 3 + j) * 8:(i * 3 + j + 1) * 8, :]

    def ada_block(blk):
        wb, kw = ring_next()
        dma("pool", wb[:], wada_d[blk], [], [kw], kw)
        for cpos in range(2):
            fc = blk * 2 + cpos
            grp = fc // 8
            bank = 6 + grp % 2
            col = (fc % 8) * 17
            for k in range(KC):
                mm(ps[bank][:, col:col + 17], wb[:, k, cpos * 128:(cpos + 1) * 128], scT[:, k, :], k == 0, k == KC - 1,
                   [kw, "scT"], pk(bank))
            if fc % 8 == 7:
                tt("dve", modall[:, grp * 8:(grp + 1) * 8, :],
                   ps[bank][:, 0:8 * 17].rearrange("p (a b) -> p a b", a=8),
                   bada[:, grp * 8:(grp + 1) * 8].unsqueeze(2).to_broadcast([128, 8, 17]), ALU.add,
                   pk(bank) + ["bada"], [("mod", grp)])

    def ada_derive(i):
        if i == 0:
            ts("dve", sc1p0[:], modv(0, 1), 1.0, None, ALU.add, None, [("mod", 1)], ["sc1p0"])
        else:
            G, B = (G1, B1) if i == 1 else (G2, B2)
            gb = lng[:, (i - 1) * 8:i * 8].unsqueeze(2).to_broadcast([128, KC, 17])
            bb = lnb[:, (i - 1) * 8:i * 8].unsqueeze(2).to_broadcast([128, KC, 17])
            ts("dve", G[:], modv(i, 1), 1.0, None, ALU.add, None, [("mod", i * 3 + 1)], [("GB", i)])
            tt("dve", B[:], G[:], bb, ALU.mult, [("GB", i), "lnb"], [("GB", i)])
            tt("dve", B[:], B[:], modv(i, 0), ALU.add, [("GB", i), ("mod", i * 3)], [("GB", i)])
            tt("dve", G[:], G[:], gb, ALU.mult, [("GB", i), "lng"], [("GB", i)])

    def ada_gate(i):
        f = (0.5 if i != 1 else 1.0) / ALPHA
        ts("dve", gsc[i][:], modv(i, 2), f, None, ALU.mult, None, [("mod", i * 3 + 2)], [("gsc", i)])

    ada_state = {"next": 0}

    def ada_more(n):
        for _ in range(n):
            if ada_state["next"] < 36:
                ada_block(ada_state["next"])
                ada_state["next"] += 1

    ada_more(8)
    ada_derive(0)
    dma("pool", wab[:], wab_d, [], ["wab"], "wab")
    dma("pool", selb[:], sel_d.rearrange("p (a b) -> p a b", a=NB), [], ["selb"], "wab")
    act(nea[:], alog[:], AF.Exp, ["alog"], ["nea"])
    ts("dve", nea[:], nea[:], -1.0, None, ALU.mult, None, ["nea"], ["nea"])
    if stop_after == "ada":
        ada_more(36)
        for i in range(3):
            ada_gate(i)
        ada_derive(1)
        ada_derive(2)
    dump("modall", modall[:], [128, 72, 17], [("mod", g_) for g_ in range(9)])

    def groups(st):
        g = [("A", 0, 512), ("B", 512, 1024)]
        if st == 0:
            g.append(("S", 1024, 1088))
        return g

    def load_x(st):
        for gi_, (g, lo, hi) in enumerate((("A", 0, 512), ("B", 512, 1024))):
            dma("sp", xT[:, :, lo:hi], xp_d[:, :, st * NTP + lo:st * NTP + hi], [],
                [("xT", c, g) for c in range(KC)], ("xload", gi_))
        if st == 0:
            dma("sp", xT[:, :, NTP:NT], xs_d, [], [("xT", c, "S") for c in range(KC)], ("xload", 2))

    def modulate0(st):
        for (g, lo, hi) in groups(st):
            for c in range(KC):
                if g != "S":
                    act(uT[:, c, lo:hi], xT[:, c, lo:hi], AF.Identity, [("xT", c, g), "sc1p0", ("mod", 0)], [("uT", c, g)],
                        bias=modv(0, 0)[:, c, 0:1], scale=sc1p0[:, c, 0:1])
                else:
                    tt("dve", v3(st_xh[:, 0:64]), v3(xT[:, c, lo:hi]), bc_s(sc1p0, c), ALU.mult,
                       [("xT", c, g), "sc1p0"], ["st_xh"])
                    tt("dve", v3(uT[:, c, lo:hi]), v3(st_xh[:, 0:64]), bc_s(modv(0, 0), c), ALU.add,
                       ["st_xh", ("mod", 0)], [("uT", c, g)])

    def ffn(st, w, gi):
        grp = groups(st)
        for fb in range(11):
            bg, kg = get_block(("wg", st, w, fb), wg_d[w][fb])
            bu, ku = get_block(("wu", st, w, fb), wu_d[w][fb])

            for (g, lo, hi) in grp:
                for fp in range(2):
                    f = fb * 2 + fp
                    if g != "S":
                        par = f % 2
                        pg, pu = ps[par * 2], ps[par * 2 + 1]
                        og, ou = pg[:, :], pu[:, :]
                        kpg, kpu = pk(par * 2), pk(par * 2 + 1)
                    else:
                        par = f % 2
                        og = ps[4 + par][:, 0:64]
                        ou = ps[4 + par][:, 64:128]
                        kpg = kpu = pk(4 + par)
                    for k in range(KC):
                        mm(og, bg[:, k, fp * 128:(fp + 1) * 128], uT[:, k, lo:hi], k == 0, k == KC - 1,
                           [kg, ("uT", k, g)], kpg)
                    for k in range(KC):
                        mm(ou, bu[:, k, fp * 128:(fp + 1) * 128], uT[:, k, lo:hi], k == 0, k == KC - 1,
                           [ku, ("uT", k, g)], kpu)
                    w_ = hi - lo
                    sgt = sg_t[par]
                    act(sgt[:, 0:w_], og, AF.Silu, kpg, [("sg", par)])
                    tt("dve", hT[:, f, lo:hi], sgt[:, 0:w_], ou, ALU.mult, [("sg", par)] + kpu, [("hT", f, g)])
            if st == 0 and w == 0:
                ada_more(2 if fb < 8 else 1)
        if st == 0 and w == 0:
            ada_more(1)
            ada_gate(0)
            ada_derive(1)
        if stop_after == "up":
            return
        for m in range(KC):
            wb = wdb[m % 2]
            kw = ("wdb", m % 2)
            dma("pool", wb[:], wd_d[w][m], [], [kw], kw)
            if st == 0 and w == 0:
                ada_more(2)
                if m == KC - 1:
                    ada_more(36)
                    ada_gate(1)
                    ada_gate(2)
                    ada_derive(2)
            for (g, lo, hi) in grp:
                if g != "S":
                    bank = (m % 2) * 2 + (0 if g == "A" else 1)
                    o = ps[bank][:, :]
                    kp = pk(bank)
                else:
                    o = ps[4 + m % 2][:, 0:64]
                    kp = pk(4 + m % 2)
                for f in range(FC):
                    mm(o, wb[:, f, :], hT[:, f, lo:hi], f == 0, f == FC - 1, [kw, ("hT", f, g)], kp)
                if g != "S":
                    stt("dve", xT[:, m, lo:hi], o, gsc[gi][:, m, 0:1], xT[:, m, lo:hi], ALU.mult, ALU.add,
                        kp + [("gsc", gi), ("xT", m, g)], [("xT", m, g)])
                else:
                    tt("dve", v3(st_xh[:, 0:64]), v3(o), bc_s(gsc[gi], m), ALU.mult, kp + [("gsc", gi)], ["st_xh"])
                    tt("dve", xT[:, m, lo:hi], st_xh[:, 0:64], xT[:, m, lo:hi], ALU.add, ["st_xh", ("xT", m, g)], [("xT", m, g)])

    lnset = [dict(t1=st_t1, rstd=st_rstd, nmr=st_nmr, xh=st_xh, k="0"),
             dict(t1=view(R5, 4096, [128, 512], F32), rstd=view(R5, 6144, [128, 512], F32),
                  nmr=view(R5, 8192, [128, 512], F32), xh=view(R5, 10240, [128, 512], F32), k="1")]

    def ln_pre(g, lo, hi, ss_):
        w_ = hi - lo
        t1, rstd, nmr, kk = ss_["t1"], ss_["rstd"], ss_["nmr"], ss_["k"]
        for c in range(KC):
            act(zb[:, c, 0:w_], xT[:, c, lo:hi], AF.Copy, [("xT", c, g)], [("zb", c)])
            tt("pool", zsq[:, c, 0:w_], xT[:, c, lo:hi], xT[:, c, lo:hi], ALU.mult, [("xT", c, g)], [("zsq", c)])
            if c % 2 == 1:
                yield
        for c in range(KC):
            mm(ps[6][:, 0:w_], onesD[:], zb[:, c, 0:w_], c == 0, c == KC - 1, ["onesD", ("zb", c)], pk(6))
        for c in range(KC):
            mm(ps[7][:, 0:w_], onesD[:], zsq[:, c, 0:w_], c == 0, c == KC - 1, ["onesD", ("zsq", c)], pk(7))
        yield
        cp("dve", nmr[:, 0:w_], ps[6][:, 0:w_], pk(6), ["nmr" + kk])
        tt("dve", t1[:, 0:w_], nmr[:, 0:w_], nmr[:, 0:w_], ALU.mult, ["nmr" + kk], ["t1" + kk])
        tt("dve", t1[:, 0:w_], ps[7][:, 0:w_], t1[:, 0:w_], ALU.subtract, pk(7) + ["t1" + kk], ["t1" + kk])
        yield
        act(rstd[:, 0:w_], t1[:, 0:w_], AF.Ln, ["t1" + kk, "epst"], ["rstd" + kk], bias=epst[:, 0:1], scale=1.0)
        act(rstd[:, 0:w_], rstd[:, 0:w_], AF.Exp, ["rstd" + kk], ["rstd" + kk], scale=-0.5)
        yield
        stt("dve", nmr[:, 0:w_], nmr[:, 0:w_], -1.0, rstd[:, 0:w_], ALU.mult, ALU.mult, ["nmr" + kk, "rstd" + kk], ["nmr" + kk])

    def ln_loop(g, lo, hi, ss_, li, G, B, gbkey):
        w_ = hi - lo
        t1, rstd, nmr, kk = ss_["t1"], ss_["rstd"], ss_["nmr"], ss_["k"]
        for c in range(KC):
            xh = ss_["xh"] if c % 2 == 0 else t1
            kxh = ("xh" + kk) if c % 2 == 0 else ("t1" + kk)
            tt("dve", xh[:, 0:w_], xT[:, c, lo:hi], rstd[:, 0:w_], ALU.mult, [("xT", c, g), "rstd" + kk], [kxh])
            tt("dve", xh[:, 0:w_], xh[:, 0:w_], nmr[:, 0:w_], ALU.add, [kxh, "nmr" + kk], [kxh])
            act(xT[:, c, lo:hi], xh[:, 0:w_], AF.Identity, [kxh, "lng", "lnb"], [("xT", c, g)],
                bias=lnb[:, li * 8 + c:li * 8 + c + 1], scale=lng[:, li * 8 + c:li * 8 + c + 1])
            if G is not None:
                if g != "S":
                    act(uT[:, c, lo:hi], xh[:, 0:w_], AF.Identity, [kxh, gbkey], [("uT", c, g)],
                        bias=B[:, c, 0:1], scale=G[:, c, 0:1])
                else:
                    tt("dve", v3(xh[:, 0:64]), v3(xh[:, 0:64]), bc_s(G, c), ALU.mult, [kxh, gbkey], [kxh])
                    tt("dve", v3(uT[:, c, lo:hi]), v3(xh[:, 0:64]), bc_s(B, c), ALU.add, [kxh, gbkey], [("uT", c, g)])
            yield

    def layernorm(st, li, G, B, gbkey):
        grp = groups(st)
        lockstep([ln_pre(*grp[0], lnset[0])])
        for i, gg in enumerate(grp):
            gens = [ln_loop(*gg, lnset[i % 2], li, G, B, gbkey)]
            if i + 1 < len(grp):
                gens.append(ln_pre(*grp[i + 1], lnset[(i + 1) % 2]))
            lockstep(gens)

    def mixer_proj(st):
        grp = groups(st)
        NTx = NT if st == 0 else NTP
        if st == 0:
            dma("sp", pcs[:, :, 0:48], cq_d, [], ["pcs"], "cstate")
            dma("sp", zcs[:, :, 0:32], cm_d, [], ["zcs"], "cstate")
        pbank = {"A": 0, "B": 1}
        deferred = []
        for blk in range(14):
            wb, kw = get_block(("win", st, blk), win_d[blk])
            for cpos in range(2):
                ci = blk * 2 + cpos
                par = ci % 2
                pkeys = {}
                prev_deferred, deferred = deferred, []
                for (g, lo, hi) in grp:
                    if g != "S":
                        bank = par * 2 + pbank[g]
                        o = ps[bank][:, :]
                    else:
                        bank = 4
                        o = ps[4][:, 0:64]
                    pkeys[g] = (o, pk(bank))
                    for k in range(KC):
                        mm(o, wb[:, k, cpos * 128:(cpos + 1) * 128], uT[:, k, lo:hi], k == 0, k == KC - 1,
                           [kw, ("uT", k, g)], pk(bank))
                if ci < 12:
                    j = ci
                    cvj, kcv = (cv, ("cv", 0)) if j % 2 == 0 else (cv2, ("cv", 1))
                    sqj, ksq = (sq, ("sq", 0)) if j % 2 == 0 else (sq2, ("sq", 1))
                    rvj, krv = (rinvp, "scc") if j % 2 == 0 else (rinv2, "rinv2")
                    p_ = pc[j % 2]
                    kpc = ("pc", j % 2)
                    cp("pool", p_[:, 0:3], halo_q[:, j, :], ["halo_q"], [kpc])
                    for (g, lo, hi) in grp:
                        o, kp = pkeys[g]
                        if g != "S":
                            act(p_[:, 3 + lo:3 + hi], o, AF.Copy, kp, [kpc])
                        else:
                            act(pcs[:, j, 48:112], o, AF.Copy, kp, ["pcs"])
                    cp("pool", halo_q[:, j, :], p_[:, NTP:NTP + 3], [kpc], ["halo_q"])
                    for t in range(4):
                        wsc = cqw[:, t, j:j + 1]
                        if t == 0:
                            ts("dve", cvj[:, 0:NTP], p_[:, 0:NTP], wsc, None, ALU.mult, None, [kpc, "cqw"], [kcv])
                        else:
                            stt("dve", cvj[:, 0:NTP], p_[:, t:t + NTP], wsc, cvj[:, 0:NTP], ALU.mult, ALU.add, [kpc, "cqw", kcv], [kcv])
                    if st == 0:
                        for t in range(4):
                            wsc = cqw[:, t, j:j + 1]
                            if t == 0:
                                ts("dve", cvj[:, NTP:NT], pcs[:, j, 0:64], wsc, None, ALU.mult, None, ["pcs", "cqw"], [kcv])
                            else:
                                stt("dve", cvj[:, NTP:NT], pcs[:, j, 16 * t:16 * t + 64], wsc, cvj[:, NTP:NT], ALU.mult, ALU.add,
                                    ["pcs", "cqw", kcv], [kcv])
                    def part2(j=j, cvj=cvj, kcv=kcv, sqj=sqj, ksq=ksq, rvj=rvj, krv=krv):
                        if j >= 8:
                            act(vT[:, j - 8, 0:NTx], cvj[:, 0:NTx], AF.Silu, [kcv], [("vT", j - 8)])
                            return
                        act(cvj[:, 0:NTx], cvj[:, 0:NTx], AF.Silu, [kcv], [kcv])
                        act(sqj[:, 0:NTx], cvj[:, 0:NTx], AF.Square, [kcv], [ksq])
                        isq = j < 4
                        for gi_, (g, lo, hi) in enumerate(grp):
                            bank = 5 + gi_
                            w_ = hi - lo
                            mm(ps[bank][:, 0:w_], ones1[:], sqj[:, lo:hi], True, True, ["ones1", ksq], pk(bank))
                            act(rvj[:, lo:hi], ps[bank][:, 0:w_], AF.Ln, pk(bank) + ["lnc"], [krv],
                                bias=lnc[:, 0:1] if isq else lnc[:, 1:2], scale=128.0 if isq else 1.0)
                        act(rvj[:, 0:NTx], rvj[:, 0:NTx], AF.Exp, [krv], [krv], scale=-0.5)
                        tt("dve", qkT[:, j, 0:NTx], cvj[:, 0:NTx], rvj[:, 0:NTx], ALU.mult, [kcv, krv], [("qkT", j)])
                    deferred.append(part2)
                elif ci < 16:
                    j = ci - 12
                    for (g, lo, hi) in grp:
                        o, kp = pkeys[g]
                        act(mix[:, j, lo:hi], o, AF.Silu, kp, [("mix", j)])
                elif ci < 20:
                    j = ci - 16
                    for (g, lo, hi) in grp:
                        o, kp = pkeys[g]
                        act(mix[:, 4 + j, lo:hi], o, AF.Copy, kp, [("mix", 4 + j)])
                else:
                    j, is_h = (ci - 20) // 2, (ci - 20) % 2
                    if not is_h:
                        for (g, lo, hi) in grp:
                            o, kp = pkeys[g]
                            act(scc[:, lo:hi], o, AF.Copy, kp, ["scc"])
                    else:
                        cp("pool", zc[:, 0:2], halo_m[:, j, :], ["halo_m"], ["zc"])
                        for (g, lo, hi) in grp:
                            o, kp = pkeys[g]
                            if g != "S":
                                tt("dve", zc[:, 2 + lo:2 + hi], scc[:, lo:hi], o, ALU.mult, ["scc"] + kp, ["zc"])
                            else:
                                tt("dve", zcs[:, j, 32:96], scc[:, lo:hi], o, ALU.mult, ["scc"] + kp, ["zcs"])
                        cp("pool", halo_m[:, j, :], zc[:, NTP:NTP + 2], ["zc"], ["halo_m"])
                        for t in range(3):
                            wsc = cmw[:, t, j:j + 1]
                            if t == 0:
                                ts("dve", cv[:, 0:NTP], zc[:, 0:NTP], wsc, None, ALU.mult, None, ["zc", "cmw"], [("cv", 0)])
                            else:
                                stt("dve", cv[:, 0:NTP], zc[:, t:t + NTP], wsc, cv[:, 0:NTP], ALU.mult, ALU.add, ["zc", "cmw", ("cv", 0)], [("cv", 0)])
                        if st == 0:
                            for t in range(3):
                                wsc = cmw[:, t, j:j + 1]
                                if t == 0:
                                    ts("dve", cv[:, NTP:NT], zcs[:, j, 0:64], wsc, None, ALU.mult, None, ["zcs", "cmw"], [("cv", 0)])
                                else:
                                    stt("dve", cv[:, NTP:NT], zcs[:, j, 16 * t:16 * t + 64], wsc, cv[:, NTP:NT], ALU.mult, ALU.add,
                                        ["zcs", "cmw", ("cv", 0)], [("cv", 0)])
                        tt("dve", mix[:, 4 + j, 0:NTx], mix[:, 4 + j, 0:NTx], cv[:, 0:NTx], ALU.mult, [("mix", 4 + j), ("cv", 0)], [("mix", 4 + j)])
                for fn_ in prev_deferred:
                    fn_()
        for fn_ in deferred:
            fn_()
        if st == 0:
            dma("sp", cqs_d, pcs[:, :, 64:112], ["pcs"], ["cq_s"], "cq_s")
            dma("sp", cms_d, zcs[:, :, 64:96], ["zcs"], ["cm_s"], "cm_s")
            out_keys.extend(["cq_s", "cm_s"])
        if st == 1:
            dma("sp", cqp_d, halo_q[:], ["halo_q"], ["cq_p"], "cq_p")
            dma("sp", cmp_d, halo_m[:], ["halo_m"], ["cm_p"], "cm_p")
            out_keys.extend(["cq_p", "cm_p"])

    def gdn_intra(C_, c0, tb, nfull, t):
        si = t.si
        P = slice(0, C_)
        tri, seqt, maskT, nstt = C("tri_" + tb), C("seq_" + tb), C("maskT_" + tb), C("nst_" + tb)
        ident = C("ident")

        def Bk(b):
            return ps[(b + 4 * si) % 8]

        def bk(b):
            return pk((b + 4 * si) % 8)

        def K(n, *a):
            return (n, si) + a

        def h3(x):
            return x.rearrange("p (h c) -> p h c", h=4)[:, :, 0:C_]
        ab_sb, g4, beta4, gcc, egc, kbgs, kdcs = t.ab_sb, t.g4, t.beta4, t.gcc, t.egc, t.kbgs, t.kdcs
        gbc, bbc, egr, dec, dnb, YT, XX = t.gbc, t.bbc, t.egr, t.dec, t.dnb, t.YT, t.XX
        for k in range(KC):
            mm(Bk(0)[P, 0:8], uT[:, k, c0:c0 + C_], wab[:, k, :], k == 0, k == KC - 1, [("uT", k, "A"), ("uT", k, "B"), ("uT", k, "S"), "wab"], bk(0))
        yield
        act(ab_sb[P, :], Bk(0)[P, 0:8], AF.Copy, bk(0), [K("ab_sb")])
        tt("dve", g4[P, :], ab_sb[P, 0:4], dtb[P, :], ALU.add, [K("ab_sb"), "dtb"], [K("g4")])
        act(g4[P, :], g4[P, :], AF.Exp, [K("g4")], [K("g4")])
        act(g4[P, :], g4[P, :], AF.Ln, [K("g4"), "epst"], [K("g4")], bias=epst[P, 2:3], scale=1.0)
        tt("dve", g4[P, :], g4[P, :], nea[P, :], ALU.mult, [K("g4"), "nea"], [K("g4")])
        act(beta4[P, :], ab_sb[P, 4:8], AF.Exp, [K("ab_sb")], [K("beta4")], scale=-1.0)
        ts("dve", beta4[P, :], beta4[P, :], 1.0, None, ALU.add, None, [K("beta4")], [K("beta4")])
        S.add("dve", lambda e: e.reciprocal(out=beta4[P, :], in_=beta4[P, :]), reads=[K("beta4")], writes=[K("beta4")])
        cp("dve", gbc[P, :, :], g4[P, :].unsqueeze(2).to_broadcast([C_, 4, 128]), [K("g4")], [K("gbc")])
        cp("dve", bbc[P, :, :], beta4[P, :].unsqueeze(2).to_broadcast([C_, 4, 128]), [K("beta4")], [K("bbc")])
        yield
        mm(Bk(1)[P, 0:4], tri[P, P], g4[P, :], True, True, ["cst", K("g4")], bk(1))
        mm(Bk(1)[P, 4:8], seqt[P, P], g4[P, :], True, True, ["cst", K("g4")], bk(1))
        for h in range(4):
            mm(Bk(2)[:, h * 128:h * 128 + C_], gbc[P, h, :], tri[P, P], True, True, [K("gbc"), "cst"], bk(2))
            mm(Bk(3)[:, h * 128:h * 128 + C_], bbc[P, h, :], identb[P, P], True, True, [K("bbc"), "identb"], bk(3))
            mm(Bk(1)[:, 64 + h * 16:64 + h * 16 + 16], gbc[P, h, :], seqt[P, 0:16], True, True, [K("gbc"), "cst"], bk(1))
        yield
        cp("dve", gcc[P, :], Bk(1)[P, 0:8], bk(1), [K("gcc")])
        act(t.glast2[:, :], Bk(1)[:, 64:128], AF.Copy, bk(1), [K("glast")])
        act(t.glast2[:, :], t.glast2[:, :], AF.Exp, [K("glast")], [K("glast")])
        act(egr[:, :, 0:C_], h3(Bk(2)[:, :]), AF.Copy, bk(2), [K("egr")])
        act(egr[:, :, 0:C_], egr[:, :, 0:C_], AF.Exp, [K("egr")], [K("egr")])
        act(egc[P, :], gcc[P, 0:4], AF.Exp, [K("gcc")], [K("egc")])
        tt("dve", kbgs[P, :], beta4[P, :], egc[P, :], ALU.mult, [K("beta4"), K("egc")], [K("kbgs")])
        tt("dve", kdcs[P, :], gcc[P, 4:8], gcc[P, 0:4], ALU.subtract, [K("gcc")], [K("kdcs")])
        act(kdcs[P, :], kdcs[P, :], AF.Exp, [K("kdcs")], [K("kdcs")])
        for h in range(4):
            stt("dve", dec[P, h, 0:C_], Bk(2)[P, h * 128:h * 128 + C_], gcc[P, h:h + 1], maskT[P, P], ALU.subtract, ALU.add,
                bk(2) + [K("gcc"), "cst"], [K("dec")])
        act(dec[P, :, 0:C_], dec[P, :, 0:C_], AF.Exp, [K("dec")], [K("dec")])
        yield
        for h in range(4):
            kh = qkT[:, 4 + h, c0:c0 + C_]
            qh = qkT[:, h, c0:c0 + C_]
            mm(Bk(4)[P, h * 128:h * 128 + C_], kh, kh, True, True, [("qkT", 4 + h)], bk(4))
            mm(Bk(5)[P, h * 128:h * 128 + C_], kh, qh, True, True, [("qkT", 4 + h), ("qkT", h)], bk(5))
        psb7 = Bk(2)[:, :].bitcast(BF16)
        for h in range(4):
            tr(psb7[P, h * 128:(h + 1) * 128], qkT[:, 4 + h, c0:c0 + C_], identb[:, :], [("qkT", 4 + h), "identb"], bk(2))
            tr(psb7[P, 512 + h * 128:512 + (h + 1) * 128], vT[:, h, c0:c0 + C_], identb[:, :], [("vT", h), "identb"], bk(2))
        yield
        tt("dve", dnb[P, :, 0:C_], dec[P, :, 0:C_], nstt[P, P].unsqueeze(1).to_broadcast([C_, 4, C_]), ALU.mult, [K("dec"), "cst"], [K("dnb")])
        tt("dve", dnb[P, :, 0:C_], h3(Bk(3)[P, :]), dnb[P, :, 0:C_], ALU.mult, bk(3) + [K("dnb")], [K("dnb")])
        Y0 = YT[0]
        tt("dve", Y0[P, :, 0, 0:C_], h3(Bk(4)[P, :]), dnb[P, :, 0:C_], ALU.mult, bk(4) + [K("dnb")], [K("YT", 0)])
        tt("dve", t.qkm[P, :, 0:C_], h3(Bk(5)[P, :]), dec[P, :, 0:C_], ALU.mult, bk(5) + [K("dec")], [K("qkm")])
        cp("dve", Y0[P, :, 1, 0:C_], identb[P, P].unsqueeze(1).to_broadcast([C_, 4, C_]), ["identb"], [K("YT", 0)])
        k3 = psb7[P, 0:512].rearrange("p (h d) -> p h d", h=4)
        v3_ = psb7[P, 512:1024].rearrange("p (h d) -> p h d", h=4)
        tt("dve", t.kbg[P, :, :], k3, kbgs[P, :].unsqueeze(2).to_broadcast([C_, 4, 128]), ALU.mult, bk(2) + [K("kbgs")], [K("kbg")])
        tt("dve", t.kdec[P, :, :], k3, kdcs[P, :].unsqueeze(2).to_broadcast([C_, 4, 128]), ALU.mult, bk(2) + [K("kdcs")], [K("kdec")])
        tt("dve", t.vb[P, :, :], v3_, beta4[P, :].unsqueeze(2).to_broadcast([C_, 4, 128]), ALU.mult, bk(2) + [K("beta4")], [K("vb")])
        yield
        psb6 = Bk(6)[:, :].bitcast(BF16)
        for h in range(4):
            tr(psb6[P, h * 128:h * 128 + C_], Y0[P, h, 0, 0:C_], identb[P, P], [K("YT", 0), "identb"], bk(6))
        yield
        cp("dve", XX[0][P, :, 0:C_], psb6[P, 0:512].rearrange("p (h c) -> p h c", h=4)[:, :, 0:C_], bk(6), [K("XX", 0)])
        yield
        cur = 0
        for s_ in range(nfull):
            nxt = 1 - cur
            for h in range(4):
                bank = 0 if h < 2 else 1
                off = (h % 2) * 256
                if C_ == 128:
                    mm(Bk(bank)[P, off:off + 256], XX[cur][P, h, 0:C_], YT[cur][P, h, :, :].rearrange("p t c -> p (t c)"), True, True,
                       [K("XX", cur), K("YT", cur)], bk(bank))
                else:
                    mm(Bk(bank)[P, off:off + C_], XX[cur][P, h, 0:C_], YT[cur][P, h, 0, 0:C_], True, True, [K("XX", cur), K("YT", cur)], bk(bank))
                    mm(Bk(bank)[P, off + 128:off + 128 + C_], XX[cur][P, h, 0:C_], YT[cur][P, h, 1, 0:C_], True, True, [K("XX", cur), K("YT", cur)], bk(bank))
                mm(Bk(2)[P, h * 128:h * 128 + C_], YT[cur][P, h, 0, 0:C_], XX[cur][P, h, 0:C_], True, True, [K("XX", cur), K("YT", cur)], bk(2))
            yield
            for bank in range(2):
                pv = Bk(bank)[P, :].rearrange("p (h t c) -> p h t c", h=2, t=2)
                act(YT[nxt][P, 2 * bank:2 * bank + 2, 0, 0:C_], pv[:, :, 0, 0:C_], AF.Copy, bk(bank), [K("YT", nxt)])
                tt("dve", YT[nxt][P, 2 * bank:2 * bank + 2, 1, 0:C_], pv[:, :, 1, 0:C_], YT[cur][P, 2 * bank:2 * bank + 2, 1, 0:C_], ALU.add,
                   bk(bank) + [K("YT", cur)], [K("YT", nxt)])
            act(XX[nxt][P, :, 0:C_], h3(Bk(2)[P, :]), AF.Copy, bk(2), [K("XX", nxt)])
            yield
            cur = nxt
        nxt = 1 - cur
        for h in range(4):
            mm(Bk(0)[P, h * 128:h * 128 + C_], XX[cur][P, h, 0:C_], YT[cur][P, h, 1, 0:C_], True, True, [K("XX", cur), K("YT", cur)], bk(0))
        yield
        tt("dve", YT[nxt][P, :, 1, 0:C_], h3(Bk(0)[P, :]), YT[cur][P, :, 1, 0:C_], ALU.add, bk(0) + [K("YT", cur)], [K("YT", nxt)])
        Tm = YT[nxt]
        kT_ = K("YT", nxt)
        yield
        for h in range(4):
            mm(Bk(3)[:, h * 128:h * 128 + C_], t.kbg[P, h, :], Tm[P, h, 1, 0:C_], True, True, [K("kbg"), kT_], bk(3))
        yield
        act(t.nwT[:, :, 0:C_], h3(Bk(3)[:, :]), AF.Copy, bk(3), [K("nwT")], scale=-1.0)
        q3 = qkT[:, 0:4, c0:c0 + C_]
        tt("dve", t.qdT[:, :, 0:C_], q3, egr[:, :, 0:C_], ALU.mult, [("qkT", h) for h in range(4)] + [K("egr")], [K("qdT")])
        return Tm, kT_

    def lockstep(gens):
        res = [None] * len(gens)
        live = list(range(len(gens)))
        while live:
            for i in list(live):
                try:
                    next(gens[i])
                except StopIteration as e:
                    res[i] = e.value
                    live.remove(i)
        return res

    def gdn_out(C_, c0, t):
        si = t.si

        def Bk(b):
            return ps[(b + 4 * si) % 8]

        def bk(b):
            return pk((b + 4 * si) % 8)

        def K(n, *a):
            return (n, si) + a

        def h3(x):
            return x.rearrange("p (h c) -> p h c", h=4)[:, :, 0:C_]
        act(t.osq[:, :, 0:C_], h3(Bk(5)[:, :]), AF.Square, bk(5), [K("osq")])
        yield
        for h in range(4):
            mm(Bk(7)[:, h * 128:h * 128 + C_], ones128[:], t.osq[:, h, 0:C_], True, True, ["ones128", K("osq")], bk(7))
        yield
        act(t.rinv[:, :, 0:C_], h3(Bk(7)[:, :]), AF.Ln, bk(7) + ["epst"], [K("rinv")], bias=epst[:, 1:2], scale=1.0)
        act(t.rinv[:, :, 0:C_], t.rinv[:, :, 0:C_], AF.Exp, [K("rinv")], [K("rinv")], scale=-0.5)
        yield
        tt("dve", t.rinv[:, :, 0:C_], h3(Bk(5)[:, :]), t.rinv[:, :, 0:C_], ALU.mult, bk(5) + [K("rinv")], [K("rinv")])
        stt("dve", mix[:, 0:4, c0:c0 + C_], t.rinv[:, :, 0:C_], dng[:, 0:1], mix[:, 0:4, c0:c0 + C_], ALU.mult, ALU.mult,
            [K("rinv"), "dng"] + [("mix", h) for h in range(4)], [("mix", h) for h in range(4)])

    def gdn_prompt_inter(c0, t, Tm, kT_):
        C_ = 128
        P = slice(0, C_)
        si = t.si

        def Bk(b):
            return ps[(b + 4 * si) % 8]

        def bk(b):
            return pk((b + 4 * si) % 8)

        def K(n, *a):
            return (n, si) + a
        for h in range(4):
            mm(Bk(4)[P, h * 128:(h + 1) * 128], Tm[P, h, 1, 0:C_], t.vb[P, h, :], True, False, [kT_, K("vb")], bk(4))
            mm(Bk(4)[P, h * 128:(h + 1) * 128], t.nwT[:, h, 0:C_], Sbf[:, h, :], False, True, [K("nwT"), "Sbf"], bk(4))
        act(t.vnew[P, :, :], Bk(4)[P, :].rearrange("p (h d) -> p h d", h=4), AF.Copy, bk(4), [K("vnew")])
        for h in range(4):
            mm(Bk(5)[:, h * 128:h * 128 + C_], Sbf[:, h, :], t.qdT[:, h, 0:C_], True, False, ["Sbf", K("qdT")], bk(5))
            mm(Bk(5)[:, h * 128:h * 128 + C_], t.vnew[P, h, :], t.qkm[P, h, 0:C_], False, True, [K("vnew"), K("qkm")], bk(5))
        for h in range(4):
            mm(Bk(6)[:, h * 128:(h + 1) * 128], t.kdec[P, h, :], t.vnew[P, h, :], True, True, [K("kdec"), K("vnew")], bk(6))
        for h in range(4):
            stt("dve", Sst[:, h, :], Sst[:, h, :], t.glast[:, h, 0:1], Bk(6)[:, h * 128:(h + 1) * 128], ALU.mult, ALU.add,
                ["Sst", K("glast")] + bk(6), ["Sst"])
        act(Sbf[:, :, :], Sst[:, :, :], AF.Copy, ["Sst"], ["Sbf"])

    def gdn_prompt_pair(c0a, c0b):
        if "SEQ" in debug:
            r = lockstep([gdn_intra(128, c0a, "p", 6, TS[0])]) + lockstep([gdn_intra(128, c0b, "p", 6, TS[1])])
        else:
            r = lockstep([gdn_intra(128, c0a, "p", 6, TS[0]), gdn_intra(128, c0b, "p", 6, TS[1])])
        if c0a == 0 and "pairdump" in debug and not dbg_outs:
            for si in range(2):
                t = TS[si]
                for nm in ("kbg", "kdec", "vb", "nwT", "qdT", "qkm"):
                    debug_dump(nm + str(si), getattr(t, nm)[:, :, :], [128, 4, 128], [(nm, si)], BF16)
                debug_dump("T" + str(si), r[si][0][:, :, :, :], [128, 4, 2, 128], [r[si][1]], BF16)
                debug_dump("egr" + str(si), t.egr[:, :, :], [128, 4, 128], [("egr", si)], F32)
                debug_dump("dec" + str(si), t.dec[:, :, :], [128, 4, 128], [("dec", si)], F32)
            raise _Stop()
        gdn_prompt_inter(c0a, TS[0], *r[0])
        gdn_prompt_inter(c0b, TS[1], *r[1])
        lockstep([gdn_out(128, c0a, TS[0]), gdn_out(128, c0b, TS[1])])

    def gdn_sample():
        C_ = NS
        c0 = NTP
        P = slice(0, C_)
        t = TS[0]
        (Tm, kT_), = lockstep([gdn_intra(C_, c0, "s", 1, t)])
        K = lambda n, *a: (n, 0) + a
        selP = C("selP")
        for h in range(4):
            s0f = S0[h % 2]
            ks0 = ("S0", h % 2)
            dma("sp", s0f[:, :, :], s0_d[:, :, h, :], [], [ks0], ks0)
            dma("pool", S0b[:, :, :], s0_d[:, :, h, :], [], ["S0b"], "S0b")
            tt("dve", nwTm[:, :, :], t.nwT[:, h, 0:C_].unsqueeze(1).to_broadcast([128, NB, C_]), selb[:, :, :], ALU.mult, [K("nwT"), "selb"], ["nwTm"])
            tt("dve", qdTm[:, :, :], t.qdT[:, h, 0:C_].unsqueeze(1).to_broadcast([128, NB, C_]), selb[:, :, :], ALU.mult, [K("qdT"), "selb"], ["qdTm"])
            tt("dve", kdm[P, :, :], t.kdec[P, h, :].unsqueeze(1).to_broadcast([C_, NB, 128]),
               selP[P, 0:NB].unsqueeze(2).to_broadcast([C_, NB, 128]), ALU.mult, [K("kdec"), "cst"], ["kdm"])
            mm(ps[4][P, 0:128], Tm[P, h, 1, 0:C_], t.vb[P, h, :], True, False, [kT_, K("vb")], pk(4))
            for b in range(NB):
                mm(ps[4][P, 0:128], nwTm[:, b, :], S0b[:, b, :], False, b == NB - 1, ["nwTm", "S0b"], pk(4))
            act(t.vnew[P, 0, :], ps[4][P, 0:128], AF.Copy, pk(4), [K("vnew")])
            for b in range(NB):
                mm(ps[5][:, h * 128:h * 128 + C_], S0b[:, b, :], qdTm[:, b, :], b == 0, False, ["qdTm", "S0b"], pk(5))
            mm(ps[5][:, h * 128:h * 128 + C_], t.vnew[P, 0, :], t.qkm[P, h, 0:C_], False, True, [K("vnew"), K("qkm")], pk(5))
            for b in range(NB):
                mm(ps[b // 4][:, (b % 4) * 128:(b % 4 + 1) * 128], kdm[P, b, :], t.vnew[P, 0, :], True, True, ["kdm", K("vnew")], pk(b // 4))
            for q in range(4):
                tt("dve", s0f[:, 4 * q:4 * q + 4, :], s0f[:, 4 * q:4 * q + 4, :],
                   t.glast[:, h, 4 * q:4 * q + 4].unsqueeze(2).to_broadcast([128, 4, 128]), ALU.mult, [ks0, K("glast")], [ks0])
                tt("dve", s0f[:, 4 * q:4 * q + 4, :], s0f[:, 4 * q:4 * q + 4, :], ps[q][:, :].rearrange("p (b d) -> p b d", b=4), ALU.add,
                   [ks0] + pk(q), [ks0])
            dma("sp", ssms_d[:, :, h, :], s0f[:, :, :], [ks0], [("ssm_s", h)], ks0)
            out_keys.append(("ssm_s", h))
        lockstep([gdn_out(C_, c0, t)])

    def mixer_out(st):
        for m in range(KC):
            for (g, lo, hi) in groups(st):
                if g != "S":
                    bank = (m % 2) * 2 + (0 if g == "A" else 1)
                    o = ps[bank][:, :]
                else:
                    bank = 4 + m % 2
                    o = ps[bank][:, 0:64]
                kp = pk(bank)
                for k in range(KC):
                    mm(o, wout[:, k, m * 128:(m + 1) * 128], mix[:, k, lo:hi], k == 0, k == KC - 1, ["wout", ("mix", k)], kp)
                if g != "S":
                    stt("dve", xT[:, m, lo:hi], o, gsc[1][:, m, 0:1], xT[:, m, lo:hi], ALU.mult, ALU.add,
                        kp + [("gsc", 1), ("xT", m, g)], [("xT", m, g)])
                else:
                    tt("dve", v3(st_xh[:, 0:64]), v3(o), bc_s(gsc[1], m), ALU.mult, kp + [("gsc", 1)], ["st_xh"])
                    tt("dve", xT[:, m, lo:hi], st_xh[:, 0:64], xT[:, m, lo:hi], ALU.add, ["st_xh", ("xT", m, g)], [("xT", m, g)])

    def store_y(st):
        for gi_, (g, lo, hi) in enumerate((("A", 0, 512), ("B", 512, 1024))):
            dma("sp", yp_d[:, :, st * NTP + lo:st * NTP + hi], xT[:, :, lo:hi], [("xT", c, g) for c in range(KC)],
                [("yp", st, g)], ("yout", st, g))
            out_keys.append(("yp", st, g))
        if st == 0:
            dma("sp", ys_d, xT[:, :, NTP:NT], [("xT", c, "S") for c in range(KC)], ["ys"], "ysout")
            out_keys.append("ys")

    try:
        for st in range(2 if stop_after != "ada" else 0):
            load_x(st)
            if stop_after == "load":
                store_y(st)
                continue
            modulate0(st)
            if stop_after == "mod0":
                S.fence()
                store_y(st)
                continue
            ffn(st, 0, 0)
            if stop_after in ("ffn1", "up"):
                store_y(st)
                continue
            layernorm(st, 0, G1, B1, ("GB", 1))
            for b_ in range(3):
                prefetch(("win", st, b_), win_d[b_])
            S.fence()
            if stop_after == "ln1":
                store_y(st)
                continue
            mixer_proj(st)
            S.fence()
            if stop_after == "proj":
                continue
            dma("pool", wout[:, :, :], wout_d, [], ["wout"], "wout")
            for n in range(4):
                gdn_prompt_pair(2 * n * 128, (2 * n + 1) * 128)
            if st == 0:
                S.fence()
                gdn_sample()
            S.fence()
            if st == 1:
                dma("sp", ssmp_d, Sst[:, :, :], ["Sst"], ["ssm_p"], "ssm_p")
                out_keys.append("ssm_p")
            if stop_after == "gdn":
                continue
            mixer_out(st)
            layernorm(st, 1, G2, B2, ("GB", 2))
            prefetch(("wg", st, 1, 0), wg_d[1][0])
            prefetch(("wu", st, 1, 0), wu_d[1][0])
            prefetch(("wg", st, 1, 1), wg_d[1][1])
            S.fence()
            if stop_after == "ln2":
                store_y(st)
                continue
            ffn(st, 1, 2)
            layernorm(st, 2, None, None, None)
            store_y(st)


    except _Stop:
        pass

    S.add("sp", lambda e: e.nop(), reads=out_keys)
    with nc.Block() as block:
        S.finalize(block)
    return nc, dbg_outs


def _blk(w, nb):
    return np.ascontiguousarray(w.reshape(KC, 128, nb, 256).transpose(2, 1, 0, 3))


def _prep_shared(inp):
    f = lambda a: np.asarray(a, dtype=np.float32)
    sh = {}
    sh["wada"] = _blk(f(inp["w_ada"])[0], 36)
    sh["bada"] = np.ascontiguousarray(f(inp["b_ada"])[0].reshape(72, 128).T)
    sh["lng"] = np.ascontiguousarray(f(inp["ln_g"])[0].reshape(24, 128).T)
    sh["lnb"] = np.ascontiguousarray(f(inp["ln_b"])[0].reshape(24, 128).T)
    for i, nm in ((1, "ffn1"), (2, "ffn2")):
        sh["wg%d" % i] = _blk(f(inp[nm + "_wg"])[0], 11)
        sh["wu%d" % i] = _blk(f(inp[nm + "_wu"])[0], 11)
        wd = f(inp[nm + "_wd"])[0]
        sh["wd%d" % i] = np.ascontiguousarray(wd.reshape(FC, 128, KC, 128).transpose(2, 1, 0, 3))
    win = f(inp["w_in"])[0]
    qkv, ab, og = win[:, 0:1536], win[:, 1536:1544], win[:, 1544:2056]
    scb, scc, sch = win[:, 2056:2568], win[:, 2568:3080], win[:, 3080:3592]
    inter = np.concatenate([np.concatenate([scc[:, j * 128:(j + 1) * 128], sch[:, j * 128:(j + 1) * 128]], axis=1)
                            for j in range(4)], axis=1)
    main = np.concatenate([qkv, og, scb, inter], axis=1)
    sh["win"] = _blk(main, 14)
    sh["wab"] = np.ascontiguousarray(ab.reshape(KC, 128, 8).transpose(1, 0, 2))
    sh["wout"] = np.ascontiguousarray(f(inp["w_out"])[0].reshape(KC, 128, D).transpose(1, 0, 2))
    sh["cqw"] = np.ascontiguousarray(f(inp["conv_qkv_w"])[0].reshape(4, 12, 128).transpose(2, 0, 1))
    sh["cmw"] = np.ascontiguousarray(f(inp["conv_mix_w"])[0].reshape(3, 4, 128).transpose(2, 0, 1))
    sh["dng"] = np.ascontiguousarray(f(inp["dn_norm_g"])[0].reshape(128, 1))
    sh["alog"] = np.ascontiguousarray(f(inp["a_log"])[0])
    sh["dtb"] = np.ascontiguousarray(f(inp["dt_bias"])[0])
    sh["cst"] = _CST
    sh["sel"] = _SEL
    return sh


def _prep_core(inp, c):
    f = lambda a: np.asarray(a, dtype=np.float32)
    m = {}
    xp = f(inp["x_prompt"])[c]
    m["xp"] = np.ascontiguousarray(xp.T.reshape(KC, 128, NP).transpose(1, 0, 2))
    xs = f(inp["x_sample"])[NB * c:NB * (c + 1)]
    xs = xs.transpose(1, 0, 2).reshape(NS, D)
    m["xs"] = np.ascontiguousarray(xs.T.reshape(KC, 128, NS).transpose(1, 0, 2))
    m["cc"] = np.ascontiguousarray(np.concatenate([f(inp["c_prompt"])[c:c + 1], f(inp["c_sample"])[NB * c:NB * (c + 1)]], axis=0))
    m["s0"] = np.ascontiguousarray(f(inp["state_ssm"])[0, NB * c:NB * (c + 1)].transpose(2, 0, 1, 3))
    cq = f(inp["state_conv_qkv"])[0, NB * c:NB * (c + 1)]
    m["cq"] = np.ascontiguousarray(cq.reshape(NB, 3, 12, 128).transpose(3, 2, 1, 0).reshape(128, 12, 48))
    cm = f(inp["state_conv_mix"])[0, NB * c:NB * (c + 1)]
    m["cm"] = np.ascontiguousarray(cm.reshape(NB, 2, 4, 128).transpose(3, 2, 1, 0).reshape(128, 4, 32))
    return m


def _run(inp, debug=(), stop_after=None, trace=False, ncores=8):
    nc, dbg = build_program(debug=debug, stop_after=stop_after)
    sh = _prep_shared(inp)
    in_maps = []
    for c in range(ncores):
        m = dict(sh)
        m.update(_prep_core(inp, c))
        in_maps.append(m)
    res = run_bass_kernel_spmd(nc, in_maps, core_ids=list(range(ncores)), trace=trace)
    return res


def _assemble(res):
    R = list(res.results)
    while len(R) < 8:
        R.append(R[0])
    y_p = np.stack([R[c]["yp"].transpose(1, 0, 2).reshape(D, NP).T for c in range(8)])
    ys = []
    for c in range(8):
        a = R[c]["ys"].transpose(1, 0, 2).reshape(D, NS).T
        ys.append(a.reshape(4, NB, D).transpose(1, 0, 2))
    y_s = np.concatenate(ys, axis=0)
    ssm_p = np.stack([R[c]["ssm_p"].transpose(1, 0, 2) for c in range(8)])[None]
    cq_p = np.stack([R[c]["cq_p"].transpose(2, 1, 0).reshape(3, 1536) for c in range(8)])[None]
    cm_p = np.stack([R[c]["cm_p"].transpose(2, 1, 0).reshape(2, 512) for c in range(8)])[None]
    ssm_s = np.concatenate([R[c]["ssm_s"].transpose(1, 2, 0, 3) for c in range(8)], axis=0)[None]
    cq_s = np.concatenate([R[c]["cq_s"].reshape(128, 12, 3, NB).transpose(3, 2, 1, 0).reshape(NB, 3, 1536)
                           for c in range(8)], axis=0)[None]
    cm_s = np.concatenate([R[c]["cm_s"].reshape(128, 4, 2, NB).transpose(3, 2, 1, 0).reshape(NB, 2, 512)
                           for c in range(8)], axis=0)[None]
    outs = (y_p, y_s, ssm_p, cq_p, cm_p, ssm_s, cq_s, cm_s)
    return tuple(np.ascontiguousarray(o, dtype=np.float32) for o in outs)


def kernel(**inputs):
    res = _run(inputs)
    return _assemble(res)
```

```python
import numpy as np
import concourse.bass as bass
import concourse.mybir as mybir
from concourse.bass_utils import run_bass_kernel_spmd

F32 = mybir.dt.float32
BF16 = mybir.dt.bfloat16
AF = mybir.ActivationFunctionType
ALU = mybir.AluOpType

D = 1024
KC = 8
FF = 2816
FC = 22
NP = 2048
NTP = 1024
NS = 64
NT = NTP + NS
NB = 16
ALPHA = 2.0 ** 0.25
LN_EPS = 1e-5
RMS_EPS = 1e-6
BIGNEG = -1.0e5
ENGS = ("pe", "act", "dve", "pool", "sp")


class _Op:
    __slots__ = ("eng", "fn", "dma", "chan", "eidx", "deps", "signal", "count", "waits")


class Sched:
    def __init__(self, nc):
        self.nc = nc
        self.ops = {e: [] for e in ENGS}
        self.lastw = {}
        self.readers = {}
        self.chan_n = {}
        self.chan_last = {}

    def add(self, eng, fn, reads=(), writes=(), dma=False, chan=None):
        op = _Op()
        op.eng, op.fn, op.dma, op.chan = eng, fn, dma, chan
        op.eidx = len(self.ops[eng])
        op.signal = False
        op.count = None
        op.waits = []
        deps = []
        for r in reads:
            w = self.lastw.get(r)
            if w is not None:
                deps.append(w)
        for w_ in writes:
            w = self.lastw.get(w_)
            if w is not None:
                deps.append(w)
            deps.extend(self.readers.get(w_, ()))
        for r in reads:
            self.readers.setdefault(r, []).append(op)
        for w_ in writes:
            self.lastw[w_] = op
            self.readers[w_] = []
        op.deps = [d for d in deps if d is not op]
        if dma:
            n = self.chan_n.get(chan, 0) + 1
            self.chan_n[chan] = n
            op.count = 16 * n
            op.signal = True
            prev = self.chan_last.get(chan)
            if prev is not None:
                op.deps.append(prev)
            self.chan_last[chan] = op
        self.ops[eng].append(op)
        return op

    def fence(self):
        lasts = [self.ops[e][-1] for e in ENGS if self.ops[e]]
        lastdma = list(self.chan_last.values())
        for e in ENGS:
            op = self.add(e, lambda en: en.nop())
            op.deps = [d for d in lasts if d is not op] + lastdma

    def finalize(self, block):
        nc = self.nc
        for e in ENGS:
            wd = {}
            for op in self.ops[e]:
                need = {}
                for d in op.deps:
                    if d.dma:
                        key, val = ("c", d.chan), d.count
                    else:
                        if d.eng == e and not op.dma and e == "pe":
                            continue
                        key, val = ("e", d.eng), d.eidx
                    if key not in need or need[key][0] < val:
                        need[key] = (val, d)
                for key, (val, d) in need.items():
                    if wd.get(key, -1) >= val:
                        continue
                    wd[key] = val
                    d.signal = True
                    op.waits.append(d)
        esem = {}
        for e in ENGS:
            c = 0
            for op in self.ops[e]:
                if not op.dma and op.signal:
                    c += 1
                    op.count = c
            esem[e] = nc.alloc_semaphore("s_" + e)
        csem = {ch: nc.alloc_semaphore("c_%d" % i) for i, ch in enumerate(self.chan_n)}
        engmap = {"pe": block.tensor, "act": block.scalar, "dve": block.vector,
                  "pool": block.gpsimd, "sp": block.sync}

        def mk(e):
            def body(en):
                for op in self.ops[e]:
                    for d in op.waits:
                        if d.dma:
                            en.wait_ge(csem[d.chan], d.count)
                        else:
                            en.wait_ge(esem[d.eng], d.count)
                    ins = op.fn(en)
                    if op.dma:
                        ins.then_inc(csem[op.chan], 16)
                    elif op.signal:
                        ins.then_inc(esem[e], 1)
            return body

        for e in ENGS:
            if self.ops[e]:
                engmap[e](mk(e))


def _const_tables():
    i = np.arange(128)
    cols = {}
    ident = np.eye(128, dtype=np.float32)
    tri_p = (i[:, None] <= i[None, :]).astype(np.float32)
    seq_p = np.ones((128, 128), np.float32)
    maskT_p = np.where(i[None, :] >= i[:, None], 0.0, BIGNEG).astype(np.float32)
    nst_p = np.where(i[None, :] > i[:, None], -1.0, 0.0).astype(np.float32)
    same = (i[:, None] % 16) == (i[None, :] % 16)
    valid = (i[:, None] < 64) & (i[None, :] < 64)
    tri_s = (same & (i[:, None] <= i[None, :]) & valid).astype(np.float32)
    seq_s = (same & valid).astype(np.float32)
    maskT_s = np.where(same & (i[None, :] >= i[:, None]) & valid, 0.0, BIGNEG).astype(np.float32)
    nst_s = np.where(same & (i[None, :] > i[:, None]) & valid, -1.0, 0.0).astype(np.float32)
    s = np.arange(64)
    sel = (s[None, :] % 16 == np.arange(16)[:, None]).astype(np.float32).reshape(1, 16 * 64)
    sel = np.repeat(sel, 128, axis=0)
    selP = ((i[:, None] % 16) == np.arange(16)[None, :]).astype(np.float32)
    blocks = [("ident", ident), ("tri_p", tri_p), ("seq_p", seq_p), ("maskT_p", maskT_p), ("nst_p", nst_p),
              ("tri_s", tri_s), ("seq_s", seq_s), ("maskT_s", maskT_s), ("nst_s", nst_s),
              ("selP", selP)]
    off = 0
    for name, a in blocks:
        cols[name] = (off, a.shape[1])
        off += a.shape[1]
    tab = np.concatenate([a for _, a in blocks], axis=1).astype(np.float32)
    return tab, cols, sel


_CST, _CSTCOLS, _SEL = _const_tables()
NCST = _CST.shape[1]


class _Stop(Exception):
    pass


def build_program(debug=(), stop_after=None):
    nc = bass.Bass("TRN2", target_bir_lowering=False)
    S = Sched(nc)
    dbg_outs = {}

    def din(name, shape, dt=F32):
        return nc.dram_tensor(name, list(shape), dt, kind="ExternalInput").ap()

    def dout(name, shape, dt=F32):
        return nc.dram_tensor(name, list(shape), dt, kind="ExternalOutput").ap()

    xp_d = din("xp", [128, KC, NP])
    xs_d = din("xs", [128, KC, NS])
    cc_d = din("cc", [17, D])
    s0_d = din("s0", [128, NB, 4, 128])
    cq_d = din("cq", [128, 12, 48])
    cm_d = din("cm", [128, 4, 32])
    wada_d = din("wada", [36, 128, KC, 256])
    bada_d = din("bada", [128, 72])
    lng_d = din("lng", [128, 24])
    lnb_d = din("lnb", [128, 24])
    wg_d = [din("wg1", [11, 128, KC, 256]), din("wg2", [11, 128, KC, 256])]
    wu_d = [din("wu1", [11, 128, KC, 256]), din("wu2", [11, 128, KC, 256])]
    wd_d = [din("wd1", [8, 128, FC, 128]), din("wd2", [8, 128, FC, 128])]
    win_d = din("win", [14, 128, KC, 256])
    wab_d = din("wab", [128, KC, 8])
    wout_d = din("wout", [128, KC, D])
    cqw_d = din("cqw", [128, 4, 12])
    cmw_d = din("cmw", [128, 3, 4])
    dng_d = din("dng", [128, 1])
    alog_d = din("alog", [4])
    dtb_d = din("dtb", [4])
    cst_d = din("cst", [128, NCST])
    sel_d = din("sel", [128, NB * NS])

    yp_d = dout("yp", [128, KC, NP])
    ys_d = dout("ys", [128, KC, NS])
    ssmp_d = dout("ssm_p", [128, 4, 128])
    cqp_d = dout("cq_p", [128, 12, 3])
    cmp_d = dout("cm_p", [128, 4, 2])
    ssms_d = dout("ssm_s", [128, NB, 4, 128])
    cqs_d = dout("cq_s", [128, 12, 48])
    cms_d = dout("cm_s", [128, 4, 32])
    out_keys = []

    def sb(name, shape, dt=F32):
        return nc.alloc_sbuf_tensor("sb_" + name, list(shape), dt)

    xT = sb("xT", [128, KC, NT])
    uT = sb("uT", [128, KC, NT], BF16)
    cst = sb("cst", [128, NCST])
    identb = sb("identb", [128, 128], BF16)
    selb = sb("selb", [128, NB, NS], BF16)
    onesD = sb("onesD", [128, 128], BF16)
    ones128 = sb("ones128", [128, 128], BF16)
    epst = sb("epst", [128, 4])
    bada = sb("bada", [128, 72])
    lng = sb("lng", [128, 24])
    lnb = sb("lnb", [128, 24])
    modall = sb("modall", [128, 72, 17])
    sc1p0 = sb("sc1p0", [128, KC, 17])
    G1 = sb("G1", [128, KC, 17]); B1 = sb("B1", [128, KC, 17])
    G2 = sb("G2", [128, KC, 17]); B2 = sb("B2", [128, KC, 17])
    gsc = [sb("gsc%d" % i, [128, KC, 17]) for i in range(3)]
    cqw = sb("cqw", [128, 4, 12]); cmw = sb("cmw", [128, 3, 4]); dng = sb("dng", [128, 1])
    alog = sb("alog", [128, 4]); dtb = sb("dtb", [128, 4]); nea = sb("nea", [128, 4])
    wab = sb("wab", [128, KC, 8], BF16)
    Sst = sb("Sst", [128, 4, 128]); Sbf = sb("Sbf", [128, 4, 128], BF16)
    halo_q = sb("halo_q", [128, 12, 3]); halo_m = sb("halo_m", [128, 4, 2])
    R1 = sb("R1", [128, 24576 // 4])
    R2 = sb("R2", [128, (FC * NT * 2) // 4])
    wblk = [sb("wblk%d" % i, [128, KC, 256], BF16) for i in range(4)]
    R4 = sb("R4", [128, 16384 // 4])
    R5 = sb("R5", [128, 32768 // 4])
    ps = [nc.alloc_psum_tensor("ps%d" % i, [128, 512], F32) for i in range(8)]

    def view(t, byte_off, shape, dt):
        esz = 2 if dt == BF16 else 4
        n = int(np.prod(shape[1:]))
        a = t[:, byte_off // 4:(byte_off + n * esz + 3) // 4]
        if dt == BF16:
            a = a.bitcast(BF16)[:, 0:n]
        if len(shape) == 3:
            a = a.rearrange("p (a b) -> p a b", a=shape[1])
        elif len(shape) == 4:
            a = a.rearrange("p (a b c) -> p a b c", a=shape[1], b=shape[2])
        return a

    zb = view(R1, 0, [128, KC, 512], BF16)
    zsq = view(R1, 8192, [128, KC, 512], BF16)
    st_t1 = view(R1, 16384, [128, 512], F32)
    st_rstd = view(R1, 18432, [128, 512], F32)
    st_nmr = view(R1, 20480, [128, 512], F32)
    st_xh = view(R1, 22528, [128, 512], F32)
    hT = view(R2, 0, [128, FC, NT], BF16)
    wdb = [view(R4, i * 5632, [128, FC, 128], BF16) for i in range(2)]
    sg_t = [view(R5, i * 1024, [128, 512], BF16) for i in range(2)]
    cc_sb = view(R5, 4096, [17 if False else 128, D], F32)
    scT = view(R5, 4096 + 4096, [128, KC, 17], BF16)

    qkT = view(R2, 0, [128, 8, NT], BF16)
    vT = view(R2, 17408, [128, 4, NT], BF16)
    mix = view(R2, 26112, [128, 8, NT], BF16)
    kdm = view(R2, 43520, [128, NB, 128], BF16)
    wout = view(R4, 0, [128, KC, D], BF16)
    pc = [view(R5, i * 4112, [128, 3 + NTP], F32) for i in range(2)]
    cv = view(R5, 8224, [128, NT], F32)
    sq = view(R5, 12576, [128, NT], BF16)
    scc = view(R5, 14752, [128, NT], F32)
    rinvp = scc
    cv2 = view(R1, 0, [128, NT], F32)
    sq2 = view(R1, 4352, [128, NT], BF16)
    rinv2 = view(R1, 6528, [128, NT], F32)
    zc = view(R5, 19104, [128, 2 + NTP], F32)
    pcs = view(R5, 23216, [128, 12, 112], F32)
    zcs = view(R5, 28592, [128, 4, 96], F32)
    class _NS:
        pass

    def gdn_set(si):
        def vw(off, shape, dt):
            if si == 0:
                return view(R5, off, shape, dt)
            if off < 24064:
                return view(R1, off, shape, dt)
            return view(R5, 29184 + off - 24064, shape, dt)
        t = _NS()
        t.si = si
        t.ab_sb = vw(0, [128, 8], F32)
        t.g4 = vw(32, [128, 4], F32)
        t.beta4 = vw(48, [128, 4], F32)
        t.gcc = vw(64, [128, 8], F32)
        t.egc = vw(96, [128, 4], F32)
        t.kbgs = vw(112, [128, 4], F32)
        t.kdcs = vw(128, [128, 4], F32)
        t.glast = vw(256, [128, 4, 16], F32)
        t.glast2 = vw(256, [128, 64], F32)
        t.gbc = vw(512, [128, 4, 128], F32)
        t.bbc = vw(2560, [128, 4, 128], BF16)
        t.egr = vw(4608, [128, 4, 128], F32)
        t.dec = vw(6656, [128, 4, 128], F32)
        t.dnb = vw(8704, [128, 4, 128], F32)
        t.YT = [vw(10752 + i * 2048, [128, 4, 2, 128], BF16) for i in range(2)]
        t.XX = [vw(14848 + i * 1024, [128, 4, 128], BF16) for i in range(2)]
        t.qkm = vw(16896, [128, 4, 128], BF16)
        t.kbg = vw(17920, [128, 4, 128], BF16)
        t.kdec = vw(18944, [128, 4, 128], BF16)
        t.vb = vw(19968, [128, 4, 128], BF16)
        t.nwT = vw(20992, [128, 4, 128], BF16)
        t.qdT = vw(22016, [128, 4, 128], BF16)
        t.vnew = vw(23040, [128, 4, 128], BF16)
        t.osq = vw(24064, [128, 4, 128], BF16)
        t.rinv = vw(25088, [128, 4, 128], F32)
        return t

    TS = [gdn_set(0), gdn_set(1)]
    S0 = [view(R1, 0, [128, NB, 128], F32), view(R1, 16384, [128, NB, 128], F32)]
    S0b = view(R1, 8192, [128, NB, 128], BF16)
    nwTm = view(R1, 12288, [128, NB, NS], BF16)
    qdTm = view(R1, 14336, [128, NB, NS], BF16)
    ones1 = sb("ones1", [128, 128], BF16)
    lnc = sb("lnc", [128, 2])

    def C(name):
        o, n = _CSTCOLS[name]
        return cst[:, o:o + n]

    def dma(q, out, in_, reads, writes, chan):
        return S.add(q, lambda e: e.dma_start(out=out, in_=in_), reads=reads, writes=writes, dma=True, chan=chan)

    def mm(out, lhsT, rhs, start, stop, reads, writes):
        return S.add("pe", lambda e: e.matmul(out, lhsT=lhsT, rhs=rhs, start=start, stop=stop), reads=reads, writes=writes)

    def tr(out, in_, ident, reads, writes):
        return S.add("pe", lambda e: e.transpose(out, in_, ident), reads=reads, writes=writes)

    def act(out, in_, func, reads, writes, bias=None, scale=None, eng="act"):
        kw = {}
        if bias is not None:
            kw["bias"] = bias
        if scale is not None:
            kw["scale"] = scale
        return S.add(eng, lambda e: e.activation(out=out, in_=in_, func=func, **kw), reads=reads, writes=writes)

    def tt(eng, out, in0, in1, op, reads, writes):
        return S.add(eng, lambda e: e.tensor_tensor(out=out, in0=in0, in1=in1, op=op), reads=reads, writes=writes)

    def stt(eng, out, in0, scalar, in1, op0, op1, reads, writes):
        return S.add(eng, lambda e: e.scalar_tensor_tensor(out=out, in0=in0, scalar=scalar, in1=in1, op0=op0, op1=op1),
                     reads=reads, writes=writes)

    def ts(eng, out, in0, s1, s2, op0, op1, reads, writes):
        if s2 is None:
            return S.add(eng, lambda e: e.tensor_scalar(out=out, in0=in0, scalar1=s1, scalar2=None, op0=op0), reads=reads, writes=writes)
        return S.add(eng, lambda e: e.tensor_scalar(out=out, in0=in0, scalar1=s1, scalar2=s2, op0=op0, op1=op1), reads=reads, writes=writes)

    def cp(eng, out, in_, reads, writes):
        return S.add(eng, lambda e: e.tensor_copy(out=out, in_=in_), reads=reads, writes=writes)

    def memset(eng, ap, val, writes):
        return S.add(eng, lambda e: e.memset(ap, val), writes=writes)

    def pk(b, q0=0, q1=4):
        return [("ps", b)]

    def dump(name, src_ap, shape, reads, dt=F32):
        if name not in debug:
            return
        d = dout("dbg_" + name, shape, dt)
        dbg_outs[name] = d
        dma("sp", d, src_ap, reads=reads, writes=[("dbg", name)], chan=("dbg", name))
        out_keys.append(("dbg", name))

    def debug_dump(name, src_ap, shape, reads, dt=F32):
        d = dout("dbg_" + name, shape, dt)
        dbg_outs[name] = d
        dma("sp", d, src_ap, reads=reads, writes=[("dbg", name)], chan=("dbg", name))
        out_keys.append(("dbg", name))

    def bc_s(tile17, c):
        return tile17[:, c, 1:17].unsqueeze(1).to_broadcast([128, 4, 16])

    def v3(ap):
        return ap.rearrange("p (t b) -> p t b", t=4)

    dma("sp", cc_sb[0:17, :], cc_d, [], ["cc_sb"], "cc")
    dma("sp", cst[:], cst_d, [], ["cst"], "cst")
    dma("sp", bada[:], bada_d, [], ["bada"], "small")
    dma("sp", lng[:], lng_d, [], ["lng"], "small")
    dma("sp", lnb[:], lnb_d, [], ["lnb"], "small")
    dma("sp", cqw[:], cqw_d, [], ["cqw"], "small")
    dma("sp", cmw[:], cmw_d, [], ["cmw"], "small")
    dma("sp", dng[:], dng_d, [], ["dng"], "small")
    dma("sp", alog[:], alog_d.partition_broadcast(128), [], ["alog"], "small")
    dma("sp", dtb[:], dtb_d.partition_broadcast(128), [], ["dtb"], "small")
    memset("dve", onesD[:], 1.0 / D, ["onesD"])
    memset("dve", ones128[:], 1.0 / 128, ["ones128"])
    memset("dve", epst[:, 0:1], LN_EPS / (ALPHA * ALPHA), ["epst"])
    memset("dve", epst[:, 1:2], RMS_EPS, ["epst"])
    memset("dve", epst[:, 2:3], 1.0, ["epst"])
    memset("dve", epst[:, 3:4], 0.0, ["epst"])
    memset("dve", ones1[:], 1.0, ["ones1"])
    memset("dve", lnc[:, 0:1], 128.0 * RMS_EPS, ["lnc"])
    memset("dve", lnc[:, 1:2], RMS_EPS, ["lnc"])
    cp("dve", identb[:], C("ident"), ["cst"], ["identb"])
    memset("dve", Sst[:], 0.0, ["Sst"])
    memset("dve", Sbf[:], 0.0, ["Sbf"])
    memset("dve", halo_q[:], 0.0, ["halo_q"])
    memset("dve", halo_m[:], 0.0, ["halo_m"])

    act(cc_sb[0:17, :], cc_sb[0:17, :], AF.Silu, ["cc_sb"], ["cc_sb"])
    for k in range(KC):
        tr(ps[7][:, k * 17:(k + 1) * 17], cc_sb[0:17, k * 128:(k + 1) * 128], C("ident")[0:17, 0:17],
           ["cc_sb", "cst"], pk(7, 0, 2))
    cp("dve", scT[:], ps[7][:, 0:KC * 17].rearrange("p (a b) -> p a b", a=KC), pk(7, 0, 2), ["scT"])
    ring = [0]

    def ring_next():
        i = ring[0] % 4
        ring[0] += 1
        return wblk[i], ("wblk", i)

    pref = {}

    def prefetch(tag, src):
        wb, kw = ring_next()
        dma("pool", wb[:], src, [], [kw], kw)
        pref[tag] = (wb, kw)

    def get_block(tag, src):
        if tag in pref:
            return pref.pop(tag)
        wb, kw = ring_next()
        dma("pool", wb[:], src, [], [kw], kw)
        return wb, kw

    def modv(i, j):
        return modall[:, (i * 3 + j) * 8:(i * 3 + j + 1) * 8, :]

    def ada_block(blk):
        wb, kw = ring_next()
        dma("pool", wb[:], wada_d[blk], [], [kw], kw)
        for cpos in range(2):
            fc = blk * 2 + cpos
            grp = fc // 8
            bank = 6 + grp % 2
            col = (fc % 8) * 17
            for k in range(KC):
                mm(ps[bank][:, col:col + 17], wb[:, k, cpos * 128:(cpos + 1) * 128], scT[:, k, :], k == 0, k == KC - 1,
                   [kw, "scT"], pk(bank))
            if fc % 8 == 7:
                tt("dve", modall[:, grp * 8:(grp + 1) * 8, :],
                   ps[bank][:, 0:8 * 17].rearrange("p (a b) -> p a b", a=8),
                   bada[:, grp * 8:(grp + 1) * 8].unsqueeze(2).to_broadcast([128, 8, 17]), ALU.add,
                   pk(bank) + ["bada"], [("mod", grp)])

    def ada_derive(i):
        if i == 0:
            ts("dve", sc1p0[:], modv(0, 1), 1.0, None, ALU.add, None, [("mod", 1)], ["sc1p0"])
        else:
            G, B = (G1, B1) if i == 1 else (G2, B2)
            gb = lng[:, (i - 1) * 8:i * 8].unsqueeze(2).to_broadcast([128, KC, 17])
            bb = lnb[:, (i - 1) * 8:i * 8].unsqueeze(2).to_broadcast([128, KC, 17])
            ts("dve", G[:], modv(i, 1), 1.0, None, ALU.add, None, [("mod", i * 3 + 1)], [("GB", i)])
            tt("dve", B[:], G[:], bb, ALU.mult, [("GB", i), "lnb"], [("GB", i)])
            tt("dve", B[:], B[:], modv(i, 0), ALU.add, [("GB", i), ("mod", i * 3)], [("GB", i)])
            tt("dve", G[:], G[:], gb, ALU.mult, [("GB", i), "lng"], [("GB", i)])

    def ada_gate(i):
        f = (0.5 if i != 1 else 1.0) / ALPHA
        ts("dve", gsc[i][:], modv(i, 2), f, None, ALU.mult, None, [("mod", i * 3 + 2)], [("gsc", i)])

    ada_state = {"next": 0}

    def ada_more(n):
        for _ in range(n):
            if ada_state["next"] < 36:
                ada_block(ada_state["next"])
                ada_state["next"] += 1

    ada_more(8)
    ada_derive(0)
    dma("pool", wab[:], wab_d, [], ["wab"], "wab")
    dma("pool", selb[:], sel_d.rearrange("p (a b) -> p a b", a=NB), [], ["selb"], "wab")
    act(nea[:], alog[:], AF.Exp, ["alog"], ["nea"])
    ts("dve", nea[:], nea[:], -1.0, None, ALU.mult, None, ["nea"], ["nea"])
    if stop_after == "ada":
        ada_more(36)
        for i in range(3):
            ada_gate(i)
        ada_derive(1)
        ada_derive(2)
    dump("modall", modall[:], [128, 72, 17], [("mod", g_) for g_ in range(9)])

    def groups(st):
        g = [("A", 0, 512), ("B", 512, 1024)]
        if st == 0:
            g.append(("S", 1024, 1088))
        return g

    def load_x(st):
        for gi_, (g, lo, hi) in enumerate((("A", 0, 512), ("B", 512, 1024))):
            dma("sp", xT[:, :, lo:hi], xp_d[:, :, st * NTP + lo:st * NTP + hi], [],
                [("xT", c, g) for c in range(KC)], ("xload", gi_))
        if st == 0:
            dma("sp", xT[:, :, NTP:NT], xs_d, [], [("xT", c, "S") for c in range(KC)], ("xload", 2))

    def modulate0(st):
        for (g, lo, hi) in groups(st):
            for c in range(KC):
                if g != "S":
                    act(uT[:, c, lo:hi], xT[:, c, lo:hi], AF.Identity, [("xT", c, g), "sc1p0", ("mod", 0)], [("uT", c, g)],
                        bias=modv(0, 0)[:, c, 0:1], scale=sc1p0[:, c, 0:1])
                else:
                    tt("dve", v3(st_xh[:, 0:64]), v3(xT[:, c, lo:hi]), bc_s(sc1p0, c), ALU.mult,
                       [("xT", c, g), "sc1p0"], ["st_xh"])
                    tt("dve", v3(uT[:, c, lo:hi]), v3(st_xh[:, 0:64]), bc_s(modv(0, 0), c), ALU.add,
                       ["st_xh", ("mod", 0)], [("uT", c, g)])

    def ffn(st, w, gi):
        grp = groups(st)
        for fb in range(11):
            bg, kg = get_block(("wg", st, w, fb), wg_d[w][fb])
            bu, ku = get_block(("wu", st, w, fb), wu_d[w][fb])

            for (g, lo, hi) in grp:
                for fp in range(2):
                    f = fb * 2 + fp
                    if g != "S":
                        par = f % 2
                        pg, pu = ps[par * 2], ps[par * 2 + 1]
                        og, ou = pg[:, :], pu[:, :]
                        kpg, kpu = pk(par * 2), pk(par * 2 + 1)
                    else:
                        par = f % 2
                        og = ps[4 + par][:, 0:64]
                        ou = ps[4 + par][:, 64:128]
                        kpg = kpu = pk(4 + par)
                    for k in range(KC):
                        mm(og, bg[:, k, fp * 128:(fp + 1) * 128], uT[:, k, lo:hi], k == 0, k == KC - 1,
                           [kg, ("uT", k, g)], kpg)
                    for k in range(KC):
                        mm(ou, bu[:, k, fp * 128:(fp + 1) * 128], uT[:, k, lo:hi], k == 0, k == KC - 1,
                           [ku, ("uT", k, g)], kpu)
                    w_ = hi - lo
                    sgt = sg_t[par]
                    act(sgt[:, 0:w_], og, AF.Silu, kpg, [("sg", par)])
                    tt("dve", hT[:, f, lo:hi], sgt[:, 0:w_], ou, ALU.mult, [("sg", par)] + kpu, [("hT", f, g)])
            if st == 0 and w == 0:
                ada_more(2 if fb < 8 else 1)
        if st == 0 and w == 0:
            ada_more(1)
            ada_gate(0)
            ada_derive(1)
        if stop_after == "up":
            return
        for m in range(KC):
            wb = wdb[m % 2]
            kw = ("wdb", m % 2)
            dma("pool", wb[:], wd_d[w][m], [], [kw], kw)
            if st == 0 and w == 0:
                ada_more(2)
                if m == KC - 1:
                    ada_more(36)
                    ada_gate(1)
                    ada_gate(2)
                    ada_derive(2)
            for (g, lo, hi) in grp:
                if g != "S":
                    bank = (m % 2) * 2 + (0 if g == "A" else 1)
                    o = ps[bank][:, :]
                    kp = pk(bank)
                else:
                    o = ps[4 + m % 2][:, 0:64]
                    kp = pk(4 + m % 2)
                for f in range(FC):
                    mm(o, wb[:, f, :], hT[:, f, lo:hi], f == 0, f == FC - 1, [kw, ("hT", f, g)], kp)
                if g != "S":
                    stt("dve", xT[:, m, lo:hi], o, gsc[gi][:, m, 0:1], xT[:, m, lo:hi], ALU.mult, ALU.add,
                        kp + [("gsc", gi), ("xT", m, g)], [("xT", m, g)])
                else:
                    tt("dve", v3(st_xh[:, 0:64]), v3(o), bc_s(gsc[gi], m), ALU.mult, kp + [("gsc", gi)], ["st_xh"])
                    tt("dve", xT[:, m, lo:hi], st_xh[:, 0:64], xT[:, m, lo:hi], ALU.add, ["st_xh", ("xT", m, g)], [("xT", m, g)])

    lnset = [dict(t1=st_t1, rstd=st_rstd, nmr=st_nmr, xh=st_xh, k="0"),
             dict(t1=view(R5, 4096, [128, 512], F32), rstd=view(R5, 6144, [128, 512], F32),
                  nmr=view(R5, 8192, [128, 512], F32), xh=view(R5, 10240, [128, 512], F32), k="1")]

    def ln_pre(g, lo, hi, ss_):
        w_ = hi - lo
        t1, rstd, nmr, kk = ss_["t1"], ss_["rstd"], ss_["nmr"], ss_["k"]
        for c in range(KC):
            act(zb[:, c, 0:w_], xT[:, c, lo:hi], AF.Copy, [("xT", c, g)], [("zb", c)])
            tt("pool", zsq[:, c, 0:w_], xT[:, c, lo:hi], xT[:, c, lo:hi], ALU.mult, [("xT", c, g)], [("zsq", c)])
            if c % 2 == 1:
                yield
        for c in range(KC):
            mm(ps[6][:, 0:w_], onesD[:], zb[:, c, 0:w_], c == 0, c == KC - 1, ["onesD", ("zb", c)], pk(6))
        for c in range(KC):
            mm(ps[7][:, 0:w_], onesD[:], zsq[:, c, 0:w_], c == 0, c == KC - 1, ["onesD", ("zsq", c)], pk(7))
        yield
        cp("dve", nmr[:, 0:w_], ps[6][:, 0:w_], pk(6), ["nmr" + kk])
        tt("dve", t1[:, 0:w_], nmr[:, 0:w_], nmr[:, 0:w_], ALU.mult, ["nmr" + kk], ["t1" + kk])
        tt("dve", t1[:, 0:w_], ps[7][:, 0:w_], t1[:, 0:w_], ALU.subtract, pk(7) + ["t1" + kk], ["t1" + kk])
        yield
        act(rstd[:, 0:w_], t1[:, 0:w_], AF.Ln, ["t1" + kk, "epst"], ["rstd" + kk], bias=epst[:, 0:1], scale=1.0)
        act(rstd[:, 0:w_], rstd[:, 0:w_], AF.Exp, ["rstd" + kk], ["rstd" + kk], scale=-0.5)
        yield
        stt("dve", nmr[:, 0:w_], nmr[:, 0:w_], -1.0, rstd[:, 0:w_], ALU.mult, ALU.mult, ["nmr" + kk, "rstd" + kk], ["nmr" + kk])

    def ln_loop(g, lo, hi, ss_, li, G, B, gbkey):
        w_ = hi - lo
        t1, rstd, nmr, kk = ss_["t1"], ss_["rstd"], ss_["nmr"], ss_["k"]
        for c in range(KC):
            xh = ss_["xh"] if c % 2 == 0 else t1
            kxh = ("xh" + kk) if c % 2 == 0 else ("t1" + kk)
            tt("dve", xh[:, 0:w_], xT[:, c, lo:hi], rstd[:, 0:w_], ALU.mult, [("xT", c, g), "rstd" + kk], [kxh])
            tt("dve", xh[:, 0:w_], xh[:, 0:w_], nmr[:, 0:w_], ALU.add, [kxh, "nmr" + kk], [kxh])
            act(xT[:, c, lo:hi], xh[:, 0:w_], AF.Identity, [kxh, "lng", "lnb"], [("xT", c, g)],
                bias=lnb[:, li * 8 + c:li * 8 + c + 1], scale=lng[:, li * 8 + c:li * 8 + c + 1])
            if G is not None:
                if g != "S":
                    act(uT[:, c, lo:hi], xh[:, 0:w_], AF.Identity, [kxh, gbkey], [("uT", c, g)],
                        bias=B[:, c, 0:1], scale=G[:, c, 0:1])
                else:
                    tt("dve", v3(xh[:, 0:64]), v3(xh[:, 0:64]), bc_s(G, c), ALU.mult, [kxh, gbkey], [kxh])
                    tt("dve", v3(uT[:, c, lo:hi]), v3(xh[:, 0:64]), bc_s(B, c), ALU.add, [kxh, gbkey], [("uT", c, g)])
            yield

    def layernorm(st, li, G, B, gbkey):
        grp = groups(st)
        lockstep([ln_pre(*grp[0], lnset[0])])
        for i, gg in enumerate(grp):
            gens = [ln_loop(*gg, lnset[i % 2], li, G, B, gbkey)]
            if i + 1 < len(grp):
                gens.append(ln_pre(*grp[i + 1], lnset[(i + 1) % 2]))
            lockstep(gens)

    def mixer_proj(st):
        grp = groups(st)
        NTx = NT if st == 0 else NTP
        if st == 0:
            dma("sp", pcs[:, :, 0:48], cq_d, [], ["pcs"], "cstate")
            dma("sp", zcs[:, :, 0:32], cm_d, [], ["zcs"], "cstate")
        pbank = {"A": 0, "B": 1}
        deferred = []
        for blk in range(14):
            wb, kw = get_block(("win", st, blk), win_d[blk])
            for cpos in range(2):
                ci = blk * 2 + cpos
                par = ci % 2
                pkeys = {}
                prev_deferred, deferred = deferred, []
                for (g, lo, hi) in grp:
                    if g != "S":
                        bank = par * 2 + pbank[g]
                        o = ps[bank][:, :]
                    else:
                        bank = 4
                        o = ps[4][:, 0:64]
                    pkeys[g] = (o, pk(bank))
                    for k in range(KC):
                        mm(o, wb[:, k, cpos * 128:(cpos + 1) * 128], uT[:, k, lo:hi], k == 0, k == KC - 1,
                           [kw, ("uT", k, g)], pk(bank))
                if ci < 12:
                    j = ci
                    cvj, kcv = (cv, ("cv", 0)) if j % 2 == 0 else (cv2, ("cv", 1))
                    sqj, ksq = (sq, ("sq", 0)) if j % 2 == 0 else (sq2, ("sq", 1))
                    rvj, krv = (rinvp, "scc") if j % 2 == 0 else (rinv2, "rinv2")
                    p_ = pc[j % 2]
                    kpc = ("pc", j % 2)
                    cp("pool", p_[:, 0:3], halo_q[:, j, :], ["halo_q"], [kpc])
                    for (g, lo, hi) in grp:
                        o, kp = pkeys[g]
                        if g != "S":
                            act(p_[:, 3 + lo:3 + hi], o, AF.Copy, kp, [kpc])
                        else:
                            act(pcs[:, j, 48:112], o, AF.Copy, kp, ["pcs"])
                    cp("pool", halo_q[:, j, :], p_[:, NTP:NTP + 3], [kpc], ["halo_q"])
                    for t in range(4):
                        wsc = cqw[:, t, j:j + 1]
                        if t == 0:
                            ts("dve", cvj[:, 0:NTP], p_[:, 0:NTP], wsc, None, ALU.mult, None, [kpc, "cqw"], [kcv])
                        else:
                            stt("dve", cvj[:, 0:NTP], p_[:, t:t + NTP], wsc, cvj[:, 0:NTP], ALU.mult, ALU.add, [kpc, "cqw", kcv], [kcv])
                    if st == 0:
                        for t in range(4):
                            wsc = cqw[:, t, j:j + 1]
                            if t == 0:
                                ts("dve", cvj[:, NTP:NT], pcs[:, j, 0:64], wsc, None, ALU.mult, None, ["pcs", "cqw"], [kcv])
                            else:
                                stt("dve", cvj[:, NTP:NT], pcs[:, j, 16 * t:16 * t + 64], wsc, cvj[:, NTP:NT], ALU.mult, ALU.add,
                                    ["pcs", "cqw", kcv], [kcv])
                    def part2(j=j, cvj=cvj, kcv=kcv, sqj=sqj, ksq=ksq, rvj=rvj, krv=krv):
                        if j >= 8:
                            act(vT[:, j - 8, 0:NTx], cvj[:, 0:NTx], AF.Silu, [kcv], [("vT", j - 8)])
                            return
                        act(cvj[:, 0:NTx], cvj[:, 0:NTx], AF.Silu, [kcv], [kcv])
                        act(sqj[:, 0:NTx], cvj[:, 0:NTx], AF.Square, [kcv], [ksq])
                        isq = j < 4
                        for gi_, (g, lo, hi) in enumerate(grp):
                            bank = 5 + gi_
                            w_ = hi - lo
                            mm(ps[bank][:, 0:w_], ones1[:], sqj[:, lo:hi], True, True, ["ones1", ksq], pk(bank))
                            act(rvj[:, lo:hi], ps[bank][:, 0:w_], AF.Ln, pk(bank) + ["lnc"], [krv],
                                bias=lnc[:, 0:1] if isq else lnc[:, 1:2], scale=128.0 if isq else 1.0)
                        act(rvj[:, 0:NTx], rvj[:, 0:NTx], AF.Exp, [krv], [krv], scale=-0.5)
                        tt("dve", qkT[:, j, 0:NTx], cvj[:, 0:NTx], rvj[:, 0:NTx], ALU.mult, [kcv, krv], [("qkT", j)])
                    deferred.append(part2)
                elif ci < 16:
                    j = ci - 12
                    for (g, lo, hi) in grp:
                        o, kp = pkeys[g]
                        act(mix[:, j, lo:hi], o, AF.Silu, kp, [("mix", j)])
                elif ci < 20:
                    j = ci - 16
                    for (g, lo, hi) in grp:
                        o, kp = pkeys[g]
                        act(mix[:, 4 + j, lo:hi], o, AF.Copy, kp, [("mix", 4 + j)])
                else:
                    j, is_h = (ci - 20) // 2, (ci - 20) % 2
                    if not is_h:
                        for (g, lo, hi) in grp:
                            o, kp = pkeys[g]
                            act(scc[:, lo:hi], o, AF.Copy, kp, ["scc"])
                    else:
                        cp("pool", zc[:, 0:2], halo_m[:, j, :], ["halo_m"], ["zc"])
                        for (g, lo, hi) in grp:
                            o, kp = pkeys[g]
                            if g != "S":
                                tt("dve", zc[:, 2 + lo:2 + hi], scc[:, lo:hi], o, ALU.mult, ["scc"] + kp, ["zc"])
                            else:
                                tt("dve", zcs[:, j, 32:96], scc[:, lo:hi], o, ALU.mult, ["scc"] + kp, ["zcs"])
                        cp("pool", halo_m[:, j, :], zc[:, NTP:NTP + 2], ["zc"], ["halo_m"])
                        for t in range(3):
                            wsc = cmw[:, t, j:j + 1]
                            if t == 0:
                                ts("dve", cv[:, 0:NTP], zc[:, 0:NTP], wsc, None, ALU.mult, None, ["zc", "cmw"], [("cv", 0)])
                            else:
                                stt("dve", cv[:, 0:NTP], zc[:, t:t + NTP], wsc, cv[:, 0:NTP], ALU.mult, ALU.add, ["zc", "cmw", ("cv", 0)], [("cv", 0)])
                        if st == 0:
                            for t in range(3):
                                wsc = cmw[:, t, j:j + 1]
                                if t == 0:
                                    ts("dve", cv[:, NTP:NT], zcs[:, j, 0:64], wsc, None, ALU.mult, None, ["zcs", "cmw"], [("cv", 0)])
                                else:
                                    stt("dve", cv[:, NTP:NT], zcs[:, j, 16 * t:16 * t + 64], wsc, cv[:, NTP:NT], ALU.mult, ALU.add,
                                        ["zcs", "cmw", ("cv", 0)], [("cv", 0)])
                        tt("dve", mix[:, 4 + j, 0:NTx], mix[:, 4 + j, 0:NTx], cv[:, 0:NTx], ALU.mult, [("mix", 4 + j), ("cv", 0)], [("mix", 4 + j)])
                for fn_ in prev_deferred:
                    fn_()
        for fn_ in deferred:
            fn_()
        if st == 0:
            dma("sp", cqs_d, pcs[:, :, 64:112], ["pcs"], ["cq_s"], "cq_s")
            dma("sp", cms_d, zcs[:, :, 64:96], ["zcs"], ["cm_s"], "cm_s")
            out_keys.extend(["cq_s", "cm_s"])
        if st == 1:
            dma("sp", cqp_d, halo_q[:], ["halo_q"], ["cq_p"], "cq_p")
            dma("sp", cmp_d, halo_m[:], ["halo_m"], ["cm_p"], "cm_p")
            out_keys.extend(["cq_p", "cm_p"])

    def gdn_intra(C_, c0, tb, nfull, t):
        si = t.si
        P = slice(0, C_)
        tri, seqt, maskT, nstt = C("tri_" + tb), C("seq_" + tb), C("maskT_" + tb), C("nst_" + tb)
        ident = C("ident")

        def Bk(b):
            return ps[(b + 4 * si) % 8]

        def bk(b):
            return pk((b + 4 * si) % 8)

        def K(n, *a):
            return (n, si) + a

        def h3(x):
            return x.rearrange("p (h c) -> p h c", h=4)[:, :, 0:C_]
        ab_sb, g4, beta4, gcc, egc, kbgs, kdcs = t.ab_sb, t.g4, t.beta4, t.gcc, t.egc, t.kbgs, t.kdcs
        gbc, bbc, egr, dec, dnb, YT, XX = t.gbc, t.bbc, t.egr, t.dec, t.dnb, t.YT, t.XX
        for k in range(KC):
            mm(Bk(0)[P, 0:8], uT[:, k, c0:c0 + C_], wab[:, k, :], k == 0, k == KC - 1, [("uT", k, "A"), ("uT", k, "B"), ("uT", k, "S"), "wab"], bk(0))
        yield
        act(ab_sb[P, :], Bk(0)[P, 0:8], AF.Copy, bk(0), [K("ab_sb")])
        tt("dve", g4[P, :], ab_sb[P, 0:4], dtb[P, :], ALU.add, [K("ab_sb"), "dtb"], [K("g4")])
        act(g4[P, :], g4[P, :], AF.Exp, [K("g4")], [K("g4")])
        act(g4[P, :], g4[P, :], AF.Ln, [K("g4"), "epst"], [K("g4")], bias=epst[P, 2:3], scale=1.0)
        tt("dve", g4[P, :], g4[P, :], nea[P, :], ALU.mult, [K("g4"), "nea"], [K("g4")])
        act(beta4[P, :], ab_sb[P, 4:8], AF.Exp, [K("ab_sb")], [K("beta4")], scale=-1.0)
        ts("dve", beta4[P, :], beta4[P, :], 1.0, None, ALU.add, None, [K("beta4")], [K("beta4")])
        S.add("dve", lambda e: e.reciprocal(out=beta4[P, :], in_=beta4[P, :]), reads=[K("beta4")], writes=[K("beta4")])
        cp("dve", gbc[P, :, :], g4[P, :].unsqueeze(2).to_broadcast([C_, 4, 128]), [K("g4")], [K("gbc")])
        cp("dve", bbc[P, :, :], beta4[P, :].unsqueeze(2).to_broadcast([C_, 4, 128]), [K("beta4")], [K("bbc")])
        yield
        mm(Bk(1)[P, 0:4], tri[P, P], g4[P, :], True, True, ["cst", K("g4")], bk(1))
        mm(Bk(1)[P, 4:8], seqt[P, P], g4[P, :], True, True, ["cst", K("g4")], bk(1))
        for h in range(4):
            mm(Bk(2)[:, h * 128:h * 128 + C_], gbc[P, h, :], tri[P, P], True, True, [K("gbc"), "cst"], bk(2))
            mm(Bk(3)[:, h * 128:h * 128 + C_], bbc[P, h, :], identb[P, P], True, True, [K("bbc"), "identb"], bk(3))
            mm(Bk(1)[:, 64 + h * 16:64 + h * 16 + 16], gbc[P, h, :], seqt[P, 0:16], True, True, [K("gbc"), "cst"], bk(1))
        yield
        cp("dve", gcc[P, :], Bk(1)[P, 0:8], bk(1), [K("gcc")])
        act(t.glast2[:, :], Bk(1)[:, 64:128], AF.Copy, bk(1), [K("glast")])
        act(t.glast2[:, :], t.glast2[:, :], AF.Exp, [K("glast")], [K("glast")])
        act(egr[:, :, 0:C_], h3(Bk(2)[:, :]), AF.Copy, bk(2), [K("egr")])
        act(egr[:, :, 0:C_], egr[:, :, 0:C_], AF.Exp, [K("egr")], [K("egr")])
        act(egc[P, :], gcc[P, 0:4], AF.Exp, [K("gcc")], [K("egc")])
        tt("dve", kbgs[P, :], beta4[P, :], egc[P, :], ALU.mult, [K("beta4"), K("egc")], [K("kbgs")])
        tt("dve", kdcs[P, :], gcc[P, 4:8], gcc[P, 0:4], ALU.subtract, [K("gcc")], [K("kdcs")])
        act(kdcs[P, :], kdcs[P, :], AF.Exp, [K("kdcs")], [K("kdcs")])
        for h in range(4):
            stt("dve", dec[P, h, 0:C_], Bk(2)[P, h * 128:h * 128 + C_], gcc[P, h:h + 1], maskT[P, P], ALU.subtract, ALU.add,
                bk(2) + [K("gcc"), "cst"], [K("dec")])
        act(dec[P, :, 0:C_], dec[P, :, 0:C_], AF.Exp, [K("dec")], [K("dec")])
        yield
        for h in range(4):
            kh = qkT[:, 4 + h, c0:c0 + C_]
            qh = qkT[:, h, c0:c0 + C_]
            mm(Bk(4)[P, h * 128:h * 128 + C_], kh, kh, True, True, [("qkT", 4 + h)], bk(4))
            mm(Bk(5)[P, h * 128:h * 128 + C_], kh, qh, True, True, [("qkT", 4 + h), ("qkT", h)], bk(5))
        psb7 = Bk(2)[:, :].bitcast(BF16)
        for h in range(4):
            tr(psb7[P, h * 128:(h + 1) * 128], qkT[:, 4 + h, c0:c0 + C_], identb[:, :], [("qkT", 4 + h), "identb"], bk(2))
            tr(psb7[P, 512 + h * 128:512 + (h + 1) * 128], vT[:, h, c0:c0 + C_], identb[:, :], [("vT", h), "identb"], bk(2))
        yield
        tt("dve", dnb[P, :, 0:C_], dec[P, :, 0:C_], nstt[P, P].unsqueeze(1).to_broadcast([C_, 4, C_]), ALU.mult, [K("dec"), "cst"], [K("dnb")])
        tt("dve", dnb[P, :, 0:C_], h3(Bk(3)[P, :]), dnb[P, :, 0:C_], ALU.mult, bk(3) + [K("dnb")], [K("dnb")])
        Y0 = YT[0]
        tt("dve", Y0[P, :, 0, 0:C_], h3(Bk(4)[P, :]), dnb[P, :, 0:C_], ALU.mult, bk(4) + [K("dnb")], [K("YT", 0)])
        tt("dve", t.qkm[P, :, 0:C_], h3(Bk(5)[P, :]), dec[P, :, 0:C_], ALU.mult, bk(5) + [K("dec")], [K("qkm")])
        cp("dve", Y0[P, :, 1, 0:C_], identb[P, P].unsqueeze(1).to_broadcast([C_, 4, C_]), ["identb"], [K("YT", 0)])
        k3 = psb7[P, 0:512].rearrange("p (h d) -> p h d", h=4)
        v3_ = psb7[P, 512:1024].rearrange("p (h d) -> p h d", h=4)
        tt("dve", t.kbg[P, :, :], k3, kbgs[P, :].unsqueeze(2).to_broadcast([C_, 4, 128]), ALU.mult, bk(2) + [K("kbgs")], [K("kbg")])
        tt("dve", t.kdec[P, :, :], k3, kdcs[P, :].unsqueeze(2).to_broadcast([C_, 4, 128]), ALU.mult, bk(2) + [K("kdcs")], [K("kdec")])
        tt("dve", t.vb[P, :, :], v3_, beta4[P, :].unsqueeze(2).to_broadcast([C_, 4, 128]), ALU.mult, bk(2) + [K("beta4")], [K("vb")])
        yield
        psb6 = Bk(6)[:, :].bitcast(BF16)
        for h in range(4):
            tr(psb6[P, h * 128:h * 128 + C_], Y0[P, h, 0, 0:C_], identb[P, P], [K("YT", 0), "identb"], bk(6))
        yield
        cp("dve", XX[0][P, :, 0:C_], psb6[P, 0:512].rearrange("p (h c) -> p h c", h=4)[:, :, 0:C_], bk(6), [K("XX", 0)])
        yield
        cur = 0
        for s_ in range(nfull):
            nxt = 1 - cur
            for h in range(4):
                bank = 0 if h < 2 else 1
                off = (h % 2) * 256
                if C_ == 128:
                    mm(Bk(bank)[P, off:off + 256], XX[cur][P, h, 0:C_], YT[cur][P, h, :, :].rearrange("p t c -> p (t c)"), True, True,
                       [K("XX", cur), K("YT", cur)], bk(bank))
                else:
                    mm(Bk(bank)[P, off:off + C_], XX[cur][P, h, 0:C_], YT[cur][P, h, 0, 0:C_], True, True, [K("XX", cur), K("YT", cur)], bk(bank))
                    mm(Bk(bank)[P, off + 128:off + 128 + C_], XX[cur][P, h, 0:C_], YT[cur][P, h, 1, 0:C_], True, True, [K("XX", cur), K("YT", cur)], bk(bank))
                mm(Bk(2)[P, h * 128:h * 128 + C_], YT[cur][P, h, 0, 0:C_], XX[cur][P, h, 0:C_], True, True, [K("XX", cur), K("YT", cur)], bk(2))
            yield
            for bank in range(2):
                pv = Bk(bank)[P, :].rearrange("p (h t c) -> p h t c", h=2, t=2)
                act(YT[nxt][P, 2 * bank:2 * bank + 2, 0, 0:C_], pv[:, :, 0, 0:C_], AF.Copy, bk(bank), [K("YT", nxt)])
                tt("dve", YT[nxt][P, 2 * bank:2 * bank + 2, 1, 0:C_], pv[:, :, 1, 0:C_], YT[cur][P, 2 * bank:2 * bank + 2, 1, 0:C_], ALU.add,
                   bk(bank) + [K("YT", cur)], [K("YT", nxt)])
            act(XX[nxt][P, :, 0:C_], h3(Bk(2)[P, :]), AF.Copy, bk(2), [K("XX", nxt)])
            yield
            cur = nxt
        nxt = 1 - cur
        for h in range(4):
            mm(Bk(0)[P, h * 128:h * 128 + C_], XX[cur][P, h, 0:C_], YT[cur][P, h, 1, 0:C_], True, True, [K("XX", cur), K("YT", cur)], bk(0))
        yield
        tt("dve", YT[nxt][P, :, 1, 0:C_], h3(Bk(0)[P, :]), YT[cur][P, :, 1, 0:C_], ALU.add, bk(0) + [K("YT", cur)], [K("YT", nxt)])
        Tm = YT[nxt]
        kT_ = K("YT", nxt)
        yield
        for h in range(4):
            mm(Bk(3)[:, h * 128:h * 128 + C_], t.kbg[P, h, :], Tm[P, h, 1, 0:C_], True, True, [K("kbg"), kT_], bk(3))
        yield
        act(t.nwT[:, :, 0:C_], h3(Bk(3)[:, :]), AF.Copy, bk(3), [K("nwT")], scale=-1.0)
        q3 = qkT[:, 0:4, c0:c0 + C_]
        tt("dve", t.qdT[:, :, 0:C_], q3, egr[:, :, 0:C_], ALU.mult, [("qkT", h) for h in range(4)] + [K("egr")], [K("qdT")])
        return Tm, kT_

    def lockstep(gens):
        res = [None] * len(gens)
        live = list(range(len(gens)))
        while live:
            for i in list(live):
                try:
                    next(gens[i])
                except StopIteration as e:
                    res[i] = e.value
                    live.remove(i)
        return res

    def gdn_out(C_, c0, t):
        si = t.si

        def Bk(b):
            return ps[(b + 4 * si) % 8]

        def bk(b):
            return pk((b + 4 * si) % 8)

        def K(n, *a):
            return (n, si) + a

        def h3(x):
            return x.rearrange("p (h c) -> p h c", h=4)[:, :, 0:C_]
        act(t.osq[:, :, 0:C_], h3(Bk(5)[:, :]), AF.Square, bk(5), [K("osq")])
        yield
        for h in range(4):
            mm(Bk(7)[:, h * 128:h * 128 + C_], ones128[:], t.osq[:, h, 0:C_], True, True, ["ones128", K("osq")], bk(7))
        yield
        act(t.rinv[:, :, 0:C_], h3(Bk(7)[:, :]), AF.Ln, bk(7) + ["epst"], [K("rinv")], bias=epst[:, 1:2], scale=1.0)
        act(t.rinv[:, :, 0:C_], t.rinv[:, :, 0:C_], AF.Exp, [K("rinv")], [K("rinv")], scale=-0.5)
        yield
        tt("dve", t.rinv[:, :, 0:C_], h3(Bk(5)[:, :]), t.rinv[:, :, 0:C_], ALU.mult, bk(5) + [K("rinv")], [K("rinv")])
        stt("dve", mix[:, 0:4, c0:c0 + C_], t.rinv[:, :, 0:C_], dng[:, 0:1], mix[:, 0:4, c0:c0 + C_], ALU.mult, ALU.mult,
            [K("rinv"), "dng"] + [("mix", h) for h in range(4)], [("mix", h) for h in range(4)])

    def gdn_prompt_inter(c0, t, Tm, kT_):
        C_ = 128
        P = slice(0, C_)
        si = t.si

        def Bk(b):
            return ps[(b + 4 * si) % 8]

        def bk(b):
            return pk((b + 4 * si) % 8)

        def K(n, *a):
            return (n, si) + a
        for h in range(4):
            mm(Bk(4)[P, h * 128:(h + 1) * 128], Tm[P, h, 1, 0:C_], t.vb[P, h, :], True, False, [kT_, K("vb")], bk(4))
            mm(Bk(4)[P, h * 128:(h + 1) * 128], t.nwT[:, h, 0:C_], Sbf[:, h, :], False, True, [K("nwT"), "Sbf"], bk(4))
        act(t.vnew[P, :, :], Bk(4)[P, :].rearrange("p (h d) -> p h d", h=4), AF.Copy, bk(4), [K("vnew")])
        for h in range(4):
            mm(Bk(5)[:, h * 128:h * 128 + C_], Sbf[:, h, :], t.qdT[:, h, 0:C_], True, False, ["Sbf", K("qdT")], bk(5))
            mm(Bk(5)[:, h * 128:h * 128 + C_], t.vnew[P, h, :], t.qkm[P, h, 0:C_], False, True, [K("vnew"), K("qkm")], bk(5))
        for h in range(4):
            mm(Bk(6)[:, h * 128:(h + 1) * 128], t.kdec[P, h, :], t.vnew[P, h, :], True, True, [K("kdec"), K("vnew")], bk(6))
        for h in range(4):
            stt("dve", Sst[:, h, :], Sst[:, h, :], t.glast[:, h, 0:1], Bk(6)[:, h * 128:(h + 1) * 128], ALU.mult, ALU.add,
                ["Sst", K("glast")] + bk(6), ["Sst"])
        act(Sbf[:, :, :], Sst[:, :, :], AF.Copy, ["Sst"], ["Sbf"])

    def gdn_prompt_pair(c0a, c0b):
        if "SEQ" in debug:
            r = lockstep([gdn_intra(128, c0a, "p", 6, TS[0])]) + lockstep([gdn_intra(128, c0b, "p", 6, TS[1])])
        else:
            r = lockstep([gdn_intra(128, c0a, "p", 6, TS[0]), gdn_intra(128, c0b, "p", 6, TS[1])])
        if c0a == 0 and "pairdump" in debug and not dbg_outs:
            for si in range(2):
                t = TS[si]
                for nm in ("kbg", "kdec", "vb", "nwT", "qdT", "qkm"):
                    debug_dump(nm + str(si), getattr(t, nm)[:, :, :], [128, 4, 128], [(nm, si)], BF16)
                debug_dump("T" + str(si), r[si][0][:, :, :, :], [128, 4, 2, 128], [r[si][1]], BF16)
                debug_dump("egr" + str(si), t.egr[:, :, :], [128, 4, 128], [("egr", si)], F32)
                debug_dump("dec" + str(si), t.dec[:, :, :], [128, 4, 128], [("dec", si)], F32)
            raise _Stop()
        gdn_prompt_inter(c0a, TS[0], *r[0])
        gdn_prompt_inter(c0b, TS[1], *r[1])
        lockstep([gdn_out(128, c0a, TS[0]), gdn_out(128, c0b, TS[1])])

    def gdn_sample():
        C_ = NS
        c0 = NTP
        P = slice(0, C_)
        t = TS[0]
        dma("sp", S0[0][:, :, :], s0_d[:, :, 0, :], [], [("S0", 0)], ("S0", 0))
        dma("pool", S0b[:, :, :], s0_d[:, :, 0, :], [], ["S0b"], "S0b")
        dma("sp", S0[1][:, :, :], s0_d[:, :, 1, :], [], [("S0", 1)], ("S0", 1))
        (Tm, kT_), = lockstep([gdn_intra(C_, c0, "s", 1, t)])
        K = lambda n, *a: (n, 0) + a
        selP = C("selP")
        for h in range(4):
            s0f = S0[h % 2]
            ks0 = ("S0", h % 2)
            if h >= 2:
                dma("sp", s0f[:, :, :], s0_d[:, :, h, :], [], [ks0], ks0)
            if h >= 1:
                dma("pool", S0b[:, :, :], s0_d[:, :, h, :], [], ["S0b"], "S0b")
            tt("dve", nwTm[:, :, :], t.nwT[:, h, 0:C_].unsqueeze(1).to_broadcast([128, NB, C_]), selb[:, :, :], ALU.mult, [K("nwT"), "selb"], ["nwTm"])
            tt("dve", qdTm[:, :, :], t.qdT[:, h, 0:C_].unsqueeze(1).to_broadcast([128, NB, C_]), selb[:, :, :], ALU.mult, [K("qdT"), "selb"], ["qdTm"])
            tt("dve", kdm[P, :, :], t.kdec[P, h, :].unsqueeze(1).to_broadcast([C_, NB, 128]),
               selP[P, 0:NB].unsqueeze(2).to_broadcast([C_, NB, 128]), ALU.mult, [K("kdec"), "cst"], ["kdm"])
            mm(ps[4][P, 0:128], Tm[P, h, 1, 0:C_], t.vb[P, h, :], True, False, [kT_, K("vb")], pk(4))
            for b in range(NB):
                mm(ps[4][P, 0:128], nwTm[:, b, :], S0b[:, b, :], False, b == NB - 1, ["nwTm", "S0b"], pk(4))
            act(t.vnew[P, 0, :], ps[4][P, 0:128], AF.Copy, pk(4), [K("vnew")])
            for b in range(NB):
                mm(ps[5][:, h * 128:h * 128 + C_], S0b[:, b, :], qdTm[:, b, :], b == 0, False, ["qdTm", "S0b"], pk(5))
            mm(ps[5][:, h * 128:h * 128 + C_], t.vnew[P, 0, :], t.qkm[P, h, 0:C_], False, True, [K("vnew"), K("qkm")], pk(5))
            for b in range(NB):
                mm(ps[b // 4][:, (b % 4) * 128:(b % 4 + 1) * 128], kdm[P, b, :], t.vnew[P, 0, :], True, True, ["kdm", K("vnew")], pk(b // 4))
            for q in range(4):
                tt("dve", s0f[:, 4 * q:4 * q + 4, :], s0f[:, 4 * q:4 * q + 4, :],
                   t.glast[:, h, 4 * q:4 * q + 4].unsqueeze(2).to_broadcast([128, 4, 128]), ALU.mult, [ks0, K("glast")], [ks0])
                tt("dve", s0f[:, 4 * q:4 * q + 4, :], s0f[:, 4 * q:4 * q + 4, :], ps[q][:, :].rearrange("p (b d) -> p b d", b=4), ALU.add,
                   [ks0] + pk(q), [ks0])
            dma("sp", ssms_d[:, :, h, :], s0f[:, :, :], [ks0], [("ssm_s", h)], ks0)
            out_keys.append(("ssm_s", h))
        lockstep([gdn_out(C_, c0, t)])

    def mixer_out(st):
        for m in range(KC):
            for (g, lo, hi) in groups(st):
                if g != "S":
                    bank = (m % 2) * 2 + (0 if g == "A" else 1)
                    o = ps[bank][:, :]
                else:
                    bank = 4 + m % 2
                    o = ps[bank][:, 0:64]
                kp = pk(bank)
                for k in range(KC):
                    mm(o, wout[:, k, m * 128:(m + 1) * 128], mix[:, k, lo:hi], k == 0, k == KC - 1, ["wout", ("mix", k)], kp)
                if g != "S":
                    stt("dve", xT[:, m, lo:hi], o, gsc[1][:, m, 0:1], xT[:, m, lo:hi], ALU.mult, ALU.add,
                        kp + [("gsc", 1), ("xT", m, g)], [("xT", m, g)])
                else:
                    tt("dve", v3(st_xh[:, 0:64]), v3(o), bc_s(gsc[1], m), ALU.mult, kp + [("gsc", 1)], ["st_xh"])
                    tt("dve", xT[:, m, lo:hi], st_xh[:, 0:64], xT[:, m, lo:hi], ALU.add, ["st_xh", ("xT", m, g)], [("xT", m, g)])

    def store_y(st):
        for gi_, (g, lo, hi) in enumerate((("A", 0, 512), ("B", 512, 1024))):
            dma("sp", yp_d[:, :, st * NTP + lo:st * NTP + hi], xT[:, :, lo:hi], [("xT", c, g) for c in range(KC)],
                [("yp", st, g)], ("yout", st, g))
            out_keys.append(("yp", st, g))
        if st == 0:
            dma("sp", ys_d, xT[:, :, NTP:NT], [("xT", c, "S") for c in range(KC)], ["ys"], "ysout")
            out_keys.append("ys")

    try:
        for st in range(2 if stop_after != "ada" else 0):
            load_x(st)
            if stop_after == "load":
                store_y(st)
                continue
            modulate0(st)
            if stop_after == "mod0":
                S.fence()
                store_y(st)
                continue
            ffn(st, 0, 0)
            if stop_after in ("ffn1", "up"):
                store_y(st)
                continue
            layernorm(st, 0, G1, B1, ("GB", 1))
            for b_ in range(3):
                prefetch(("win", st, b_), win_d[b_])
            S.fence()
            if stop_after == "ln1":
                store_y(st)
                continue
            mixer_proj(st)
            S.fence()
            if stop_after == "proj":
                continue
            dma("pool", wout[:, :, :], wout_d, [], ["wout"], "wout")
            for n in range(4):
                gdn_prompt_pair(2 * n * 128, (2 * n + 1) * 128)
            if st == 0:
                S.fence()
                gdn_sample()
            S.fence()
            if st == 1:
                dma("sp", ssmp_d, Sst[:, :, :], ["Sst"], ["ssm_p"], "ssm_p")
                out_keys.append("ssm_p")
            if stop_after == "gdn":
                continue
            mixer_out(st)
            layernorm(st, 1, G2, B2, ("GB", 2))
            prefetch(("wg", st, 1, 0), wg_d[1][0])
            prefetch(("wu", st, 1, 0), wu_d[1][0])
            prefetch(("wg", st, 1, 1), wg_d[1][1])
            S.fence()
            if stop_after == "ln2":
                store_y(st)
                continue
            ffn(st, 1, 2)
            layernorm(st, 2, None, None, None)
            store_y(st)


    except _Stop:
        pass

    S.add("sp", lambda e: e.nop(), reads=out_keys)
    with nc.Block() as block:
        S.finalize(block)
    return nc, dbg_outs


def _blk(w, nb):
    return np.ascontiguousarray(w.reshape(KC, 128, nb, 256).transpose(2, 1, 0, 3))


def _prep_shared(inp):
    f = lambda a: np.asarray(a, dtype=np.float32)
    sh = {}
    sh["wada"] = _blk(f(inp["w_ada"])[0], 36)
    sh["bada"] = np.ascontiguousarray(f(inp["b_ada"])[0].reshape(72, 128).T)
    sh["lng"] = np.ascontiguousarray(f(inp["ln_g"])[0].reshape(24, 128).T)
    sh["lnb"] = np.ascontiguousarray(f(inp["ln_b"])[0].reshape(24, 128).T)
    for i, nm in ((1, "ffn1"), (2, "ffn2")):
        sh["wg%d" % i] = _blk(f(inp[nm + "_wg"])[0], 11)
        sh["wu%d" % i] = _blk(f(inp[nm + "_wu"])[0], 11)
        wd = f(inp[nm + "_wd"])[0]
        sh["wd%d" % i] = np.ascontiguousarray(wd.reshape(FC, 128, KC, 128).transpose(2, 1, 0, 3))
    win = f(inp["w_in"])[0]
    qkv, ab, og = win[:, 0:1536], win[:, 1536:1544], win[:, 1544:2056]
    scb, scc, sch = win[:, 2056:2568], win[:, 2568:3080], win[:, 3080:3592]
    inter = np.concatenate([np.concatenate([scc[:, j * 128:(j + 1) * 128], sch[:, j * 128:(j + 1) * 128]], axis=1)
                            for j in range(4)], axis=1)
    main = np.concatenate([qkv, og, scb, inter], axis=1)
    sh["win"] = _blk(main, 14)
    sh["wab"] = np.ascontiguousarray(ab.reshape(KC, 128, 8).transpose(1, 0, 2))
    sh["wout"] = np.ascontiguousarray(f(inp["w_out"])[0].reshape(KC, 128, D).transpose(1, 0, 2))
    sh["cqw"] = np.ascontiguousarray(f(inp["conv_qkv_w"])[0].reshape(4, 12, 128).transpose(2, 0, 1))
    sh["cmw"] = np.ascontiguousarray(f(inp["conv_mix_w"])[0].reshape(3, 4, 128).transpose(2, 0, 1))
    sh["dng"] = np.ascontiguousarray(f(inp["dn_norm_g"])[0].reshape(128, 1))
    sh["alog"] = np.ascontiguousarray(f(inp["a_log"])[0])
    sh["dtb"] = np.ascontiguousarray(f(inp["dt_bias"])[0])
    sh["cst"] = _CST
    sh["sel"] = _SEL
    return sh


def _prep_core(inp, c):
    f = lambda a: np.asarray(a, dtype=np.float32)
    m = {}
    xp = f(inp["x_prompt"])[c]
    m["xp"] = np.ascontiguousarray(xp.T.reshape(KC, 128, NP).transpose(1, 0, 2))
    xs = f(inp["x_sample"])[NB * c:NB * (c + 1)]
    xs = xs.transpose(1, 0, 2).reshape(NS, D)
    m["xs"] = np.ascontiguousarray(xs.T.reshape(KC, 128, NS).transpose(1, 0, 2))
    m["cc"] = np.ascontiguousarray(np.concatenate([f(inp["c_prompt"])[c:c + 1], f(inp["c_sample"])[NB * c:NB * (c + 1)]], axis=0))
    m["s0"] = np.ascontiguousarray(f(inp["state_ssm"])[0, NB * c:NB * (c + 1)].transpose(2, 0, 1, 3))
    cq = f(inp["state_conv_qkv"])[0, NB * c:NB * (c + 1)]
    m["cq"] = np.ascontiguousarray(cq.reshape(NB, 3, 12, 128).transpose(3, 2, 1, 0).reshape(128, 12, 48))
    cm = f(inp["state_conv_mix"])[0, NB * c:NB * (c + 1)]
    m["cm"] = np.ascontiguousarray(cm.reshape(NB, 2, 4, 128).transpose(3, 2, 1, 0).reshape(128, 4, 32))
    return m


def _run(inp, debug=(), stop_after=None, trace=False, ncores=8):
    nc, dbg = build_program(debug=debug, stop_after=stop_after)
    sh = _prep_shared(inp)
    in_maps = []
    for c in range(ncores):
        m = dict(sh)
        m.update(_prep_core(inp, c))
        in_maps.append(m)
    res = run_bass_kernel_spmd(nc, in_maps, core_ids=list(range(ncores)), trace=trace)
    return res


def _assemble(res):
    R = list(res.results)
    while len(R) < 8:
        R.append(R[0])
    y_p = np.stack([R[c]["yp"].transpose(1, 0, 2).reshape(D, NP).T for c in range(8)])
    ys = []
    for c in range(8):
        a = R[c]["ys"].transpose(1, 0, 2).reshape(D, NS).T
        ys.append(a.reshape(4, NB, D).transpose(1, 0, 2))
    y_s = np.concatenate(ys, axis=0)
    ssm_p = np.stack([R[c]["ssm_p"].transpose(1, 0, 2) for c in range(8)])[None]
    cq_p = np.stack([R[c]["cq_p"].transpose(2, 1, 0).reshape(3, 1536) for c in range(8)])[None]
    cm_p = np.stack([R[c]["cm_p"].transpose(2, 1, 0).reshape(2, 512) for c in range(8)])[None]
    ssm_s = np.concatenate([R[c]["ssm_s"].transpose(1, 2, 0, 3) for c in range(8)], axis=0)[None]
    cq_s = np.concatenate([R[c]["cq_s"].reshape(128, 12, 3, NB).transpose(3, 2, 1, 0).reshape(NB, 3, 1536)
                           for c in range(8)], axis=0)[None]
    cm_s = np.concatenate([R[c]["cm_s"].reshape(128, 4, 2, NB).transpose(3, 2, 1, 0).reshape(NB, 2, 512)
                           for c in range(8)], axis=0)[None]
    outs = (y_p, y_s, ssm_p, cq_p, cm_p, ssm_s, cq_s, cm_s)
    return tuple(np.ascontiguousarray(o, dtype=np.float32) for o in outs)


def kernel(**inputs):
    res = _run(inputs)
    return _assemble(res)
```

```python
import numpy as np
import concourse.bass as bass
import concourse.mybir as mybir
from concourse.bass_utils import run_bass_kernel_spmd

F32 = mybir.dt.float32
BF16 = mybir.dt.bfloat16
AF = mybir.ActivationFunctionType
ALU = mybir.AluOpType

D = 1024
KC = 8
FF = 2816
FC = 22
NP = 2048
NTP = 1024
NS = 64
NT = NTP + NS
NB = 16
ALPHA = 2.0 ** 0.25
LN_EPS = 1e-5
RMS_EPS = 1e-6
BIGNEG = -1.0e5
ENGS = ("pe", "act", "dve", "pool", "sp")


class _Op:
    __slots__ = ("eng", "fn", "dma", "chan", "eidx", "deps", "signal", "count", "waits")


class Sched:
    def __init__(self, nc):
        self.nc = nc
        self.ops = {e: [] for e in ENGS}
        self.lastw = {}
        self.readers = {}
        self.chan_n = {}
        self.chan_last = {}

    def add(self, eng, fn, reads=(), writes=(), dma=False, chan=None):
        op = _Op()
        op.eng, op.fn, op.dma, op.chan = eng, fn, dma, chan
        op.eidx = len(self.ops[eng])
        op.signal = False
        op.count = None
        op.waits = []
        deps = []
        for r in reads:
            w = self.lastw.get(r)
            if w is not None:
                deps.append(w)
        for w_ in writes:
            w = self.lastw.get(w_)
            if w is not None:
                deps.append(w)
            deps.extend(self.readers.get(w_, ()))
        for r in reads:
            self.readers.setdefault(r, []).append(op)
        for w_ in writes:
            self.lastw[w_] = op
            self.readers[w_] = []
        op.deps = [d for d in deps if d is not op]
        if dma:
            n = self.chan_n.get(chan, 0) + 1
            self.chan_n[chan] = n
            op.count = 16 * n
            op.signal = True
            prev = self.chan_last.get(chan)
            if prev is not None:
                op.deps.append(prev)
            self.chan_last[chan] = op
        self.ops[eng].append(op)
        return op

    def fence(self):
        lasts = [self.ops[e][-1] for e in ENGS if self.ops[e]]
        lastdma = list(self.chan_last.values())
        for e in ENGS:
            op = self.add(e, lambda en: en.nop())
            op.deps = [d for d in lasts if d is not op] + lastdma

    def finalize(self, block):
        nc = self.nc
        for e in ENGS:
            wd = {}
            for op in self.ops[e]:
                need = {}
                for d in op.deps:
                    if d.dma:
                        key, val = ("c", d.chan), d.count
                    else:
                        if d.eng == e and not op.dma and e == "pe":
                            continue
                        key, val = ("e", d.eng), d.eidx
                    if key not in need or need[key][0] < val:
                        need[key] = (val, d)
                for key, (val, d) in need.items():
                    if wd.get(key, -1) >= val:
                        continue
                    wd[key] = val
                    d.signal = True
                    op.waits.append(d)
        esem = {}
        for e in ENGS:
            c = 0
            for op in self.ops[e]:
                if not op.dma and op.signal:
                    c += 1
                    op.count = c
            esem[e] = nc.alloc_semaphore("s_" + e)
        csem = {ch: nc.alloc_semaphore("c_%d" % i) for i, ch in enumerate(self.chan_n)}
        engmap = {"pe": block.tensor, "act": block.scalar, "dve": block.vector,
                  "pool": block.gpsimd, "sp": block.sync}

        def mk(e):
            def body(en):
                for op in self.ops[e]:
                    for d in op.waits:
                        if d.dma:
                            en.wait_ge(csem[d.chan], d.count)
                        else:
                            en.wait_ge(esem[d.eng], d.count)
                    ins = op.fn(en)
                    if op.dma:
                        ins.then_inc(csem[op.chan], 16)
                    elif op.signal:
                        ins.then_inc(esem[e], 1)
            return body

        for e in ENGS:
            if self.ops[e]:
                engmap[e](mk(e))


def _const_tables():
    i = np.arange(128)
    cols = {}
    ident = np.eye(128, dtype=np.float32)
    tri_p = (i[:, None] <= i[None, :]).astype(np.float32)
    seq_p = np.ones((128, 128), np.float32)
    maskT_p = np.where(i[None, :] >= i[:, None], 0.0, BIGNEG).astype(np.float32)
    nst_p = np.where(i[None, :] > i[:, None], -1.0, 0.0).astype(np.float32)
    same = (i[:, None] % 16) == (i[None, :] % 16)
    valid = (i[:, None] < 64) & (i[None, :] < 64)
    tri_s = (same & (i[:, None] <= i[None, :]) & valid).astype(np.float32)
    seq_s = (same & valid).astype(np.float32)
    maskT_s = np.where(same & (i[None, :] >= i[:, None]) & valid, 0.0, BIGNEG).astype(np.float32)
    nst_s = np.where(same & (i[None, :] > i[:, None]) & valid, -1.0, 0.0).astype(np.float32)
    s = np.arange(64)
    sel = (s[None, :] % 16 == np.arange(16)[:, None]).astype(np.float32).reshape(1, 16 * 64)
    sel = np.repeat(sel, 128, axis=0)
    selP = ((i[:, None] % 16) == np.arange(16)[None, :]).astype(np.float32)
    blocks = [("ident", ident), ("tri_p", tri_p), ("seq_p", seq_p), ("maskT_p", maskT_p), ("nst_p", nst_p),
              ("tri_s", tri_s), ("seq_s", seq_s), ("maskT_s", maskT_s), ("nst_s", nst_s),
              ("selP", selP)]
    off = 0
    for name, a in blocks:
        cols[name] = (off, a.shape[1])
        off += a.shape[1]
    tab = np.concatenate([a for _, a in blocks], axis=1).astype(np.float32)
    return tab, cols, sel


_CST, _CSTCOLS, _SEL = _const_tables()
NCST = _CST.shape[1]


class _Stop(Exception):
    pass


def build_program(debug=(), stop_after=None):
    nc = bass.Bass("TRN2", target_bir_lowering=False)
    S = Sched(nc)
    dbg_outs = {}

    def din(name, shape, dt=F32):
        return nc.dram_tensor(name, list(shape), dt, kind="ExternalInput").ap()

    def dout(name, shape, dt=F32):
        return nc.dram_tensor(name, list(shape), dt, kind="ExternalOutput").ap()

    xp_d = din("xp", [128, KC, NP])
    xs_d = din("xs", [128, KC, NS])
    cc_d = din("cc", [17, D])
    s0_d = din("s0", [128, NB, 4, 128])
    cq_d = din("cq", [128, 12, 48])
    cm_d = din("cm", [128, 4, 32])
    wada_d = din("wada", [36, 128, KC, 256])
    bada_d = din("bada", [128, 72])
    lng_d = din("lng", [128, 24])
    lnb_d = din("lnb", [128, 24])
    wg_d = [din("wg1", [11, 128, KC, 256]), din("wg2", [11, 128, KC, 256])]
    wu_d = [din("wu1", [11, 128, KC, 256]), din("wu2", [11, 128, KC, 256])]
    wd_d = [din("wd1", [8, 128, FC, 128]), din("wd2", [8, 128, FC, 128])]
    win_d = din("win", [14, 128, KC, 256])
    wab_d = din("wab", [128, KC, 8])
    wout_d = din("wout", [128, KC, D])
    cqw_d = din("cqw", [128, 4, 12])
    cmw_d = din("cmw", [128, 3, 4])
    dng_d = din("dng", [128, 1])
    alog_d = din("alog", [4])
    dtb_d = din("dtb", [4])
    cst_d = din("cst", [128, NCST])
    sel_d = din("sel", [128, NB * NS])

    yp_d = dout("yp", [128, KC, NP])
    ys_d = dout("ys", [128, KC, NS])
    ssmp_d = dout("ssm_p", [128, 4, 128])
    cqp_d = dout("cq_p", [128, 12, 3])
    cmp_d = dout("cm_p", [128, 4, 2])
    ssms_d = dout("ssm_s", [128, NB, 4, 128])
    cqs_d = dout("cq_s", [128, 12, 48])
    cms_d = dout("cm_s", [128, 4, 32])
    out_keys = []

    def sb(name, shape, dt=F32):
        return nc.alloc_sbuf_tensor("sb_" + name, list(shape), dt)

    xT = sb("xT", [128, KC, NT])
    uT = sb("uT", [128, KC, NT], BF16)
    cst = sb("cst", [128, NCST])
    identb = sb("identb", [128, 128], BF16)
    selb = sb("selb", [128, NB, NS], BF16)
    onesD = sb("onesD", [128, 128], BF16)
    ones128 = sb("ones128", [128, 128], BF16)
    epst = sb("epst", [128, 4])
    bada = sb("bada", [128, 72])
    lng = sb("lng", [128, 24])
    lnb = sb("lnb", [128, 24])
    modall = sb("modall", [128, 72, 17])
    sc1p0 = sb("sc1p0", [128, KC, 17])
    G1 = sb("G1", [128, KC, 17]); B1 = sb("B1", [128, KC, 17])
    G2 = sb("G2", [128, KC, 17]); B2 = sb("B2", [128, KC, 17])
    gsc = [sb("gsc%d" % i, [128, KC, 17]) for i in range(3)]
    cqw = sb("cqw", [128, 4, 12]); cmw = sb("cmw", [128, 3, 4]); dng = sb("dng", [128, 1])
    alog = sb("alog", [128, 4]); dtb = sb("dtb", [128, 4]); nea = sb("nea", [128, 4])
    wab = sb("wab", [128, KC, 8], BF16)
    Sst = sb("Sst", [128, 4, 128]); Sbf = sb("Sbf", [128, 4, 128], BF16)
    halo_q = sb("halo_q", [128, 12, 3]); halo_m = sb("halo_m", [128, 4, 2])
    R1 = sb("R1", [128, 24576 // 4])
    R2 = sb("R2", [128, (FC * NT * 2) // 4])
    wblk = [sb("wblk%d" % i, [128, KC, 256], BF16) for i in range(4)]
    R4 = sb("R4", [128, 16384 // 4])
    R5 = sb("R5", [128, 32768 // 4])
    ps = [nc.alloc_psum_tensor("ps%d" % i, [128, 512], F32) for i in range(8)]

    def view(t, byte_off, shape, dt):
        esz = 2 if dt == BF16 else 4
        n = int(np.prod(shape[1:]))
        a = t[:, byte_off // 4:(byte_off + n * esz + 3) // 4]
        if dt == BF16:
            a = a.bitcast(BF16)[:, 0:n]
        if len(shape) == 3:
            a = a.rearrange("p (a b) -> p a b", a=shape[1])
        elif len(shape) == 4:
            a = a.rearrange("p (a b c) -> p a b c", a=shape[1], b=shape[2])
        return a

    zb = view(R1, 0, [128, KC, 512], BF16)
    zsq = view(R1, 8192, [128, KC, 512], BF16)
    st_t1 = view(R1, 16384, [128, 512], F32)
    st_rstd = view(R1, 18432, [128, 512], F32)
    st_nmr = view(R1, 20480, [128, 512], F32)
    st_xh = view(R1, 22528, [128, 512], F32)
    hT = view(R2, 0, [128, FC, NT], BF16)
    wdb = [view(R4, i * 5632, [128, FC, 128], BF16) for i in range(2)]
    sg_t = [view(R5, i * 1024, [128, 512], BF16) for i in range(2)]
    cc_sb = view(R5, 4096, [17 if False else 128, D], F32)
    scT = view(R5, 4096 + 4096, [128, KC, 17], BF16)

    qkT = view(R2, 0, [128, 8, NT], BF16)
    vT = view(R2, 17408, [128, 4, NT], BF16)
    mix = view(R2, 26112, [128, 8, NT], BF16)
    kdm = view(R2, 43520, [128, NB, 128], BF16)
    wout = view(R4, 0, [128, KC, D], BF16)
    pc = [view(R5, i * 4112, [128, 3 + NTP], F32) for i in range(2)]
    cv = view(R5, 8224, [128, NT], F32)
    sq = view(R5, 12576, [128, NT], BF16)
    scc = view(R5, 14752, [128, NT], F32)
    rinvp = scc
    cv2 = view(R1, 0, [128, NT], F32)
    sq2 = view(R1, 4352, [128, NT], BF16)
    rinv2 = view(R1, 6528, [128, NT], F32)
    zc = view(R5, 19104, [128, 2 + NTP], F32)
    pcs = view(R5, 23216, [128, 12, 112], F32)
    zcs = view(R5, 28592, [128, 4, 96], F32)
    class _NS:
        pass

    def gdn_set(si):
        def vw(off, shape, dt):
            if si == 0:
                return view(R5, off, shape, dt)
            if off < 24064:
                return view(R1, off, shape, dt)
            return view(R5, 29184 + off - 24064, shape, dt)
        t = _NS()
        t.si = si
        t.ab_sb = vw(0, [128, 8], F32)
        t.g4 = vw(32, [128, 4], F32)
        t.beta4 = vw(48, [128, 4], F32)
        t.gcc = vw(64, [128, 8], F32)
        t.egc = vw(96, [128, 4], F32)
        t.kbgs = vw(112, [128, 4], F32)
        t.kdcs = vw(128, [128, 4], F32)
        t.glast = vw(256, [128, 4, 16], F32)
        t.glast2 = vw(256, [128, 64], F32)
        t.gbc = vw(512, [128, 4, 128], F32)
        t.bbc = vw(2560, [128, 4, 128], BF16)
        t.egr = vw(4608, [128, 4, 128], F32)
        t.dec = vw(6656, [128, 4, 128], F32)
        t.dnb = vw(8704, [128, 4, 128], F32)
        t.YT = [vw(10752 + i * 2048, [128, 4, 2, 128], BF16) for i in range(2)]
        t.XX = [vw(14848 + i * 1024, [128, 4, 128], BF16) for i in range(2)]
        t.qkm = vw(16896, [128, 4, 128], BF16)
        t.kbg = vw(17920, [128, 4, 128], BF16)
        t.kdec = vw(18944, [128, 4, 128], BF16)
        t.vb = vw(19968, [128, 4, 128], BF16)
        t.nwT = vw(20992, [128, 4, 128], BF16)
        t.qdT = vw(22016, [128, 4, 128], BF16)
        t.vnew = vw(23040, [128, 4, 128], BF16)
        t.osq = vw(24064, [128, 4, 128], BF16)
        t.rinv = vw(25088, [128, 4, 128], F32)
        return t

    TS = [gdn_set(0), gdn_set(1)]
    S0 = [view(R1, 0, [128, NB, 128], F32), view(R1, 16384, [128, NB, 128], F32)]
    S0b = view(R1, 8192, [128, NB, 128], BF16)
    nwTm = view(R1, 12288, [128, NB, NS], BF16)
    qdTm = view(R1, 14336, [128, NB, NS], BF16)
    ones1 = sb("ones1", [128, 128], BF16)
    lnc = sb("lnc", [128, 2])

    def C(name):
        o, n = _CSTCOLS[name]
        return cst[:, o:o + n]

    def dma(q, out, in_, reads, writes, chan):
        return S.add(q, lambda e: e.dma_start(out=out, in_=in_), reads=reads, writes=writes, dma=True, chan=chan)

    def mm(out, lhsT, rhs, start, stop, reads, writes):
        return S.add("pe", lambda e: e.matmul(out, lhsT=lhsT, rhs=rhs, start=start, stop=stop), reads=reads, writes=writes)

    def tr(out, in_, ident, reads, writes):
        return S.add("pe", lambda e: e.transpose(out, in_, ident), reads=reads, writes=writes)

    def act(out, in_, func, reads, writes, bias=None, scale=None, eng="act"):
        kw = {}
        if bias is not None:
            kw["bias"] = bias
        if scale is not None:
            kw["scale"] = scale
        return S.add(eng, lambda e: e.activation(out=out, in_=in_, func=func, **kw), reads=reads, writes=writes)

    def tt(eng, out, in0, in1, op, reads, writes):
        return S.add(eng, lambda e: e.tensor_tensor(out=out, in0=in0, in1=in1, op=op), reads=reads, writes=writes)

    def stt(eng, out, in0, scalar, in1, op0, op1, reads, writes):
        return S.add(eng, lambda e: e.scalar_tensor_tensor(out=out, in0=in0, scalar=scalar, in1=in1, op0=op0, op1=op1),
                     reads=reads, writes=writes)

    def ts(eng, out, in0, s1, s2, op0, op1, reads, writes):
        if s2 is None:
            return S.add(eng, lambda e: e.tensor_scalar(out=out, in0=in0, scalar1=s1, scalar2=None, op0=op0), reads=reads, writes=writes)
        return S.add(eng, lambda e: e.tensor_scalar(out=out, in0=in0, scalar1=s1, scalar2=s2, op0=op0, op1=op1), reads=reads, writes=writes)

    def cp(eng, out, in_, reads, writes):
        return S.add(eng, lambda e: e.tensor_copy(out=out, in_=in_), reads=reads, writes=writes)

    def memset(eng, ap, val, writes):
        return S.add(eng, lambda e: e.memset(ap, val), writes=writes)

    def pk(b, q0=0, q1=4):
        return [("ps", b)]

    def dump(name, src_ap, shape, reads, dt=F32):
        if name not in debug:
            return
        d = dout("dbg_" + name, shape, dt)
        dbg_outs[name] = d
        dma("sp", d, src_ap, reads=reads, writes=[("dbg", name)], chan=("dbg", name))
        out_keys.append(("dbg", name))

    def debug_dump(name, src_ap, shape, reads, dt=F32):
        d = dout("dbg_" + name, shape, dt)
        dbg_outs[name] = d
        dma("sp", d, src_ap, reads=reads, writes=[("dbg", name)], chan=("dbg", name))
        out_keys.append(("dbg", name))

    def bc_s(tile17, c):
        return tile17[:, c, 1:17].unsqueeze(1).to_broadcast([128, 4, 16])

    def v3(ap):
        return ap.rearrange("p (t b) -> p t b", t=4)

    dma("sp", cc_sb[0:17, :], cc_d, [], ["cc_sb"], "cc")
    dma("sp", cst[:], cst_d, [], ["cst"], "cst")
    dma("sp", bada[:], bada_d, [], ["bada"], "small")
    dma("sp", lng[:], lng_d, [], ["lng"], "small")
    dma("sp", lnb[:], lnb_d, [], ["lnb"], "small")
    dma("sp", cqw[:], cqw_d, [], ["cqw"], "small")
    dma("sp", cmw[:], cmw_d, [], ["cmw"], "small")
    dma("sp", dng[:], dng_d, [], ["dng"], "small")
    dma("sp", alog[:], alog_d.partition_broadcast(128), [], ["alog"], "small")
    dma("sp", dtb[:], dtb_d.partition_broadcast(128), [], ["dtb"], "small")
    memset("dve", onesD[:], 1.0 / D, ["onesD"])
    memset("dve", ones128[:], 1.0 / 128, ["ones128"])
    memset("dve", epst[:, 0:1], LN_EPS / (ALPHA * ALPHA), ["epst"])
    memset("dve", epst[:, 1:2], RMS_EPS, ["epst"])
    memset("dve", epst[:, 2:3], 1.0, ["epst"])
    memset("dve", epst[:, 3:4], 0.0, ["epst"])
    memset("dve", ones1[:], 1.0, ["ones1"])
    memset("dve", lnc[:, 0:1], 128.0 * RMS_EPS, ["lnc"])
    memset("dve", lnc[:, 1:2], RMS_EPS, ["lnc"])
    cp("dve", identb[:], C("ident"), ["cst"], ["identb"])
    memset("dve", Sst[:], 0.0, [("Sst", h_) for h_ in range(4)])
    memset("dve", Sbf[:], 0.0, [("Sbf", h_) for h_ in range(4)])
    memset("dve", halo_q[:], 0.0, ["halo_q"])
    memset("dve", halo_m[:], 0.0, ["halo_m"])

    act(cc_sb[0:17, :], cc_sb[0:17, :], AF.Silu, ["cc_sb"], ["cc_sb"])
    for k in range(KC):
        tr(ps[7][:, k * 17:(k + 1) * 17], cc_sb[0:17, k * 128:(k + 1) * 128], C("ident")[0:17, 0:17],
           ["cc_sb", "cst"], pk(7, 0, 2))
    cp("dve", scT[:], ps[7][:, 0:KC * 17].rearrange("p (a b) -> p a b", a=KC), pk(7, 0, 2), ["scT"])
    ring = [0]

    def ring_next():
        i = ring[0] % 4
        ring[0] += 1
        return wblk[i], ("wblk", i)

    pref = {}

    def prefetch(tag, src):
        wb, kw = ring_next()
        dma("pool", wb[:], src, [], [kw], kw)
        pref[tag] = (wb, kw)

    def get_block(tag, src):
        if tag in pref:
            return pref.pop(tag)
        wb, kw = ring_next()
        dma("pool", wb[:], src, [], [kw], kw)
        return wb, kw

    def modv(i, j):
        return modall[:, (i * 3 + j) * 8:(i * 3 + j + 1) * 8, :]

    def ada_block(blk):
        wb, kw = ring_next()
        dma("pool", wb[:], wada_d[blk], [], [kw], kw)
        for cpos in range(2):
            fc = blk * 2 + cpos
            grp = fc // 8
            bank = 6 + grp % 2
            col = (fc % 8) * 17
            for k in range(KC):
                mm(ps[bank][:, col:col + 17], wb[:, k, cpos * 128:(cpos + 1) * 128], scT[:, k, :], k == 0, k == KC - 1,
                   [kw, "scT"], pk(bank))
            if fc % 8 == 7:
                tt("dve", modall[:, grp * 8:(grp + 1) * 8, :],
                   ps[bank][:, 0:8 * 17].rearrange("p (a b) -> p a b", a=8),
                   bada[:, grp * 8:(grp + 1) * 8].unsqueeze(2).to_broadcast([128, 8, 17]), ALU.add,
                   pk(bank) + ["bada"], [("mod", grp)])

    def ada_derive(i):
        if i == 0:
            ts("dve", sc1p0[:], modv(0, 1), 1.0, None, ALU.add, None, [("mod", 1)], ["sc1p0"])
        else:
            G, B = (G1, B1) if i == 1 else (G2, B2)
            gb = lng[:, (i - 1) * 8:i * 8].unsqueeze(2).to_broadcast([128, KC, 17])
            bb = lnb[:, (i - 1) * 8:i * 8].unsqueeze(2).to_broadcast([128, KC, 17])
            ts("dve", G[:], modv(i, 1), 1.0, None, ALU.add, None, [("mod", i * 3 + 1)], [("GB", i)])
            tt("dve", B[:], G[:], bb, ALU.mult, [("GB", i), "lnb"], [("GB", i)])
            tt("dve", B[:], B[:], modv(i, 0), ALU.add, [("GB", i), ("mod", i * 3)], [("GB", i)])
            tt("dve", G[:], G[:], gb, ALU.mult, [("GB", i), "lng"], [("GB", i)])

    def ada_gate(i):
        f = (0.5 if i != 1 else 1.0) / ALPHA
        ts("dve", gsc[i][:], modv(i, 2), f, None, ALU.mult, None, [("mod", i * 3 + 2)], [("gsc", i)])

    ada_state = {"next": 0}

    def ada_more(n):
        for _ in range(n):
            if ada_state["next"] < 36:
                ada_block(ada_state["next"])
                ada_state["next"] += 1

    ada_more(8)
    ada_derive(0)
    dma("pool", wab[:], wab_d, [], ["wab"], "wab")
    dma("pool", selb[:], sel_d.rearrange("p (a b) -> p a b", a=NB), [], ["selb"], "wab")
    act(nea[:], alog[:], AF.Exp, ["alog"], ["nea"])
    ts("dve", nea[:], nea[:], -1.0, None, ALU.mult, None, ["nea"], ["nea"])
    if stop_after == "ada":
        ada_more(36)
        for i in range(3):
            ada_gate(i)
        ada_derive(1)
        ada_derive(2)
    dump("modall", modall[:], [128, 72, 17], [("mod", g_) for g_ in range(9)])

    def groups(st):
        g = [("A", 0, 512), ("B", 512, 1024)]
        if st == 0:
            g.append(("S", 1024, 1088))
        return g

    def load_x(st):
        for gi_, (g, lo, hi) in enumerate((("A", 0, 512), ("B", 512, 1024))):
            dma("sp", xT[:, :, lo:hi], xp_d[:, :, st * NTP + lo:st * NTP + hi], [],
                [("xT", c, g) for c in range(KC)], ("xload", gi_))
        if st == 0:
            dma("sp", xT[:, :, NTP:NT], xs_d, [], [("xT", c, "S") for c in range(KC)], ("xload", 2))

    def modulate0(st):
        for (g, lo, hi) in groups(st):
            for c in range(KC):
                if g != "S":
                    act(uT[:, c, lo:hi], xT[:, c, lo:hi], AF.Identity, [("xT", c, g), "sc1p0", ("mod", 0)], [("uT", c, g)],
                        bias=modv(0, 0)[:, c, 0:1], scale=sc1p0[:, c, 0:1])
                else:
                    tt("dve", v3(st_xh[:, 0:64]), v3(xT[:, c, lo:hi]), bc_s(sc1p0, c), ALU.mult,
                       [("xT", c, g), "sc1p0"], ["st_xh"])
                    tt("dve", v3(uT[:, c, lo:hi]), v3(st_xh[:, 0:64]), bc_s(modv(0, 0), c), ALU.add,
                       ["st_xh", ("mod", 0)], [("uT", c, g)])

    def ffn(st, w, gi):
        grp = groups(st)
        for fb in range(11):
            bg, kg = get_block(("wg", st, w, fb), wg_d[w][fb])
            bu, ku = get_block(("wu", st, w, fb), wu_d[w][fb])

            for (g, lo, hi) in grp:
                for fp in range(2):
                    f = fb * 2 + fp
                    if g != "S":
                        par = f % 2
                        pg, pu = ps[par * 2], ps[par * 2 + 1]
                        og, ou = pg[:, :], pu[:, :]
                        kpg, kpu = pk(par * 2), pk(par * 2 + 1)
                    else:
                        par = f % 2
                        og = ps[4 + par][:, 0:64]
                        ou = ps[4 + par][:, 64:128]
                        kpg = kpu = pk(4 + par)
                    for k in range(KC):
                        mm(og, bg[:, k, fp * 128:(fp + 1) * 128], uT[:, k, lo:hi], k == 0, k == KC - 1,
                           [kg, ("uT", k, g)], kpg)
                    for k in range(KC):
                        mm(ou, bu[:, k, fp * 128:(fp + 1) * 128], uT[:, k, lo:hi], k == 0, k == KC - 1,
                           [ku, ("uT", k, g)], kpu)
                    w_ = hi - lo
                    sgt = sg_t[par]
                    act(sgt[:, 0:w_], og, AF.Silu, kpg, [("sg", par)])
                    tt("dve", hT[:, f, lo:hi], sgt[:, 0:w_], ou, ALU.mult, [("sg", par)] + kpu, [("hT", f, g)])
            if st == 0 and w == 0:
                ada_more(2 if fb < 8 else 1)
        if st == 0 and w == 0:
            ada_more(1)
            ada_gate(0)
            ada_derive(1)
        if stop_after == "up":
            return
        for m in range(KC):
            wb = wdb[m % 2]
            kw = ("wdb", m % 2)
            dma("pool", wb[:], wd_d[w][m], [], [kw], kw)
            if st == 0 and w == 0:
                ada_more(2)
                if m == KC - 1:
                    ada_more(36)
                    ada_gate(1)
                    ada_gate(2)
                    ada_derive(2)
            for (g, lo, hi) in grp:
                if g != "S":
                    bank = (m % 2) * 2 + (0 if g == "A" else 1)
                    o = ps[bank][:, :]
                    kp = pk(bank)
                else:
                    o = ps[4 + m % 2][:, 0:64]
                    kp = pk(4 + m % 2)
                for f in range(FC):
                    mm(o, wb[:, f, :], hT[:, f, lo:hi], f == 0, f == FC - 1, [kw, ("hT", f, g)], kp)
                if g != "S":
                    stt("dve", xT[:, m, lo:hi], o, gsc[gi][:, m, 0:1], xT[:, m, lo:hi], ALU.mult, ALU.add,
                        kp + [("gsc", gi), ("xT", m, g)], [("xT", m, g)])
                else:
                    tt("dve", v3(st_xh[:, 0:64]), v3(o), bc_s(gsc[gi], m), ALU.mult, kp + [("gsc", gi)], ["st_xh"])
                    tt("dve", xT[:, m, lo:hi], st_xh[:, 0:64], xT[:, m, lo:hi], ALU.add, ["st_xh", ("xT", m, g)], [("xT", m, g)])

    lnset = [dict(t1=st_t1, rstd=st_rstd, nmr=st_nmr, xh=st_xh, k="0"),
             dict(t1=view(R5, 4096, [128, 512], F32), rstd=view(R5, 6144, [128, 512], F32),
                  nmr=view(R5, 8192, [128, 512], F32), xh=view(R5, 10240, [128, 512], F32), k="1")]

    def ln_pre(g, lo, hi, ss_):
        w_ = hi - lo
        t1, rstd, nmr, kk = ss_["t1"], ss_["rstd"], ss_["nmr"], ss_["k"]
        for c in range(KC):
            act(zb[:, c, 0:w_], xT[:, c, lo:hi], AF.Copy, [("xT", c, g)], [("zb", c)])
            tt("pool", zsq[:, c, 0:w_], xT[:, c, lo:hi], xT[:, c, lo:hi], ALU.mult, [("xT", c, g)], [("zsq", c)])
            if c % 2 == 1:
                yield
        for c in range(KC):
            mm(ps[6][:, 0:w_], onesD[:], zb[:, c, 0:w_], c == 0, c == KC - 1, ["onesD", ("zb", c)], pk(6))
        for c in range(KC):
            mm(ps[7][:, 0:w_], onesD[:], zsq[:, c, 0:w_], c == 0, c == KC - 1, ["onesD", ("zsq", c)], pk(7))
        yield
        cp("dve", nmr[:, 0:w_], ps[6][:, 0:w_], pk(6), ["nmr" + kk])
        tt("dve", t1[:, 0:w_], nmr[:, 0:w_], nmr[:, 0:w_], ALU.mult, ["nmr" + kk], ["t1" + kk])
        tt("dve", t1[:, 0:w_], ps[7][:, 0:w_], t1[:, 0:w_], ALU.subtract, pk(7) + ["t1" + kk], ["t1" + kk])
        yield
        act(rstd[:, 0:w_], t1[:, 0:w_], AF.Ln, ["t1" + kk, "epst"], ["rstd" + kk], bias=epst[:, 0:1], scale=1.0)
        act(rstd[:, 0:w_], rstd[:, 0:w_], AF.Exp, ["rstd" + kk], ["rstd" + kk], scale=-0.5)
        yield
        stt("dve", nmr[:, 0:w_], nmr[:, 0:w_], -1.0, rstd[:, 0:w_], ALU.mult, ALU.mult, ["nmr" + kk, "rstd" + kk], ["nmr" + kk])

    def ln_loop(g, lo, hi, ss_, li, G, B, gbkey):
        w_ = hi - lo
        t1, rstd, nmr, kk = ss_["t1"], ss_["rstd"], ss_["nmr"], ss_["k"]
        for c in range(KC):
            xh = ss_["xh"] if c % 2 == 0 else t1
            kxh = ("xh" + kk) if c % 2 == 0 else ("t1" + kk)
            tt("dve", xh[:, 0:w_], xT[:, c, lo:hi], rstd[:, 0:w_], ALU.mult, [("xT", c, g), "rstd" + kk], [kxh])
            tt("dve", xh[:, 0:w_], xh[:, 0:w_], nmr[:, 0:w_], ALU.add, [kxh, "nmr" + kk], [kxh])
            act(xT[:, c, lo:hi], xh[:, 0:w_], AF.Identity, [kxh, "lng", "lnb"], [("xT", c, g)],
                bias=lnb[:, li * 8 + c:li * 8 + c + 1], scale=lng[:, li * 8 + c:li * 8 + c + 1])
            if G is not None:
                if g != "S":
                    act(uT[:, c, lo:hi], xh[:, 0:w_], AF.Identity, [kxh, gbkey], [("uT", c, g)],
                        bias=B[:, c, 0:1], scale=G[:, c, 0:1])
                else:
                    tt("dve", v3(xh[:, 0:64]), v3(xh[:, 0:64]), bc_s(G, c), ALU.mult, [kxh, gbkey], [kxh])
                    tt("dve", v3(uT[:, c, lo:hi]), v3(xh[:, 0:64]), bc_s(B, c), ALU.add, [kxh, gbkey], [("uT", c, g)])
            yield

    def layernorm(st, li, G, B, gbkey):
        grp = groups(st)
        lockstep([ln_pre(*grp[0], lnset[0])])
        for i, gg in enumerate(grp):
            gens = [ln_loop(*gg, lnset[i % 2], li, G, B, gbkey)]
            if i + 1 < len(grp):
                gens.append(ln_pre(*grp[i + 1], lnset[(i + 1) % 2]))
            lockstep(gens)

    def mixer_proj(st):
        grp = groups(st)
        NTx = NT if st == 0 else NTP
        if st == 0:
            dma("sp", pcs[:, :, 0:48], cq_d, [], ["pcs"], "cstate")
            dma("sp", zcs[:, :, 0:32], cm_d, [], ["zcs"], "cstate")
        pbank = {"A": 0, "B": 1}
        deferred = []
        for blk in range(14):
            wb, kw = get_block(("win", st, blk), win_d[blk])
            for cpos in range(2):
                ci = blk * 2 + cpos
                par = ci % 2
                pkeys = {}
                prev_deferred, deferred = deferred, []
                for (g, lo, hi) in grp:
                    if g != "S":
                        bank = par * 2 + pbank[g]
                        o = ps[bank][:, :]
                    else:
                        bank = 4
                        o = ps[4][:, 0:64]
                    pkeys[g] = (o, pk(bank))
                    for k in range(KC):
                        mm(o, wb[:, k, cpos * 128:(cpos + 1) * 128], uT[:, k, lo:hi], k == 0, k == KC - 1,
                           [kw, ("uT", k, g)], pk(bank))
                if ci < 12:
                    j = ci
                    cvj, kcv = (cv, ("cv", 0)) if j % 2 == 0 else (cv2, ("cv", 1))
                    sqj, ksq = (sq, ("sq", 0)) if j % 2 == 0 else (sq2, ("sq", 1))
                    rvj, krv = (rinvp, "scc") if j % 2 == 0 else (rinv2, "rinv2")
                    p_ = pc[j % 2]
                    kpc = ("pc", j % 2)
                    cp("pool", p_[:, 0:3], halo_q[:, j, :], ["halo_q"], [kpc])
                    for (g, lo, hi) in grp:
                        o, kp = pkeys[g]
                        if g != "S":
                            act(p_[:, 3 + lo:3 + hi], o, AF.Copy, kp, [kpc])
                        else:
                            act(pcs[:, j, 48:112], o, AF.Copy, kp, ["pcs"])
                    cp("pool", halo_q[:, j, :], p_[:, NTP:NTP + 3], [kpc], ["halo_q"])
                    for t in range(4):
                        wsc = cqw[:, t, j:j + 1]
                        if t == 0:
                            ts("dve", cvj[:, 0:NTP], p_[:, 0:NTP], wsc, None, ALU.mult, None, [kpc, "cqw"], [kcv])
                        else:
                            stt("dve", cvj[:, 0:NTP], p_[:, t:t + NTP], wsc, cvj[:, 0:NTP], ALU.mult, ALU.add, [kpc, "cqw", kcv], [kcv])
                    if st == 0:
                        for t in range(4):
                            wsc = cqw[:, t, j:j + 1]
                            if t == 0:
                                ts("dve", cvj[:, NTP:NT], pcs[:, j, 0:64], wsc, None, ALU.mult, None, ["pcs", "cqw"], [kcv])
                            else:
                                stt("dve", cvj[:, NTP:NT], pcs[:, j, 16 * t:16 * t + 64], wsc, cvj[:, NTP:NT], ALU.mult, ALU.add,
                                    ["pcs", "cqw", kcv], [kcv])
                    def part2(j=j, cvj=cvj, kcv=kcv, sqj=sqj, ksq=ksq, rvj=rvj, krv=krv):
                        if j >= 8:
                            act(vT[:, j - 8, 0:NTx], cvj[:, 0:NTx], AF.Silu, [kcv], [("vT", j - 8)])
                            return
                        act(cvj[:, 0:NTx], cvj[:, 0:NTx], AF.Silu, [kcv], [kcv])
                        act(sqj[:, 0:NTx], cvj[:, 0:NTx], AF.Square, [kcv], [ksq])
                        isq = j < 4
                        for gi_, (g, lo, hi) in enumerate(grp):
                            bank = 5 + gi_
                            w_ = hi - lo
                            mm(ps[bank][:, 0:w_], ones1[:], sqj[:, lo:hi], True, True, ["ones1", ksq], pk(bank))
                            act(rvj[:, lo:hi], ps[bank][:, 0:w_], AF.Ln, pk(bank) + ["lnc"], [krv],
                                bias=lnc[:, 0:1] if isq else lnc[:, 1:2], scale=128.0 if isq else 1.0)
                        act(rvj[:, 0:NTx], rvj[:, 0:NTx], AF.Exp, [krv], [krv], scale=-0.5)
                        tt("dve", qkT[:, j, 0:NTx], cvj[:, 0:NTx], rvj[:, 0:NTx], ALU.mult, [kcv, krv], [("qkT", j)])
                    deferred.append(part2)
                elif ci < 16:
                    j = ci - 12
                    for (g, lo, hi) in grp:
                        o, kp = pkeys[g]
                        act(mix[:, j, lo:hi], o, AF.Silu, kp, [("mix", j)])
                elif ci < 20:
                    j = ci - 16
                    for (g, lo, hi) in grp:
                        o, kp = pkeys[g]
                        act(mix[:, 4 + j, lo:hi], o, AF.Copy, kp, [("mix", 4 + j)])
                else:
                    j, is_h = (ci - 20) // 2, (ci - 20) % 2
                    if not is_h:
                        for (g, lo, hi) in grp:
                            o, kp = pkeys[g]
                            act(scc[:, lo:hi], o, AF.Copy, kp, ["scc"])
                    else:
                        cp("pool", zc[:, 0:2], halo_m[:, j, :], ["halo_m"], ["zc"])
                        for (g, lo, hi) in grp:
                            o, kp = pkeys[g]
                            if g != "S":
                                tt("dve", zc[:, 2 + lo:2 + hi], scc[:, lo:hi], o, ALU.mult, ["scc"] + kp, ["zc"])
                            else:
                                tt("dve", zcs[:, j, 32:96], scc[:, lo:hi], o, ALU.mult, ["scc"] + kp, ["zcs"])
                        cp("pool", halo_m[:, j, :], zc[:, NTP:NTP + 2], ["zc"], ["halo_m"])
                        for t in range(3):
                            wsc = cmw[:, t, j:j + 1]
                            if t == 0:
                                ts("dve", cv[:, 0:NTP], zc[:, 0:NTP], wsc, None, ALU.mult, None, ["zc", "cmw"], [("cv", 0)])
                            else:
                                stt("dve", cv[:, 0:NTP], zc[:, t:t + NTP], wsc, cv[:, 0:NTP], ALU.mult, ALU.add, ["zc", "cmw", ("cv", 0)], [("cv", 0)])
                        if st == 0:
                            for t in range(3):
                                wsc = cmw[:, t, j:j + 1]
                                if t == 0:
                                    ts("dve", cv[:, NTP:NT], zcs[:, j, 0:64], wsc, None, ALU.mult, None, ["zcs", "cmw"], [("cv", 0)])
                                else:
                                    stt("dve", cv[:, NTP:NT], zcs[:, j, 16 * t:16 * t + 64], wsc, cv[:, NTP:NT], ALU.mult, ALU.add,
                                        ["zcs", "cmw", ("cv", 0)], [("cv", 0)])
                        tt("dve", mix[:, 4 + j, 0:NTx], mix[:, 4 + j, 0:NTx], cv[:, 0:NTx], ALU.mult, [("mix", 4 + j), ("cv", 0)], [("mix", 4 + j)])
                for fn_ in prev_deferred:
                    fn_()
        for fn_ in deferred:
            fn_()
        if st == 0:
            dma("sp", cqs_d, pcs[:, :, 64:112], ["pcs"], ["cq_s"], "cq_s")
            dma("sp", cms_d, zcs[:, :, 64:96], ["zcs"], ["cm_s"], "cm_s")
            out_keys.extend(["cq_s", "cm_s"])
        if st == 1:
            dma("sp", cqp_d, halo_q[:], ["halo_q"], ["cq_p"], "cq_p")
            dma("sp", cmp_d, halo_m[:], ["halo_m"], ["cm_p"], "cm_p")
            out_keys.extend(["cq_p", "cm_p"])

    def gdn_intra(C_, c0, tb, nfull, t):
        si = t.si
        P = slice(0, C_)
        tri, seqt, maskT, nstt = C("tri_" + tb), C("seq_" + tb), C("maskT_" + tb), C("nst_" + tb)
        ident = C("ident")

        def Bk(b):
            return ps[(b + 4 * si) % 8]

        def bk(b):
            return pk((b + 4 * si) % 8)

        def K(n, *a):
            return (n, si) + a

        def h3(x):
            return x.rearrange("p (h c) -> p h c", h=4)[:, :, 0:C_]
        ab_sb, g4, beta4, gcc, egc, kbgs, kdcs = t.ab_sb, t.g4, t.beta4, t.gcc, t.egc, t.kbgs, t.kdcs
        gbc, bbc, egr, dec, dnb, YT, XX = t.gbc, t.bbc, t.egr, t.dec, t.dnb, t.YT, t.XX
        for k in range(KC):
            mm(Bk(0)[P, 0:8], uT[:, k, c0:c0 + C_], wab[:, k, :], k == 0, k == KC - 1, [("uT", k, "A"), ("uT", k, "B"), ("uT", k, "S"), "wab"], bk(0))
        yield
        act(ab_sb[P, :], Bk(0)[P, 0:8], AF.Copy, bk(0), [K("ab_sb")])
        tt("dve", g4[P, :], ab_sb[P, 0:4], dtb[P, :], ALU.add, [K("ab_sb"), "dtb"], [K("g4")])
        act(g4[P, :], g4[P, :], AF.Exp, [K("g4")], [K("g4")])
        act(g4[P, :], g4[P, :], AF.Ln, [K("g4"), "epst"], [K("g4")], bias=epst[P, 2:3], scale=1.0)
        tt("dve", g4[P, :], g4[P, :], nea[P, :], ALU.mult, [K("g4"), "nea"], [K("g4")])
        act(beta4[P, :], ab_sb[P, 4:8], AF.Exp, [K("ab_sb")], [K("beta4")], scale=-1.0)
        ts("dve", beta4[P, :], beta4[P, :], 1.0, None, ALU.add, None, [K("beta4")], [K("beta4")])
        S.add("dve", lambda e: e.reciprocal(out=beta4[P, :], in_=beta4[P, :]), reads=[K("beta4")], writes=[K("beta4")])
        cp("dve", gbc[P, :, :], g4[P, :].unsqueeze(2).to_broadcast([C_, 4, 128]), [K("g4")], [K("gbc")])
        cp("dve", bbc[P, :, :], beta4[P, :].unsqueeze(2).to_broadcast([C_, 4, 128]), [K("beta4")], [K("bbc")])
        yield
        mm(Bk(1)[P, 0:4], tri[P, P], g4[P, :], True, True, ["cst", K("g4")], bk(1))
        mm(Bk(1)[P, 4:8], seqt[P, P], g4[P, :], True, True, ["cst", K("g4")], bk(1))
        for h in range(4):
            mm(Bk(2)[:, h * 128:h * 128 + C_], gbc[P, h, :], tri[P, P], True, True, [K("gbc"), "cst"], bk(2))
            mm(Bk(3)[:, h * 128:h * 128 + C_], bbc[P, h, :], identb[P, P], True, True, [K("bbc"), "identb"], bk(3))
            mm(Bk(1)[:, 64 + h * 16:64 + h * 16 + 16], gbc[P, h, :], seqt[P, 0:16], True, True, [K("gbc"), "cst"], bk(1))
        yield
        cp("dve", gcc[P, :], Bk(1)[P, 0:8], bk(1), [K("gcc")])
        act(t.glast2[:, :], Bk(1)[:, 64:128], AF.Copy, bk(1), [K("glast")])
        act(t.glast2[:, :], t.glast2[:, :], AF.Exp, [K("glast")], [K("glast")])
        act(egr[:, :, 0:C_], h3(Bk(2)[:, :]), AF.Copy, bk(2), [K("egr")])
        act(egr[:, :, 0:C_], egr[:, :, 0:C_], AF.Exp, [K("egr")], [K("egr")])
        act(egc[P, :], gcc[P, 0:4], AF.Exp, [K("gcc")], [K("egc")])
        tt("dve", kbgs[P, :], beta4[P, :], egc[P, :], ALU.mult, [K("beta4"), K("egc")], [K("kbgs")])
        tt("dve", kdcs[P, :], gcc[P, 4:8], gcc[P, 0:4], ALU.subtract, [K("gcc")], [K("kdcs")])
        act(kdcs[P, :], kdcs[P, :], AF.Exp, [K("kdcs")], [K("kdcs")])
        for h in range(4):
            stt("dve", dec[P, h, 0:C_], Bk(2)[P, h * 128:h * 128 + C_], gcc[P, h:h + 1], maskT[P, P], ALU.subtract, ALU.add,
                bk(2) + [K("gcc"), "cst"], [K("dec")])
        act(dec[P, :, 0:C_], dec[P, :, 0:C_], AF.Exp, [K("dec")], [K("dec")])
        yield
        for h in range(4):
            kh = qkT[:, 4 + h, c0:c0 + C_]
            qh = qkT[:, h, c0:c0 + C_]
            mm(Bk(4)[P, h * 128:h * 128 + C_], kh, kh, True, True, [("qkT", 4 + h)], bk(4))
            mm(Bk(5)[P, h * 128:h * 128 + C_], kh, qh, True, True, [("qkT", 4 + h), ("qkT", h)], bk(5))
        psb7 = Bk(2)[:, :].bitcast(BF16)
        for h in range(4):
            tr(psb7[P, h * 128:(h + 1) * 128], qkT[:, 4 + h, c0:c0 + C_], identb[:, :], [("qkT", 4 + h), "identb"], bk(2))
            tr(psb7[P, 512 + h * 128:512 + (h + 1) * 128], vT[:, h, c0:c0 + C_], identb[:, :], [("vT", h), "identb"], bk(2))
        yield
        tt("dve", dnb[P, :, 0:C_], dec[P, :, 0:C_], nstt[P, P].unsqueeze(1).to_broadcast([C_, 4, C_]), ALU.mult, [K("dec"), "cst"], [K("dnb")])
        tt("dve", dnb[P, :, 0:C_], h3(Bk(3)[P, :]), dnb[P, :, 0:C_], ALU.mult, bk(3) + [K("dnb")], [K("dnb")])
        Y0 = YT[0]
        tt("dve", Y0[P, :, 0, 0:C_], h3(Bk(4)[P, :]), dnb[P, :, 0:C_], ALU.mult, bk(4) + [K("dnb")], [K("YT", 0)])
        tt("dve", t.qkm[P, :, 0:C_], h3(Bk(5)[P, :]), dec[P, :, 0:C_], ALU.mult, bk(5) + [K("dec")], [K("qkm")])
        cp("dve", Y0[P, :, 1, 0:C_], identb[P, P].unsqueeze(1).to_broadcast([C_, 4, C_]), ["identb"], [K("YT", 0)])
        k3 = psb7[P, 0:512].rearrange("p (h d) -> p h d", h=4)
        v3_ = psb7[P, 512:1024].rearrange("p (h d) -> p h d", h=4)
        tt("dve", t.kbg[P, :, :], k3, kbgs[P, :].unsqueeze(2).to_broadcast([C_, 4, 128]), ALU.mult, bk(2) + [K("kbgs")], [K("kbg")])
        tt("dve", t.kdec[P, :, :], k3, kdcs[P, :].unsqueeze(2).to_broadcast([C_, 4, 128]), ALU.mult, bk(2) + [K("kdcs")], [K("kdec")])
        tt("dve", t.vb[P, :, :], v3_, beta4[P, :].unsqueeze(2).to_broadcast([C_, 4, 128]), ALU.mult, bk(2) + [K("beta4")], [K("vb")])
        yield
        psb6 = Bk(6)[:, :].bitcast(BF16)
        for h in range(4):
            tr(psb6[P, h * 128:h * 128 + C_], Y0[P, h, 0, 0:C_], identb[P, P], [K("YT", 0), "identb"], bk(6))
        yield
        cp("dve", XX[0][P, :, 0:C_], psb6[P, 0:512].rearrange("p (h c) -> p h c", h=4)[:, :, 0:C_], bk(6), [K("XX", 0)])
        yield
        cur = 0
        for s_ in range(nfull):
            nxt = 1 - cur
            for h in range(4):
                bank = 0 if h < 2 else 1
                off = (h % 2) * 256
                if C_ == 128:
                    mm(Bk(bank)[P, off:off + 256], XX[cur][P, h, 0:C_], YT[cur][P, h, :, :].rearrange("p t c -> p (t c)"), True, True,
                       [K("XX", cur), K("YT", cur)], bk(bank))
                else:
                    mm(Bk(bank)[P, off:off + C_], XX[cur][P, h, 0:C_], YT[cur][P, h, 0, 0:C_], True, True, [K("XX", cur), K("YT", cur)], bk(bank))
                    mm(Bk(bank)[P, off + 128:off + 128 + C_], XX[cur][P, h, 0:C_], YT[cur][P, h, 1, 0:C_], True, True, [K("XX", cur), K("YT", cur)], bk(bank))
                mm(Bk(2)[P, h * 128:h * 128 + C_], YT[cur][P, h, 0, 0:C_], XX[cur][P, h, 0:C_], True, True, [K("XX", cur), K("YT", cur)], bk(2))
            yield
            for bank in range(2):
                pv = Bk(bank)[P, :].rearrange("p (h t c) -> p h t c", h=2, t=2)
                act(YT[nxt][P, 2 * bank:2 * bank + 2, 0, 0:C_], pv[:, :, 0, 0:C_], AF.Copy, bk(bank), [K("YT", nxt)])
                tt("dve", YT[nxt][P, 2 * bank:2 * bank + 2, 1, 0:C_], pv[:, :, 1, 0:C_], YT[cur][P, 2 * bank:2 * bank + 2, 1, 0:C_], ALU.add,
                   bk(bank) + [K("YT", cur)], [K("YT", nxt)])
            act(XX[nxt][P, :, 0:C_], h3(Bk(2)[P, :]), AF.Copy, bk(2), [K("XX", nxt)])
            yield
            cur = nxt
        nxt = 1 - cur
        for h in range(4):
            mm(Bk(0)[P, h * 128:h * 128 + C_], XX[cur][P, h, 0:C_], YT[cur][P, h, 1, 0:C_], True, True, [K("XX", cur), K("YT", cur)], bk(0))
        yield
        tt("dve", YT[nxt][P, :, 1, 0:C_], h3(Bk(0)[P, :]), YT[cur][P, :, 1, 0:C_], ALU.add, bk(0) + [K("YT", cur)], [K("YT", nxt)])
        Tm = YT[nxt]
        kT_ = K("YT", nxt)
        yield
        for h in range(4):
            mm(Bk(3)[:, h * 128:h * 128 + C_], t.kbg[P, h, :], Tm[P, h, 1, 0:C_], True, True, [K("kbg"), kT_], bk(3))
        yield
        act(t.nwT[:, :, 0:C_], h3(Bk(3)[:, :]), AF.Copy, bk(3), [K("nwT")], scale=-1.0)
        q3 = qkT[:, 0:4, c0:c0 + C_]
        tt("dve", t.qdT[:, :, 0:C_], q3, egr[:, :, 0:C_], ALU.mult, [("qkT", h) for h in range(4)] + [K("egr")], [K("qdT")])
        return Tm, kT_

    def lockstep(gens):
        res = [None] * len(gens)
        live = list(range(len(gens)))
        while live:
            for i in list(live):
                try:
                    next(gens[i])
                except StopIteration as e:
                    res[i] = e.value
                    live.remove(i)
        return res

    def gdn_out(C_, c0, t):
        si = t.si

        def Bk(b):
            return ps[(b + 4 * si) % 8]

        def bk(b):
            return pk((b + 4 * si) % 8)

        def K(n, *a):
            return (n, si) + a

        def h3(x):
            return x.rearrange("p (h c) -> p h c", h=4)[:, :, 0:C_]
        act(t.osq[:, :, 0:C_], h3(Bk(5)[:, :]), AF.Square, bk(5), [K("osq")])
        yield
        for h in range(4):
            mm(Bk(7)[:, h * 128:h * 128 + C_], ones128[:], t.osq[:, h, 0:C_], True, True, ["ones128", K("osq")], bk(7))
        yield
        act(t.rinv[:, :, 0:C_], h3(Bk(7)[:, :]), AF.Ln, bk(7) + ["epst"], [K("rinv")], bias=epst[:, 1:2], scale=1.0)
        act(t.rinv[:, :, 0:C_], t.rinv[:, :, 0:C_], AF.Exp, [K("rinv")], [K("rinv")], scale=-0.5)
        yield
        tt("dve", t.rinv[:, :, 0:C_], h3(Bk(5)[:, :]), t.rinv[:, :, 0:C_], ALU.mult, bk(5) + [K("rinv")], [K("rinv")])
        stt("dve", mix[:, 0:4, c0:c0 + C_], t.rinv[:, :, 0:C_], dng[:, 0:1], mix[:, 0:4, c0:c0 + C_], ALU.mult, ALU.mult,
            [K("rinv"), "dng"] + [("mix", h) for h in range(4)], [("mix", h) for h in range(4)])

    def gdn_prompt_inter(c0, t, Tm, kT_):
        C_ = 128
        P = slice(0, C_)
        si = t.si

        def Bk(b):
            return ps[(b + 4 * si) % 8]

        def bk(b):
            return pk((b + 4 * si) % 8)

        def K(n, *a):
            return (n, si) + a
        for h in range(4):
            mm(Bk(4)[P, h * 128:(h + 1) * 128], Tm[P, h, 1, 0:C_], t.vb[P, h, :], True, False, [kT_, K("vb")], bk(4))
            mm(Bk(4)[P, h * 128:(h + 1) * 128], t.nwT[:, h, 0:C_], Sbf[:, h, :], False, True, [K("nwT"), ("Sbf", h)], bk(4))
        act(t.vnew[P, :, :], Bk(4)[P, :].rearrange("p (h d) -> p h d", h=4), AF.Copy, bk(4), [K("vnew")])
        for h in range(4):
            mm(Bk(5)[:, h * 128:h * 128 + C_], Sbf[:, h, :], t.qdT[:, h, 0:C_], True, False, [("Sbf", h), K("qdT")], bk(5))
            mm(Bk(5)[:, h * 128:h * 128 + C_], t.vnew[P, h, :], t.qkm[P, h, 0:C_], False, True, [K("vnew"), K("qkm")], bk(5))
        for h in range(4):
            mm(Bk(6)[:, h * 128:(h + 1) * 128], t.kdec[P, h, :], t.vnew[P, h, :], True, True, [K("kdec"), K("vnew")], bk(6))
        for h in range(4):
            stt("dve", Sst[:, h, :], Sst[:, h, :], t.glast[:, h, 0:1], Bk(6)[:, h * 128:(h + 1) * 128], ALU.mult, ALU.add,
                [("Sst", h), K("glast")] + bk(6), [("Sst", h)])
            act(Sbf[:, h, :], Sst[:, h, :], AF.Copy, [("Sst", h)], [("Sbf", h)])

    def gdn_prompt_pair(c0a, c0b):
        if "SEQ" in debug:
            r = lockstep([gdn_intra(128, c0a, "p", 6, TS[0])]) + lockstep([gdn_intra(128, c0b, "p", 6, TS[1])])
        else:
            r = lockstep([gdn_intra(128, c0a, "p", 6, TS[0]), gdn_intra(128, c0b, "p", 6, TS[1])])
        if c0a == 0 and "pairdump" in debug and not dbg_outs:
            for si in range(2):
                t = TS[si]
                for nm in ("kbg", "kdec", "vb", "nwT", "qdT", "qkm"):
                    debug_dump(nm + str(si), getattr(t, nm)[:, :, :], [128, 4, 128], [(nm, si)], BF16)
                debug_dump("T" + str(si), r[si][0][:, :, :, :], [128, 4, 2, 128], [r[si][1]], BF16)
                debug_dump("egr" + str(si), t.egr[:, :, :], [128, 4, 128], [("egr", si)], F32)
                debug_dump("dec" + str(si), t.dec[:, :, :], [128, 4, 128], [("dec", si)], F32)
            raise _Stop()
        gdn_prompt_inter(c0a, TS[0], *r[0])
        gdn_prompt_inter(c0b, TS[1], *r[1])
        lockstep([gdn_out(128, c0a, TS[0]), gdn_out(128, c0b, TS[1])])

    def gdn_sample():
        C_ = NS
        c0 = NTP
        P = slice(0, C_)
        t = TS[0]
        (Tm, kT_), = lockstep([gdn_intra(C_, c0, "s", 1, t)])
        K = lambda n, *a: (n, 0) + a
        selP = C("selP")
        for h in range(4):
            s0f = S0[h % 2]
            ks0 = ("S0", h % 2)
            dma("sp", s0f[:, :, :], s0_d[:, :, h, :], [], [ks0], ks0)
            dma("pool", S0b[:, :, :], s0_d[:, :, h, :], [], ["S0b"], "S0b")
            tt("dve", nwTm[:, :, :], t.nwT[:, h, 0:C_].unsqueeze(1).to_broadcast([128, NB, C_]), selb[:, :, :], ALU.mult, [K("nwT"), "selb"], ["nwTm"])
            tt("dve", qdTm[:, :, :], t.qdT[:, h, 0:C_].unsqueeze(1).to_broadcast([128, NB, C_]), selb[:, :, :], ALU.mult, [K("qdT"), "selb"], ["qdTm"])
            tt("dve", kdm[P, :, :], t.kdec[P, h, :].unsqueeze(1).to_broadcast([C_, NB, 128]),
               selP[P, 0:NB].unsqueeze(2).to_broadcast([C_, NB, 128]), ALU.mult, [K("kdec"), "cst"], ["kdm"])
            mm(ps[4][P, 0:128], Tm[P, h, 1, 0:C_], t.vb[P, h, :], True, False, [kT_, K("vb")], pk(4))
            for b in range(NB):
                mm(ps[4][P, 0:128], nwTm[:, b, :], S0b[:, b, :], False, b == NB - 1, ["nwTm", "S0b"], pk(4))
            act(t.vnew[P, 0, :], ps[4][P, 0:128], AF.Copy, pk(4), [K("vnew")])
            for b in range(NB):
                mm(ps[5][:, h * 128:h * 128 + C_], S0b[:, b, :], qdTm[:, b, :], b == 0, False, ["qdTm", "S0b"], pk(5))
            mm(ps[5][:, h * 128:h * 128 + C_], t.vnew[P, 0, :], t.qkm[P, h, 0:C_], False, True, [K("vnew"), K("qkm")], pk(5))
            for b in range(NB):
                mm(ps[b // 4][:, (b % 4) * 128:(b % 4 + 1) * 128], kdm[P, b, :], t.vnew[P, 0, :], True, True, ["kdm", K("vnew")], pk(b // 4))
            for q in range(4):
                tt("dve", s0f[:, 4 * q:4 * q + 4, :], s0f[:, 4 * q:4 * q + 4, :],
                   t.glast[:, h, 4 * q:4 * q + 4].unsqueeze(2).to_broadcast([128, 4, 128]), ALU.mult, [ks0, K("glast")], [ks0])
                tt("dve", s0f[:, 4 * q:4 * q + 4, :], s0f[:, 4 * q:4 * q + 4, :], ps[q][:, :].rearrange("p (b d) -> p b d", b=4), ALU.add,
                   [ks0] + pk(q), [ks0])
            dma("sp", ssms_d[:, :, h, :], s0f[:, :, :], [ks0], [("ssm_s", h)], ks0)
            out_keys.append(("ssm_s", h))
        lockstep([gdn_out(C_, c0, t)])

    def mixer_out(st):
        for m in range(KC):
            for (g, lo, hi) in groups(st):
                if g != "S":
                    bank = (m % 2) * 2 + (0 if g == "A" else 1)
                    o = ps[bank][:, :]
                else:
                    bank = 4 + m % 2
                    o = ps[bank][:, 0:64]
                kp = pk(bank)
                for k in range(KC):
                    mm(o, wout[:, k, m * 128:(m + 1) * 128], mix[:, k, lo:hi], k == 0, k == KC - 1, ["wout", ("mix", k)], kp)
                if g != "S":
                    stt("dve", xT[:, m, lo:hi], o, gsc[1][:, m, 0:1], xT[:, m, lo:hi], ALU.mult, ALU.add,
                        kp + [("gsc", 1), ("xT", m, g)], [("xT", m, g)])
                else:
                    tt("dve", v3(st_xh[:, 0:64]), v3(o), bc_s(gsc[1], m), ALU.mult, kp + [("gsc", 1)], ["st_xh"])
                    tt("dve", xT[:, m, lo:hi], st_xh[:, 0:64], xT[:, m, lo:hi], ALU.add, ["st_xh", ("xT", m, g)], [("xT", m, g)])

    def store_y(st):
        for gi_, (g, lo, hi) in enumerate((("A", 0, 512), ("B", 512, 1024))):
            dma("sp", yp_d[:, :, st * NTP + lo:st * NTP + hi], xT[:, :, lo:hi], [("xT", c, g) for c in range(KC)],
                [("yp", st, g)], ("yout", st, g))
            out_keys.append(("yp", st, g))
        if st == 0:
            dma("sp", ys_d, xT[:, :, NTP:NT], [("xT", c, "S") for c in range(KC)], ["ys"], "ysout")
            out_keys.append("ys")

    try:
        for st in range(2 if stop_after != "ada" else 0):
            load_x(st)
            if stop_after == "load":
                store_y(st)
                continue
            modulate0(st)
            if stop_after == "mod0":
                S.fence()
                store_y(st)
                continue
            ffn(st, 0, 0)
            if stop_after in ("ffn1", "up"):
                store_y(st)
                continue
            layernorm(st, 0, G1, B1, ("GB", 1))
            for b_ in range(3):
                prefetch(("win", st, b_), win_d[b_])
            S.fence()
            if stop_after == "ln1":
                store_y(st)
                continue
            mixer_proj(st)
            S.fence()
            if stop_after == "proj":
                continue
            dma("pool", wout[:, :, :], wout_d, [], ["wout"], "wout")
            for n in range(4):
                gdn_prompt_pair(2 * n * 128, (2 * n + 1) * 128)
            if st == 0:
                S.fence()
                gdn_sample()
            S.fence()
            if st == 1:
                dma("sp", ssmp_d, Sst[:, :, :], [("Sst", h_) for h_ in range(4)], ["ssm_p"], "ssm_p")
                out_keys.append("ssm_p")
            if stop_after == "gdn":
                continue
            mixer_out(st)
            layernorm(st, 1, G2, B2, ("GB", 2))
            prefetch(("wg", st, 1, 0), wg_d[1][0])
            prefetch(("wu", st, 1, 0), wu_d[1][0])
            prefetch(("wg", st, 1, 1), wg_d[1][1])
            S.fence()
            if stop_after == "ln2":
                store_y(st)
                continue
            ffn(st, 1, 2)
            layernorm(st, 2, None, None, None)
            store_y(st)


    except _Stop:
        pass

    S.add("sp", lambda e: e.nop(), reads=out_keys)
    with nc.Block() as block:
        S.finalize(block)
    return nc, dbg_outs


def _blk(w, nb):
    return np.ascontiguousarray(w.reshape(KC, 128, nb, 256).transpose(2, 1, 0, 3))


def _prep_shared(inp):
    f = lambda a: np.asarray(a, dtype=np.float32)
    sh = {}
    sh["wada"] = _blk(f(inp["w_ada"])[0], 36)
    sh["bada"] = np.ascontiguousarray(f(inp["b_ada"])[0].reshape(72, 128).T)
    sh["lng"] = np.ascontiguousarray(f(inp["ln_g"])[0].reshape(24, 128).T)
    sh["lnb"] = np.ascontiguousarray(f(inp["ln_b"])[0].reshape(24, 128).T)
    for i, nm in ((1, "ffn1"), (2, "ffn2")):
        sh["wg%d" % i] = _blk(f(inp[nm + "_wg"])[0], 11)
        sh["wu%d" % i] = _blk(f(inp[nm + "_wu"])[0], 11)
        wd = f(inp[nm + "_wd"])[0]
        sh["wd%d" % i] = np.ascontiguousarray(wd.reshape(FC, 128, KC, 128).transpose(2, 1, 0, 3))
    win = f(inp["w_in"])[0]
    qkv, ab, og = win[:, 0:1536], win[:, 1536:1544], win[:, 1544:2056]
    scb, scc, sch = win[:, 2056:2568], win[:, 2568:3080], win[:, 3080:3592]
    inter = np.concatenate([np.concatenate([scc[:, j * 128:(j + 1) * 128], sch[:, j * 128:(j + 1) * 128]], axis=1)
                            for j in range(4)], axis=1)
    main = np.concatenate([qkv, og, scb, inter], axis=1)
    sh["win"] = _blk(main, 14)
    sh["wab"] = np.ascontiguousarray(ab.reshape(KC, 128, 8).transpose(1, 0, 2))
    sh["wout"] = np.ascontiguousarray(f(inp["w_out"])[0].reshape(KC, 128, D).transpose(1, 0, 2))
    sh["cqw"] = np.ascontiguousarray(f(inp["conv_qkv_w"])[0].reshape(4, 12, 128).transpose(2, 0, 1))
    sh["cmw"] = np.ascontiguousarray(f(inp["conv_mix_w"])[0].reshape(3, 4, 128).transpose(2, 0, 1))
    sh["dng"] = np.ascontiguousarray(f(inp["dn_norm_g"])[0].reshape(128, 1))
    sh["alog"] = np.ascontiguousarray(f(inp["a_log"])[0])
    sh["dtb"] = np.ascontiguousarray(f(inp["dt_bias"])[0])
    sh["cst"] = _CST
    sh["sel"] = _SEL
    return sh


def _prep_core(inp, c):
    f = lambda a: np.asarray(a, dtype=np.float32)
    m = {}
    xp = f(inp["x_prompt"])[c]
    m["xp"] = np.ascontiguousarray(xp.T.reshape(KC, 128, NP).transpose(1, 0, 2))
    xs = f(inp["x_sample"])[NB * c:NB * (c + 1)]
    xs = xs.transpose(1, 0, 2).reshape(NS, D)
    m["xs"] = np.ascontiguousarray(xs.T.reshape(KC, 128, NS).transpose(1, 0, 2))
    m["cc"] = np.ascontiguousarray(np.concatenate([f(inp["c_prompt"])[c:c + 1], f(inp["c_sample"])[NB * c:NB * (c + 1)]], axis=0))
    m["s0"] = np.ascontiguousarray(f(inp["state_ssm"])[0, NB * c:NB * (c + 1)].transpose(2, 0, 1, 3))
    cq = f(inp["state_conv_qkv"])[0, NB * c:NB * (c + 1)]
    m["cq"] = np.ascontiguousarray(cq.reshape(NB, 3, 12, 128).transpose(3, 2, 1, 0).reshape(128, 12, 48))
    cm = f(inp["state_conv_mix"])[0, NB * c:NB * (c + 1)]
    m["cm"] = np.ascontiguousarray(cm.reshape(NB, 2, 4, 128).transpose(3, 2, 1, 0).reshape(128, 4, 32))
    return m


def _run(inp, debug=(), stop_after=None, trace=False, ncores=8):
    nc, dbg = build_program(debug=debug, stop_after=stop_after)
    sh = _prep_shared(inp)
    in_maps = []
    for c in range(ncores):
        m = dict(sh)
        m.update(_prep_core(inp, c))
        in_maps.append(m)
    res = run_bass_kernel_spmd(nc, in_maps, core_ids=list(range(ncores)), trace=trace)
    return res


def _assemble(res):
    R = list(res.results)
    while len(R) < 8:
        R.append(R[0])
    y_p = np.stack([R[c]["yp"].transpose(1, 0, 2).reshape(D, NP).T for c in range(8)])
    ys = []
    for c in range(8):
        a = R[c]["ys"].transpose(1, 0, 2).reshape(D, NS).T
        ys.append(a.reshape(4, NB, D).transpose(1, 0, 2))
    y_s = np.concatenate(ys, axis=0)
    ssm_p = np.stack([R[c]["ssm_p"].transpose(1, 0, 2) for c in range(8)])[None]
    cq_p = np.stack([R[c]["cq_p"].transpose(2, 1, 0).reshape(3, 1536) for c in range(8)])[None]
    cm_p = np.stack([R[c]["cm_p"].transpose(2, 1, 0).reshape(2, 512) for c in range(8)])[None]
    ssm_s = np.concatenate([R[c]["ssm_s"].transpose(1, 2, 0, 3) for c in range(8)], axis=0)[None]
    cq_s = np.concatenate([R[c]["cq_s"].reshape(128, 12, 3, NB).transpose(3, 2, 1, 0).reshape(NB, 3, 1536)
                           for c in range(8)], axis=0)[None]
    cm_s = np.concatenate([R[c]["cm_s"].reshape(128, 4, 2, NB).transpose(3, 2, 1, 0).reshape(NB, 2, 512)
                           for c in range(8)], axis=0)[None]
    outs = (y_p, y_s, ssm_p, cq_p, cm_p, ssm_s, cq_s, cm_s)
    return tuple(np.ascontiguousarray(o, dtype=np.float32) for o in outs)


def kernel(**inputs):
    res = _run(inputs)
    return _assemble(res)
```
